# Optimizing a Trainium2 kernel written in Bass

```python
import math
import jax
import jax.numpy as jnp
from jax import lax
import numpy as np

D_MODEL = 1024
BATCH = 4
SEQ = 8192
DEPTH = 1
DEC_BATCH = 16
DEC_SEQ = 4096
PAST_LEN = 128

MEM_TOKENS = 256
MIX_WIDTH = D_MODEL
DIFF_WIDTH = MIX_WIDTH // 2
SSM_WIDTH = MIX_WIDTH - DIFF_WIDTH
DIFF_HEAD_DIM = 64
DIFF_HEADS = DIFF_WIDTH // (2 * DIFF_HEAD_DIM)
SSM_GROUP = 16
SSM_GROUPS = SSM_WIDTH // SSM_GROUP
SSM_STATE = 64
N_BUCKETS = 32
MAX_DISTANCE = 128
Q_BLOCK = 128
MEM_HEADS = 4
MEM_HEAD_DIM = D_MODEL // MEM_HEADS
D_FF = 2816
IN_WIDTH = 3 * DIFF_WIDTH + SSM_WIDTH
ALPHA = (2 * DEPTH) ** 0.25
BETA = (8 * DEPTH) ** -0.25
LN_EPS = 1e-5
SUBLN_EPS = 1e-5

kernel_name = 'hybrid_diffattn_s5_macaron_encoder'


def layer_norm(x, g, b):
    xf = x.astype(jnp.float32)
    mu = jnp.mean(xf, axis=-1, keepdims=True)
    var = jnp.mean(jnp.square(xf - mu), axis=-1, keepdims=True)
    y = (xf - mu) * lax.rsqrt(var + LN_EPS) * g.astype(jnp.float32) + b.astype(jnp.float32)
    return y.astype(x.dtype)


def swiglu(x, w13, w2):
    a, b = jnp.split(x @ w13, 2, axis=-1)
    return (jax.nn.silu(a) * b) @ w2


def rel_bucket(rel):
    half = N_BUCKETS // 2
    max_exact = half // 2
    ret = jnp.where(rel > 0, half, 0).astype(jnp.int32)
    n = jnp.abs(rel)
    nf = jnp.maximum(n, 1).astype(jnp.float32)
    large = max_exact + (jnp.log(nf / max_exact) / math.log(MAX_DISTANCE / max_exact)
                         * (half - max_exact)).astype(jnp.int32)
    large = jnp.minimum(large, half - 1)
    return ret + jnp.where(n < max_exact, n, large)


def diff_attention(q, k, v, lam, lam_init, subln_g, rel_bias):
    bsz, seq = q.shape[0], q.shape[1]
    nblk = seq // Q_BLOCK
    scale = DIFF_HEAD_DIM ** -0.5
    qb = q.reshape(bsz, nblk, Q_BLOCK, DIFF_HEADS, 2, DIFF_HEAD_DIM).transpose(1, 0, 2, 3, 4, 5)
    starts = jnp.arange(nblk, dtype=jnp.int32) * Q_BLOCK
    kpos = jnp.arange(seq, dtype=jnp.int32)
    table = rel_bias.astype(jnp.float32)

    def block(args):
        q_blk, s0 = args
        qpos = s0 + jnp.arange(Q_BLOCK, dtype=jnp.int32)
        bucket = rel_bucket(kpos[None, :] - qpos[:, None])
        bias = jnp.take(table, bucket, axis=0).transpose(2, 0, 1)
        s = jnp.einsum('bqhmd,bkhmd->bhmqk', q_blk, k).astype(jnp.float32) * scale
        p = jax.nn.softmax(s + bias[None, :, None], axis=-1)
        w = p[:, :, 0] - lam * p[:, :, 1]
        return jnp.einsum('bhqk,bkhe->bqhe', w.astype(v.dtype), v)

    o = lax.map(block, (qb, starts))
    o = o.transpose(1, 0, 2, 3, 4).reshape(bsz, seq, DIFF_HEADS, 2 * DIFF_HEAD_DIM)
    of = o.astype(jnp.float32)
    of = of * lax.rsqrt(jnp.mean(jnp.square(of), axis=-1, keepdims=True) + SUBLN_EPS)
    of = of * subln_g.astype(jnp.float32) * (1.0 - lam_init)
    return of.reshape(bsz, seq, DIFF_WIDTH).astype(q.dtype)


def _ssm_combine(c1, c2):
    a1, b1 = c1
    a2, b2 = c2
    return a1 * a2, a2 * b1 + b2


def s5_scan(u, lam_re, lam_im, log_step, b_re, b_im, c_re, c_im, reverse):
    f32 = jnp.float32
    lam = lax.complex(lam_re.astype(f32), lam_im.astype(f32))
    step = jnp.exp(log_step.astype(f32))[:, None]
    a_bar = jnp.exp(lam * step)
    b = lax.complex(b_re.astype(f32), b_im.astype(f32))
    b_bar = ((a_bar - 1.0) / lam)[..., None] * b
    bu = jnp.einsum('blgh,gph->blgp', u, b_bar)
    a = jnp.broadcast_to(a_bar, bu.shape)
    _, states = lax.associative_scan(_ssm_combine, (a, bu), reverse=reverse, axis=1)
    c = lax.complex(c_re.astype(f32), c_im.astype(f32))
    return jnp.einsum('blgp,ghp->blgh', states, c).real


def s5_mixer(u, lam_re, lam_im, log_step, b_re, b_im, c_re, c_im, d, glu_w, glu_b):
    bsz, seq = u.shape[0], u.shape[1]
    uf = u.astype(jnp.float32)
    ug = uf.reshape(bsz, seq, SSM_GROUPS, SSM_GROUP)
    y_fwd = s5_scan(ug, lam_re[0], lam_im[0], log_step[0], b_re[0], b_im[0], c_re[0], c_im[0], False)
    y_bwd = s5_scan(ug, lam_re[1], lam_im[1], log_step[1], b_re[1], b_im[1], c_re[1], c_im[1], True)
    y = (y_fwd + y_bwd).reshape(bsz, seq, SSM_WIDTH) + d.astype(jnp.float32) * uf
    z = jax.nn.gelu(y).astype(u.dtype)
    return z * jax.nn.sigmoid(z @ glu_w + glu_b)


def memory_attention(x, mem, wq, wkv, wo):
    bsz, seq = x.shape[0], x.shape[1]
    m = mem.shape[1]
    q = (x @ wq).reshape(bsz, seq, MEM_HEADS, MEM_HEAD_DIM)
    k, v = jnp.split(mem @ wkv, 2, axis=-1)
    k = k.reshape(bsz, m, MEM_HEADS, MEM_HEAD_DIM)
    v = v.reshape(bsz, m, MEM_HEADS, MEM_HEAD_DIM)
    s = jnp.einsum('bqhd,bkhd->bhqk', q, k).astype(jnp.float32) * (MEM_HEAD_DIM ** -0.5)
    p = jax.nn.softmax(s, axis=-1)
    o = jnp.einsum('bhqk,bkhd->bqhd', p.astype(v.dtype), v).reshape(bsz, seq, D_MODEL)
    return o @ wo


def encoder_trunk(x, mem, p):
    bsz, seq = x.shape[0], x.shape[1]
    for l in range(DEPTH):
        lam_init = 0.8 - 0.6 * math.exp(-0.3 * l)
        x = layer_norm(ALPHA * x + 0.5 * swiglu(x, p['ffn1_w13'][l], p['ffn1_w2'][l]),
                       p['ln_ffn1_g'][l], p['ln_ffn1_b'][l])
        h = x @ p['w_in'][l]
        q = h[..., :DIFF_WIDTH].reshape(bsz, seq, DIFF_HEADS, 2, DIFF_HEAD_DIM)
        k = h[..., DIFF_WIDTH:2 * DIFF_WIDTH].reshape(bsz, seq, DIFF_HEADS, 2, DIFF_HEAD_DIM)
        v = h[..., 2 * DIFF_WIDTH:3 * DIFF_WIDTH].reshape(bsz, seq, DIFF_HEADS, 2 * DIFF_HEAD_DIM)
        u = h[..., 3 * DIFF_WIDTH:]
        dl = p['diff_lambda'][l].astype(jnp.float32)
        lam = jnp.exp(jnp.sum(dl[0] * dl[1])) - jnp.exp(jnp.sum(dl[2] * dl[3])) + lam_init
        o_diff = diff_attention(q, k, v, lam, lam_init, p['diff_subln_g'][l], p['rel_bias'])
        o_ssm = s5_mixer(u, p['ssm_lam_re'][l], p['ssm_lam_im'][l], p['ssm_log_step'][l],
                         p['ssm_b_re'][l], p['ssm_b_im'][l], p['ssm_c_re'][l], p['ssm_c_im'][l],
                         p['ssm_d'][l], p['ssm_glu_w'][l], p['ssm_glu_b'][l])
        mix = jnp.concatenate([o_diff, o_ssm], axis=-1) @ p['w_out'][l]
        x = layer_norm(ALPHA * x + mix, p['ln_mix_g'][l], p['ln_mix_b'][l])
        x = layer_norm(ALPHA * x + memory_attention(x, mem, p['mem_wq'][l], p['mem_wkv'][l], p['mem_wo'][l]),
                       p['ln_mem_g'][l], p['ln_mem_b'][l])
        x = layer_norm(ALPHA * x + 0.5 * swiglu(x, p['ffn2_w13'][l], p['ffn2_w2'][l]),
                       p['ln_ffn2_g'][l], p['ln_ffn2_b'][l])
    return x


def setup_inputs(seed: int = 0) -> dict:
    key = jax.random.key(seed)
    keys = iter(jax.random.split(key, 64))
    f32 = jnp.float32

    def nrm(shape, scale):
        return jax.random.normal(next(keys), shape, f32) * scale

    def gain():
        return 1.0 + nrm((DEPTH, D_MODEL), 0.02)

    def bias():
        return nrm((DEPTH, D_MODEL), 0.02)

    d_in = D_MODEL ** -0.5
    in_col_scale = jnp.concatenate([jnp.ones((2 * DIFF_WIDTH,), f32),
                                    jnp.full((DIFF_WIDTH,), BETA, f32),
                                    jnp.ones((SSM_WIDTH,), f32)])
    kv_col_scale = jnp.concatenate([jnp.ones((D_MODEL,), f32), jnp.full((D_MODEL,), BETA, f32)])
    state_idx = jnp.arange(SSM_STATE, dtype=f32)
    ssm_shape = (DEPTH, 2, SSM_GROUPS, SSM_STATE)
    log_step = jax.random.uniform(next(keys), (DEPTH, 2, SSM_GROUPS), f32,
                                  math.log(1e-3), math.log(1e-1))
    return {
        'x_prompt': nrm((BATCH, SEQ, D_MODEL), 1.0),
        'x_sample': nrm((DEC_BATCH, DEC_SEQ, D_MODEL), 1.0),
        'mem_prompt': nrm((BATCH, MEM_TOKENS, D_MODEL), 1.0),
        'mem_sample': nrm((DEC_BATCH, MEM_TOKENS, D_MODEL), 1.0),
        'ffn1_w13': nrm((DEPTH, D_MODEL, 2 * D_FF), d_in * BETA),
        'ffn1_w2': nrm((DEPTH, D_FF, D_MODEL), D_FF ** -0.5 * BETA),
        'ln_ffn1_g': gain(),
        'ln_ffn1_b': bias(),
        'w_in': nrm((DEPTH, D_MODEL, IN_WIDTH), d_in) * in_col_scale,
        'diff_lambda': nrm((DEPTH, 4, DIFF_HEAD_DIM), 0.1),
        'diff_subln_g': 1.0 + nrm((DEPTH, 2 * DIFF_HEAD_DIM), 0.02),
        'rel_bias': nrm((N_BUCKETS, DIFF_HEADS), 0.1),
        'ssm_lam_re': -0.5 + nrm(ssm_shape, 0.01),
        'ssm_lam_im': math.pi * state_idx + nrm(ssm_shape, 0.01),
        'ssm_log_step': log_step,
        'ssm_b_re': nrm((DEPTH, 2, SSM_GROUPS, SSM_STATE, SSM_GROUP), (2 * SSM_GROUP) ** -0.5),
        'ssm_b_im': nrm((DEPTH, 2, SSM_GROUPS, SSM_STATE, SSM_GROUP), (2 * SSM_GROUP) ** -0.5),
        'ssm_c_re': nrm((DEPTH, 2, SSM_GROUPS, SSM_GROUP, SSM_STATE), 2.0 * (2 * SSM_STATE) ** -0.5),
        'ssm_c_im': nrm((DEPTH, 2, SSM_GROUPS, SSM_GROUP, SSM_STATE), 2.0 * (2 * SSM_STATE) ** -0.5),
        'ssm_d': nrm((DEPTH, SSM_WIDTH), 1.0),
        'ssm_glu_w': nrm((DEPTH, SSM_WIDTH, SSM_WIDTH), SSM_WIDTH ** -0.5),
        'ssm_glu_b': nrm((DEPTH, SSM_WIDTH), 0.01),
        'w_out': nrm((DEPTH, MIX_WIDTH, D_MODEL), MIX_WIDTH ** -0.5 * BETA),
        'ln_mix_g': gain(),
        'ln_mix_b': bias(),
        'mem_wq': nrm((DEPTH, D_MODEL, D_MODEL), d_in),
        'mem_wkv': nrm((DEPTH, D_MODEL, 2 * D_MODEL), d_in) * kv_col_scale,
        'mem_wo': nrm((DEPTH, D_MODEL, D_MODEL), d_in * BETA),
        'ln_mem_g': gain(),
        'ln_mem_b': bias(),
        'ffn2_w13': nrm((DEPTH, D_MODEL, 2 * D_FF), d_in * BETA),
        'ffn2_w2': nrm((DEPTH, D_FF, D_MODEL), D_FF ** -0.5 * BETA),
        'ln_ffn2_g': gain(),
        'ln_ffn2_b': bias(),
    }


def reference(x_prompt, x_sample, mem_prompt, mem_sample,
              ffn1_w13, ffn1_w2, ln_ffn1_g, ln_ffn1_b,
              w_in, diff_lambda, diff_subln_g, rel_bias,
              ssm_lam_re, ssm_lam_im, ssm_log_step, ssm_b_re, ssm_b_im, ssm_c_re, ssm_c_im,
              ssm_d, ssm_glu_w, ssm_glu_b,
              w_out, ln_mix_g, ln_mix_b,
              mem_wq, mem_wkv, mem_wo, ln_mem_g, ln_mem_b,
              ffn2_w13, ffn2_w2, ln_ffn2_g, ln_ffn2_b):
    params = dict(
        ffn1_w13=ffn1_w13, ffn1_w2=ffn1_w2, ln_ffn1_g=ln_ffn1_g, ln_ffn1_b=ln_ffn1_b,
        w_in=w_in, diff_lambda=diff_lambda, diff_subln_g=diff_subln_g, rel_bias=rel_bias,
        ssm_lam_re=ssm_lam_re, ssm_lam_im=ssm_lam_im, ssm_log_step=ssm_log_step,
        ssm_b_re=ssm_b_re, ssm_b_im=ssm_b_im, ssm_c_re=ssm_c_re, ssm_c_im=ssm_c_im,
        ssm_d=ssm_d, ssm_glu_w=ssm_glu_w, ssm_glu_b=ssm_glu_b,
        w_out=w_out, ln_mix_g=ln_mix_g, ln_mix_b=ln_mix_b,
        mem_wq=mem_wq, mem_wkv=mem_wkv, mem_wo=mem_wo, ln_mem_g=ln_mem_g, ln_mem_b=ln_mem_b,
        ffn2_w13=ffn2_w13, ffn2_w2=ffn2_w2, ln_ffn2_g=ln_ffn2_g, ln_ffn2_b=ln_ffn2_b,
    )
    y_prompt = encoder_trunk(x_prompt, mem_prompt, params)
    y_sample = encoder_trunk(x_sample, mem_sample, params)
    return (y_prompt, y_sample)
```

```python
import math
from contextlib import ExitStack

import numpy as np
import concourse.bass as bass
import concourse.mybir as mybir
from concourse.bass_utils import run_bass_kernel_spmd

F32 = mybir.dt.float32
BF16 = mybir.dt.bfloat16
I32 = mybir.dt.int32
ALU = mybir.AluOpType
AF = mybir.ActivationFunctionType
AX = mybir.AxisListType

D = 1024
DFF = 2816
NJ = DFF // 128
ALPHA = 2.0 ** 0.25
LN_EPS = 1e-5
LAM_INIT = 0.8 - 0.6 * math.exp(0.0)
NG = 32
NP_ = 64
T1 = 16

ENGS = ["pe", "act", "dve", "pool", "sp"]
BLOCKNAME = {"pe": "tensor", "act": "scalar", "dve": "vector", "pool": "gpsimd", "sp": "sync"}


class Op:
    __slots__ = ("eng", "fn", "deps", "signal", "semval", "is_dma", "dsem", "dval", "pos")


class Prog:
    def __init__(self, nc, es, ring=None):
        self.nc = nc
        self.ops = {e: [] for e in ENGS}
        self.lastw = {}
        self.readers = {}
        self.sem = {e: es.enter_context(nc.semaphore("s_" + e)) for e in ENGS if e != "sp"}
        ring = ring or {"sp": 12, "pool": 8, "act": 4}
        self.ring = {q: [[es.enter_context(nc.semaphore("d_%s%d" % (q, i))), 0, None] for i in range(n)]
                     for q, n in ring.items()}
        self.ringpos = {q: 0 for q in ring}
        self.bar = {e: [] for e in ENGS}
        self.last = {e: None for e in ENGS}

    def add(self, eng, fn, reads=(), writes=(), dma=False):
        op = Op()
        op.eng = eng; op.fn = fn; op.deps = set(); op.signal = False; op.is_dma = dma
        op.semval = 0; op.dsem = None; op.dval = 0
        pbr = [r for r in reads if isinstance(r, tuple) and r[0] == "pb"]
        if pbr:
            reads = [r for r in reads if not (isinstance(r, tuple) and r[0] == "pb")]
            writes = list(writes) + pbr
        for r in reads:
            w = self.lastw.get(r)
            if w is not None:
                op.deps.add(w)
        for r in writes:
            w = self.lastw.get(r)
            if w is not None:
                op.deps.add(w)
            lastrd = {}
            for rd in self.readers.get(r, ()):
                if rd.is_dma:
                    op.deps.add(rd)
                else:
                    lastrd[rd.eng] = rd
            for rd in lastrd.values():
                op.deps.add(rd)
        for r in writes:
            self.lastw[r] = op
            self.readers[r] = []
        for r in reads:
            self.readers.setdefault(r, []).append(op)
        if self.bar[eng]:
            op.deps.update(self.bar[eng])
            self.bar[eng] = []
        if dma:
            slot = self.ring[eng][self.ringpos[eng] % len(self.ring[eng])]
            self.ringpos[eng] += 1
            if slot[2] is not None:
                op.deps.add(slot[2])
            slot[1] += 16
            slot[2] = op
            op.dsem = slot[0]
            op.dval = slot[1]
        op.deps.discard(op)
        self.ops[eng].append(op)
        self.last[eng] = op
        return op

    def barrier(self):
        pend = [self.last[e] for e in ENGS if self.last[e] is not None]
        for q in self.ring:
            for slot in self.ring[q]:
                if slot[2] is not None:
                    pend.append(slot[2])
        for e in ENGS:
            self.bar[e] = list(pend)
        self.lastw = {}
        self.readers = {}

    def emit(self):
        for e in ENGS:
            for i, op in enumerate(self.ops[e]):
                op.pos = i
        for e in ENGS:
            for op in self.ops[e]:
                keep = set()
                for d in op.deps:
                    if (not d.is_dma) and d.eng == e:
                        if e == "pe":
                            continue
                    keep.add(d)
                op.deps = keep
                for d in op.deps:
                    d.signal = True
        for e in ENGS:
            n = 0
            for op in self.ops[e]:
                if op.signal and not op.is_dma:
                    n += 1
                    op.semval = n
        with self.nc.Block() as block:
            for e in ENGS:
                def body(eng, e=e):
                    seen = {}
                    for op in self.ops[e]:
                        waits = {}
                        for d in op.deps:
                            if d.is_dma:
                                key, val = d.dsem, d.dval
                            else:
                                if d.eng == e and e == "pe":
                                    continue
                                key, val = self.sem[d.eng], d.semval
                            k = id(key)
                            if k not in waits or waits[k][1] < val:
                                waits[k] = (key, val)
                        for k, (key, val) in waits.items():
                            if seen.get(k, 0) < val:
                                eng.wait_ge(key, val)
                                seen[k] = val
                        ins = op.fn(eng)
                        if op.is_dma:
                            ins.then_inc(op.dsem, 16)
                        elif op.signal:
                            ins.then_inc(self.sem[e], 1)
                getattr(block, BLOCKNAME[e])(body)


def _bucket_table():
    i = np.arange(1281)
    rel = 640 - i
    half, max_exact = 16, 8
    ret = np.where(rel > 0, half, 0).astype(np.int32)
    n = np.abs(rel)
    nf = np.maximum(n, 1).astype(np.float32)
    lg = (np.log(nf / np.float32(max_exact)) / np.float32(math.log(128 / max_exact)) * np.float32(half - max_exact))
    large = max_exact + lg.astype(np.float32).astype(np.int32)
    large = np.minimum(large, half - 1)
    b = ret + np.where(n < max_exact, n, large)
    oh = np.zeros((32, 1281), np.float32)
    oh[b, i] = 1.0
    return oh


class Cfg:
    def __init__(self, nown, has_ctx=True, nsamp=2, debug=None):
        self.NOWN = nown
        self.NCTX = nown if has_ctx else 0
        self.nsamp = nsamp
        self.debug = debug
        self.TT = 512
        assert nown % 512 == 0


def build(cfg):
    nc = bass.Bass("TRN2", target_bir_lowering=False)
    NOWN, NCTX, TT = cfg.NOWN, cfg.NCTX, cfg.TT
    NSEQ = 1 + cfg.nsamp
    dbg = cfg.debug

    def din(name, shape, dt=F32):
        return nc.dram_tensor(name, list(shape), dt, kind="ExternalInput").ap()

    def dscr(name, shape, dt, out=False):
        return nc.dram_tensor(name, list(shape), dt, kind="ExternalOutput" if out else "Internal").ap()

    xp_own = din("xp_own", [NOWN, D]); xp_ctx = din("xp_ctx", [NOWN, D])
    xs = din("xs", [cfg.nsamp, NOWN, D])
    mem = din("mem", [NSEQ, 256, D])
    flags = din("flags", [128, 2])
    ident_in = din("ident", [128, 128])
    antiid_in = din("antiid", [128, 128])
    oh_in = din("bucket_oh", [32, 1281])
    kv_in = din("kv17", [64, 2, 17])
    msk_in = din("ssm_mask", [128, 2, 2, 256])
    W = {}
    for nm, shp in [("ffn1_w13", [D, 2 * DFF]), ("ffn1_w2", [DFF, D]), ("ffn2_w13", [D, 2 * DFF]), ("ffn2_w2", [DFF, D]),
                    ("w_in", [D, 2048]), ("w_out", [D, D]), ("mem_wq", [D, D]), ("mem_wkv", [D, 2048]), ("mem_wo", [D, D]),
                    ("ssm_glu_w", [512, 512])]:
        W[nm] = din(nm, shp)
    V = {}
    for nm in ["ln_ffn1_g", "ln_ffn1_b", "ln_mix_g", "ln_mix_b", "ln_mem_g", "ln_mem_b", "ln_ffn2_g", "ln_ffn2_b"]:
        V[nm] = din(nm, [D])
    diff_lambda = din("diff_lambda", [256]); subln_g = din("diff_subln_g", [128]); rel_bias = din("rel_bias", [32, 4])
    ssm = {}
    for nm, shp in [("ssm_lam_re", [2, NG, NP_]), ("ssm_lam_im", [2, NG, NP_]), ("ssm_log_step", [2, NG]),
                    ("ssm_b_re", [2, NG, NP_, 16]), ("ssm_b_im", [2, NG, NP_, 16]),
                    ("ssm_c_re", [2, NG, 16, NP_]), ("ssm_c_im", [2, NG, 16, NP_]),
                    ("ssm_d", [512]), ("ssm_glu_b", [512])]:
        ssm[nm] = din(nm, shp)

    yp = nc.dram_tensor("yp", [NOWN, D], F32, kind="ExternalOutput").ap()
    ys = nc.dram_tensor("ys", [cfg.nsamp, NOWN, D], F32, kind="ExternalOutput").ap()

    wd13 = [dscr("wd13_%d" % f, [11, 128, 4096], BF16) for f in range(2)]
    wd2 = [dscr("wd2_%d" % f, [8, 128, NJ * 128], BF16) for f in range(2)]
    wdin = dscr("wdin", [4, 128, 4096], BF16)
    wdout = dscr("wdout", [2, 128, 4096], BF16)
    wdq = dscr("wdq", [2, 128, 4096], BF16)
    wdkv = dscr("wdkv", [4, 128, 4096], BF16)
    wdo = dscr("wdo", [2, 128, 4096], BF16)
    wdglu = dscr("wdglu", [128, 2048], BF16)
    NK = [NOWN + NCTX] + [NOWN] * cfg.nsamp
    dflag = dbg is not None
    x1s = [dscr("x1s%d" % s, [D, NOWN], F32, out=dflag) for s in range(NSEQ)]
    qTs = [dscr("qTs%d" % s, [512, NOWN], BF16, out=dflag) for s in range(NSEQ)]
    kTs = [dscr("kTs%d" % s, [512, NK[s]], BF16, out=dflag) for s in range(NSEQ)]
    vs = [dscr("vs%d" % s, [NK[s], 512], BF16, out=dflag) for s in range(NSEQ)]
    us = [dscr("us%d" % s, [NK[s], 512], BF16, out=dflag) for s in range(NSEQ)]
    ods = [dscr("ods%d" % s, [NOWN, 512], BF16, out=dflag) for s in range(NSEQ)]
    oss = [dscr("oss%d" % s, [NOWN, 512], BF16, out=dflag) for s in range(NSEQ)]

    es = ExitStack()
    P = Prog(nc, es)

    def sb(name, shape, dt, stack=None):
        return (stack or es).enter_context(nc.sbuf_tensor("sb_" + name, list(shape), dt))

    def ps(name, shape, dt, stack=None):
        return (stack or es).enter_context(nc.psum_tensor("ps_" + name, list(shape), dt))

    ident = sb("ident", [128, 128], F32)
    identb = sb("identb", [128, 128], BF16)
    antiidb = sb("antiidb", [128, 128], BF16)
    onesb = sb("onesb", [128, 128], BF16)
    flg = sb("flg", [128, 2], F32)
    lnv = sb("lnv", [128, 8, 8], F32)
    epsc = sb("epsc", [128, 1], F32)
    P.add("sp", lambda e: e.dma_start(out=ident[:, :], in_=ident_in[:, :]), writes=["ident"], dma=True)
    P.add("sp", lambda e: e.dma_start(out=flg[:, :], in_=flags[:, :]), writes=["flg"], dma=True)
    P.add("pool", lambda e: e.dma_start(out=identb[:, :], in_=ident_in[:, :]), writes=["identb"], dma=True)
    P.add("pool", lambda e: e.dma_start(out=antiidb[:, :], in_=antiid_in[:, :]), writes=["antiidb"], dma=True)
    P.add("dve", lambda e: e.memset(onesb[:, :], 1.0), writes=["onesb"])
    P.add("dve", lambda e: e.memset(epsc[:, :], LN_EPS), writes=["epsc"])
    LNI = {}
    lnraw = sb("lnraw", [64, 128], F32)
    for i, nm in enumerate(["ln_ffn1_g", "ln_ffn1_b", "ln_mix_g", "ln_mix_b", "ln_mem_g", "ln_mem_b", "ln_ffn2_g", "ln_ffn2_b"]):
        LNI[nm] = i
        src = V[nm].rearrange("(c p) -> c p", p=128)
        P.add("sp", lambda e, i=i, src=src: e.dma_start(out=lnraw[i * 8:(i + 1) * 8, :], in_=src), writes=["lnraw"], dma=True)

    def castdma(dst, src, key):
        P.add("pool", lambda e: e.dma_start(out=dst, in_=src), writes=[key], dma=True)

    for f, nm in enumerate(["ffn1_w13", "ffn2_w13"]):
        w = W[nm].rearrange("(kc p) n -> p kc n", p=128)
        for s_ in range(11):
            dst = wd13[f][s_].rearrange("p (jj ab kc c) -> p jj ab kc c", jj=2, ab=2, kc=8)
            for jj in range(2):
                for ab in range(2):
                    c0 = ab * DFF + (2 * s_ + jj) * 128
                    castdma(dst[:, jj, ab], w[:, :, c0:c0 + 128], ("wd13", f, s_))
    for f, nm in enumerate(["ffn1_w2", "ffn2_w2"]):
        w = W[nm].rearrange("(j p) n -> p j n", p=128)
        for dc in range(8):
            dst = wd2[f][dc].rearrange("p (j c) -> p j c", j=NJ)
            castdma(dst, w[:, :, dc * 128:(dc + 1) * 128], ("wd2", f, dc))
    for (nm, dd, ns) in [("w_in", wdin, 4), ("w_out", wdout, 2), ("mem_wq", wdq, 2), ("mem_wkv", wdkv, 4), ("mem_wo", wdo, 2)]:
        w = W[nm].rearrange("(kc p) n -> p kc n", p=128)
        for s_ in range(ns):
            dst = dd[s_].rearrange("p (kc c) -> p kc c", kc=8)
            castdma(dst, w[:, :, s_ * 512:(s_ + 1) * 512], (nm, s_))
    castdma(wdglu.rearrange("p (kc c) -> p kc c", kc=4), W["ssm_glu_w"].rearrange("(kc p) n -> p kc n", p=128), ("wdglu",))

    NWS = 4
    wslots = []
    wpos = [0]

    def wload(src, key, ncols=4096):
        i = wpos[0] % NWS
        wpos[0] += 1
        t = wslots[i]
        P.add("sp", lambda e: e.dma_start(out=t[:, 0:ncols], in_=src), reads=[key], writes=[("wslot", i)], dma=True)
        return t, ("wslot", i)

    pball = ps("pball", [128, 4096], F32)
    pb = [pball[:, i * 512:(i + 1) * 512] for i in range(8)]
    pbk = [("pb", i) for i in range(8)]
    rr = {"ab": 0, "o": 0}
    P.add("pe", lambda e: e.transpose(pb[6][:, 0:64], lnraw[:, :], ident[0:64, 0:64]), reads=["lnraw", "ident"], writes=[pbk[6]])
    P.add("dve", lambda e: e.tensor_copy(out=lnv[:, :, :].rearrange("p a b -> p (a b)"), in_=pb[6][:, 0:64]), reads=[pbk[6]], writes=["lnv"])

    def layernorm(r, rk, gname, bname, outf, outb, okf, okb, tmp):
        rb, rsq, mean, rstd, t1 = tmp["rb"], tmp["rsq"], tmp["mean"], tmp["rstd"], tmp["t1"]
        for dc in range(8):
            P.add("dve", lambda e, dc=dc: e.tensor_copy(out=rb[:, dc % 2, :], in_=r[:, dc, :]), reads=[(rk, dc)], writes=[("rb", dc % 2)])
            P.add("act", lambda e, dc=dc: e.activation(out=rsq[:, dc % 2, :], in_=r[:, dc, :], func=AF.Square), reads=[(rk, dc)], writes=[("rsq", dc % 2)])
            P.add("pe", lambda e, dc=dc: e.matmul(pb[6][:, :], onesb[:, :], rb[:, dc % 2, :], start=(dc == 0), stop=(dc == 7)),
                  reads=[("rb", dc % 2), "onesb"], writes=[pbk[6]])
            P.add("pe", lambda e, dc=dc: e.matmul(pb[7][:, :], onesb[:, :], rsq[:, dc % 2, :], start=(dc == 0), stop=(dc == 7)),
                  reads=[("rsq", dc % 2), "onesb"], writes=[pbk[7]])
        P.add("dve", lambda e: e.tensor_scalar(out=mean[:, :], in0=pb[6][:, :], scalar1=1.0 / D, scalar2=None, op0=ALU.mult),
              reads=[pbk[6]], writes=["mean"])
        P.add("dve", lambda e: e.tensor_tensor(out=t1[:, :], in0=mean[:, :], in1=mean[:, :], op=ALU.mult), reads=["mean"], writes=["t1"])
        P.add("dve", lambda e: e.scalar_tensor_tensor(out=t1[:, :], in0=pb[7][:, :], scalar=1.0 / D, in1=t1[:, :], op0=ALU.mult, op1=ALU.subtract),
              reads=[pbk[7], "t1"], writes=["t1"])
        P.add("act", lambda e: e.activation(out=t1[:, :], in_=t1[:, :], func=AF.Sqrt, bias=epsc[:, :], scale=1.0), reads=["t1", "epsc"], writes=["t1"])
        P.add("dve", lambda e: e.reciprocal(out=rstd[:, :], in_=t1[:, :]), reads=["t1"], writes=["rstd"])
        gi, bi = LNI[gname], LNI[bname]
        for dc in range(8):
            P.add("dve", lambda e, dc=dc: e.tensor_tensor(out=r[:, dc, :], in0=r[:, dc, :], in1=mean[:, :], op=ALU.subtract),
                  reads=[(rk, dc), "mean"], writes=[(rk, dc)])
            P.add("dve", lambda e, dc=dc: e.tensor_tensor(out=r[:, dc, :], in0=r[:, dc, :], in1=rstd[:, :], op=ALU.mult),
                  reads=[(rk, dc), "rstd"], writes=[(rk, dc)])
            P.add("act", lambda e, dc=dc: e.activation(out=outb[:, dc, :], in_=r[:, dc, :], func=AF.Identity,
                                                        bias=lnv[:, bi, dc:dc + 1], scale=lnv[:, gi, dc:dc + 1]),
                  reads=[(rk, dc), "lnv"], writes=[(okb, dc)])
        for dc in range(8):
            P.add("act", lambda e, dc=dc: e.activation(out=outf[:, dc, :], in_=r[:, dc, :], func=AF.Identity,
                                                        bias=lnv[:, bi, dc:dc + 1], scale=lnv[:, gi, dc:dc + 1]),
                  reads=[(rk, dc), "lnv"], writes=[(okf, dc)])

    def ffn(f, xb, xbk, xf, xfk, r, rk, gT, sg, stage=9):
        for s_ in range(11):
            wt, wk = wload(wd13[f][s_], ("wd13", f, s_))
            wv = wt[:, :].rearrange("p (jj ab kc c) -> p jj ab kc c", jj=2, ab=2, kc=8)
            for jj in range(2):
                j = 2 * s_ + jj
                pa = rr["ab"] % 2
                rr["ab"] += 1
                ba, bb_ = pb[2 * pa], pb[2 * pa + 1]
                for ab, bank, bkey in ((0, ba, pbk[2 * pa]), (1, bb_, pbk[2 * pa + 1])):
                    for kc in range(8):
                        P.add("pe", lambda e, bank=bank, jj=jj, ab=ab, kc=kc, wv=wv: e.matmul(bank[:, :], wv[:, jj, ab, kc, :], xb[:, kc, :], start=(kc == 0), stop=(kc == 7)),
                              reads=[wk, (xbk, kc)], writes=[bkey])
                sgt = sg[pa]
                P.add("act", lambda e, ba=ba, sgt=sgt: e.activation(out=sgt[:, :], in_=ba[:, :], func=AF.Silu), reads=[pbk[2 * pa]], writes=[("sg", pa)])
                P.add("dve", lambda e, bb_=bb_, sgt=sgt, j=j: e.scalar_tensor_tensor(out=gT[:, j, :], in0=sgt[:, :], scalar=0.5, in1=bb_[:, :], op0=ALU.mult, op1=ALU.mult),
                      reads=[("sg", pa), pbk[2 * pa + 1]], writes=[("gT", j)])
        if stage < 3:
            return
        for dc in range(8):
            wt, wk = wload(wd2[f][dc], ("wd2", f, dc), ncols=NJ * 128)
            wv = wt[:, 0:NJ * 128].rearrange("p (j c) -> p j c", j=NJ)
            po = 4 + rr["o"] % 2
            rr["o"] += 1
            for j in range(NJ):
                P.add("pe", lambda e, po=po, j=j, wv=wv: e.matmul(pb[po][:, :], wv[:, j, :], gT[:, j, :], start=(j == 0), stop=(j == NJ - 1)),
                      reads=[wk, ("gT", j)], writes=[pbk[po]])
            P.add("dve", lambda e, po=po, dc=dc: e.scalar_tensor_tensor(out=r[:, dc, :], in0=xf[:, dc, :], scalar=ALPHA, in1=pb[po][:, :], op0=ALU.mult, op1=ALU.add),
                  reads=[pbk[po], (xfk, dc)], writes=[(rk, dc)])

    pa_es = ExitStack()
    wslots[:] = [sb("wslotA%d" % i, [128, 4096], BF16, pa_es) for i in range(NWS)]
    xt = [sb("xt%d" % i, [128, 4, D], F32, pa_es) for i in range(2)]
    xTf = sb("xTf", [128, 8, 512], F32, pa_es)
    xTb = sb("xTb", [128, 8, 512], BF16, pa_es)
    gT = sb("gT", [128, NJ, 512], BF16, pa_es)
    sg = [sb("sg%d" % i, [128, 512], F32, pa_es) for i in range(2)]
    rA = sb("rA", [128, 8, 512], F32, pa_es)
    x1f = sb("x1f", [128, 8, 512], F32, pa_es)
    x1b = sb("x1b", [128, 8, 512], BF16, pa_es)
    tmpA = {"rb": sb("rb", [128, 2, 512], BF16, pa_es), "rsq": sb("rsq", [128, 2, 512], BF16, pa_es),
            "mean": sb("mean", [128, 512], F32, pa_es), "rstd": sb("rstd", [128, 512], F32, pa_es), "t1": sb("t1", [128, 512], F32, pa_es)}
    stg = [sb("stg%d" % i, [128, 4, 512], BF16, pa_es) for i in range(2)]
    stgpos = [0]

    tiles = []
    for s in range(NSEQ):
        srcs = [("own", xp_own if s == 0 else xs[s - 1])]
        if s == 0 and NCTX:
            srcs.append(("ctx", xp_ctx))
        for kind, src in srcs:
            for i in range(NOWN // TT):
                tiles.append((s, kind, src, i))

    if dbg == 'P':
        tiles = []
    stage = 9
    if dbg and dbg.startswith('T'):
        stage = int(dbg[1:]); tiles = tiles[:1]
    if dbg == 'A1':
        tiles = tiles[:1]
    for ti, (s, kind, src, i) in enumerate(tiles):
        xtt = xt[ti % 2]
        xk = ("xt", ti % 2)
        srcv = src[i * TT:(i + 1) * TT, :].rearrange("(st p) d -> p st d", p=128)
        P.add("sp", lambda e, xtt=xtt, srcv=srcv: e.dma_start(out=xtt[:, :, :], in_=srcv), writes=[xk], dma=True)
        for kc in range(8):
            for st in range(4):
                P.add("pe", lambda e, kc=kc, st=st, xtt=xtt: e.transpose(pb[6][:, st * 128:(st + 1) * 128], xtt[:, st, kc * 128:(kc + 1) * 128], ident[:, :]),
                      reads=[xk, "ident"], writes=[pbk[6]])
            P.add("act", lambda e, kc=kc: e.copy(out=xTf[:, kc, :], in_=pb[6][:, :]), reads=[pbk[6]], writes=[("xTf", kc)])
            P.add("pool", lambda e, kc=kc: e.tensor_copy(out=xTb[:, kc, :], in_=xTf[:, kc, :]), reads=[("xTf", kc)], writes=[("xTb", kc)])
        if stage < 2:
            continue
        ffn(0, xTb, "xTb", xTf, "xTf", rA, "rA", gT, sg, stage)
        if stage < 4:
            continue
        layernorm(rA, "rA", "ln_ffn1_g", "ln_ffn1_b", x1f, x1b, "x1f", "x1b", tmpA)
        if stage < 5:
            continue
        tok0 = i * TT + (NOWN if kind == "ctx" else 0)
        if kind == "own":
            dst = x1s[s][:, i * TT:(i + 1) * TT].rearrange("(dc p) t -> p dc t", p=128)
            P.add("pool", lambda e, dst=dst: e.dma_start(out=dst, in_=x1f[:, :, :]), reads=[("x1f", dc) for dc in range(8)], writes=[("x1s", s, i)], dma=True)
        for cg in range(4):
            if cg == 0 and kind != "own":
                continue
            wt, wk = wload(wdin[cg], ("w_in", cg))
            wv = wt[:, :].rearrange("p (kc c) -> p kc c", kc=8)
            sgi = stgpos[0] % 2
            stgpos[0] += 1
            stt = stg[sgi]
            for c4 in range(4):
                po = 4 + rr["o"] % 2
                rr["o"] += 1
                for kc in range(8):
                    if cg < 2:
                        P.add("pe", lambda e, po=po, kc=kc, c4=c4, wv=wv: e.matmul(pb[po][:, :], wv[:, kc, c4 * 128:(c4 + 1) * 128], x1b[:, kc, :], start=(kc == 0), stop=(kc == 7)),
                              reads=[wk, ("x1b", kc)], writes=[pbk[po]])
                    else:
                        P.add("pe", lambda e, po=po, kc=kc, c4=c4, wv=wv: e.matmul(pb[po][:, :], x1b[:, kc, c4 * 128:(c4 + 1) * 128], wv[:, kc, :], start=(kc == 0), stop=(kc == 7)),
                              reads=[wk, ("x1b", kc)], writes=[pbk[po]])
                P.add("act", lambda e, po=po, c4=c4, stt=stt: e.copy(out=stt[:, c4, :], in_=pb[po][:, :]), reads=[pbk[po]], writes=[("stg", sgi)])
            if cg == 0:
                dst = qTs[s][:, i * TT:(i + 1) * TT].rearrange("(c p) t -> p c t", p=128)
                key = ("qTs", s, i)
            elif cg == 1:
                dst = kTs[s][:, tok0:tok0 + TT].rearrange("(c p) t -> p c t", p=128)
                key = ("kTs", s, tok0)
            elif cg == 2:
                dst = vs[s][tok0:tok0 + TT, :].rearrange("(st p) c -> p st c", p=128)
                key = ("vs", s, tok0)
            else:
                dst = us[s][tok0:tok0 + TT, :].rearrange("(st p) c -> p st c", p=128)
                key = ("us", s, tok0)
            P.add("pool", lambda e, dst=dst, stt=stt: e.dma_start(out=dst, in_=stt[:, :, :]), reads=[("stg", sgi)], writes=[key], dma=True)
    P.barrier()
    pa_es.close()


    if dbg in (None, "B", "C", "S", "Ss", "Sm"):
        pb_es = ExitStack()
        NOB = NOWN // 128
        NQG = NOWN // 512
        NKBmax = max(NK) // 128
        Gd = dscr("Gd", [4, 1281], F32)
        Gd8 = dscr("Gd8", [4, 1281], F32)
        dl = sb("dl", [128, 256], F32, pb_es)
        dlp = sb("dlp", [128, 128], F32, pb_es)
        lam2 = sb("lam2", [128, 4], F32, pb_es)
        neglam = sb("neglam", [128, 1], F32, pb_es)
        gsub = sb("gsub", [128, 128], F32, pb_es)
        rbt = sb("rbt", [32, 4], F32, pb_es)
        oht = sb("oht", [32, 1281], F32, pb_es)
        Gsb = sb("Gsb", [4, 1281], F32, pb_es)
        Gsb8 = sb("Gsb8", [4, 1281], F32, pb_es)
        cpm = sb("cpm", [128, 4, 2], F32, pb_es)
        coth = sb("coth", [128, 4], F32, pb_es)
        tcol = sb("tcol", [128, 4, 2], F32, pb_es)
        Wt = [[sb("Wt%d_%d" % (h, m), [128, 512], BF16, pb_es) for m in range(6)] for h in range(4)]
        SW = [[sb("SW%d_%d" % (h, a), [128, 512], BF16, pb_es) for a in range(2)] for h in range(4)]
        P.add("sp", lambda e: e.dma_start(out=dl[:, :], in_=diff_lambda.partition_broadcast(128)), writes=["dl"], dma=True)
        P.add("sp", lambda e: e.dma_start(out=gsub[:, :], in_=subln_g.partition_broadcast(128)), writes=["gsub"], dma=True)
        P.add("sp", lambda e: e.dma_start(out=rbt[:, :], in_=rel_bias[:, :]), writes=["rbt"], dma=True)
        P.add("sp", lambda e: e.dma_start(out=oht[:, :], in_=oh_in[:, :]), writes=["oht"], dma=True)
        dlv = dl[:, :].rearrange("p (a b) -> p a b", a=4)
        dpv = dlp[:, :].rearrange("p (a b) -> p a b", a=2)
        P.add("dve", lambda e: e.tensor_tensor(out=dpv[:, 0, :], in0=dlv[:, 0, :], in1=dlv[:, 1, :], op=ALU.mult), reads=["dl"], writes=["dlp"])
        P.add("dve", lambda e: e.tensor_tensor(out=dpv[:, 1, :], in0=dlv[:, 2, :], in1=dlv[:, 3, :], op=ALU.mult), reads=["dl"], writes=["dlp"])
        P.add("dve", lambda e: e.tensor_reduce(out=lam2[:, 0:2], in_=dpv, axis=AX.X, op=ALU.add), reads=["dlp"], writes=["lam2"])
        P.add("act", lambda e: e.activation(out=lam2[:, 2:4], in_=lam2[:, 0:2], func=AF.Exp), reads=["lam2"], writes=["lam2e"])
        P.add("dve", lambda e: e.tensor_tensor(out=neglam[:, :], in0=lam2[:, 3:4], in1=lam2[:, 2:3], op=ALU.subtract), reads=["lam2e"], writes=["neglam"])
        P.add("dve", lambda e: e.tensor_scalar(out=neglam[:, :], in0=neglam[:, :], scalar1=-LAM_INIT, scalar2=None, op0=ALU.add), reads=["neglam"], writes=["neglam"])
        P.add("dve", lambda e: e.tensor_scalar(out=gsub[:, :], in0=gsub[:, :], scalar1=1.0 - LAM_INIT, scalar2=None, op0=ALU.mult), reads=["gsub"], writes=["gsub"])
        for c0 in range(0, 1281, 512):
            n = min(512, 1281 - c0)
            P.add("pe", lambda e, c0=c0, n=n: e.matmul(pb[0][0:4, 0:n], rbt[:, :], oht[:, c0:c0 + n], start=True, stop=True), reads=["rbt", "oht"], writes=[pbk[0]])
            P.add("act", lambda e, c0=c0, n=n: e.copy(out=Gsb[:, c0:c0 + n], in_=pb[0][0:4, 0:n]), reads=[pbk[0]], writes=["Gsb"])
        P.add("dve", lambda e: e.tensor_scalar(out=Gsb8[:, :], in0=Gsb[:, :], scalar1=8.0, scalar2=None, op0=ALU.mult), reads=["Gsb"], writes=["Gsb8"])
        P.add("sp", lambda e: e.dma_start(out=Gd[:, :], in_=Gsb[:, :]), reads=["Gsb"], writes=["Gd"], dma=True)
        P.add("sp", lambda e: e.dma_start(out=Gd8[:, :], in_=Gsb8[:, :]), reads=["Gsb8"], writes=["Gd8"], dma=True)
        for h in range(4):
            P.add("sp", lambda e, h=h: e.dma_start(out=cpm[:, h, 0:1], in_=Gd[h:h + 1, 1:2].partition_broadcast(128)), reads=["Gd"], writes=["cpm"], dma=True)
            P.add("sp", lambda e, h=h: e.dma_start(out=cpm[:, h, 1:2], in_=Gd[h:h + 1, 1279:1280].partition_broadcast(128)), reads=["Gd"], writes=["cpm"], dma=True)
            for m in range(6):
                src = bass.AP(tensor=Gd8.tensor, offset=h * 1281 + 641 - 128 * m, ap=[[1, 128], [1, 512]])
                P.add("pool", lambda e, h=h, m=m, src=src: e.dma_start(out=Wt[h][m][:, :], in_=src), reads=["Gd8"], writes=[("Wt", h, m)], dma=True)
        for h in range(4):
            P.add("dve", lambda e, h=h: e.tensor_scalar(out=coth[:, h:h + 1], in0=cpm[:, h, 0:1], scalar1=flg[:, 1:2], scalar2=None, op0=ALU.mult), reads=["cpm", "flg"], writes=["coth"])
            P.add("dve", lambda e, h=h: e.scalar_tensor_tensor(out=coth[:, h:h + 1], in0=cpm[:, h, 1:2], scalar=flg[:, 0:1], in1=coth[:, h:h + 1], op0=ALU.mult, op1=ALU.add), reads=["cpm", "flg", "coth"], writes=["coth"])
            P.add("dve", lambda e, h=h: e.tensor_scalar(out=tcol[:, h, 0:1], in0=cpm[:, h, 1:2], scalar1=flg[:, 0:1], scalar2=8.0, op0=ALU.mult, op1=ALU.mult), reads=["cpm", "flg"], writes=["tcol"])
            P.add("dve", lambda e, h=h: e.tensor_scalar(out=tcol[:, h, 1:2], in0=cpm[:, h, 0:1], scalar1=flg[:, 1:2], scalar2=8.0, op0=ALU.mult, op1=ALU.mult), reads=["cpm", "flg"], writes=["tcol"])
            P.add("dve", lambda e, h=h: e.tensor_scalar(out=SW[h][0][:, :], in0=Wt[h][5][:, :], scalar1=flg[:, 1:2], scalar2=tcol[:, h, 0:1], op0=ALU.mult, op1=ALU.add), reads=[("Wt", h, 5), "flg", "tcol"], writes=[("SW", h, 0)])
            P.add("dve", lambda e, h=h: e.tensor_scalar(out=SW[h][1][:, :], in0=Wt[h][0][:, :], scalar1=flg[:, 0:1], scalar2=tcol[:, h, 1:2], op0=ALU.mult, op1=ALU.add), reads=[("Wt", h, 0), "flg", "tcol"], writes=[("SW", h, 1)])

        KT = [sb("KT%d" % i, [128, max(NK)], BF16, pb_es) for i in range(2)]
        VH = [sb("VH%d" % i, [128, NKBmax, 129], BF16, pb_es) for i in range(2)]
        QT = [sb("QT%d" % i, [128, NOWN], BF16, pb_es) for i in range(2)]
        ET = [sb("ET%d" % i, [128, 1024], BF16, pb_es) for i in range(3)]
        o0 = sb("o0", [128, 4, 128], F32, pb_es)
        ob = [sb("ob%d" % i, [128, 128], F32, pb_es) for i in range(2)]
        junk = sb("junk", [128, 128], F32, pb_es)
        sm = sb("sm", [128, 16], F32, pb_es)
        ostg = [sb("ostg%d" % i, [128, 4, 128], BF16, pb_es) for i in range(2)]
        accs = [sb("accs%d" % i, [128, 2048], F32, pb_es) for i in range(2)]
        deferred = []
        for i in range(2):
            P.add("dve", lambda e, i=i: e.memset(VH[i][:, :, 128:129], 1.0), writes=[("VH", i)])
        cnt = {"hs": 0, "sc": 0, "et": 0, "og": 0}
        seqsB = range(NSEQ) if dbg != "B1" else range(1)
        for s in seqsB:
            NKB = NK[s] // 128
            for h in range(4):
                bi = cnt["hs"] % 2
                cnt["hs"] += 1
                kt, vh, qt = KT[bi], VH[bi], QT[bi]
                P.add("sp", lambda e, kt=kt, s=s, h=h: e.dma_start(out=kt[:, 0:NK[s]], in_=kTs[s][h * 128:(h + 1) * 128, :]), writes=[("KT", bi)], dma=True)
                P.add("sp", lambda e, qt=qt, s=s, h=h: e.dma_start(out=qt[:, :], in_=qTs[s][h * 128:(h + 1) * 128, :]), writes=[("QT", bi)], dma=True)
                vsrc = vs[s][:, h * 128:(h + 1) * 128].rearrange("(kb p) c -> p kb c", p=128)
                P.add("sp", lambda e, vh=vh, vsrc=vsrc, NKB=NKB: e.dma_start(out=vh[:, 0:NKB, 0:128], in_=vsrc), reads=[("VH", bi)], writes=[("VHd", bi)], dma=True)
                for I in range(NQG):
                    og = cnt["og"] % 2
                    cnt["og"] += 1
                    P.add("dve", lambda e: e.memset(pball[:, 2048:4096], 0.0), writes=[pbk[4], pbk[5], pbk[6], pbk[7]])
                    while deferred:
                        deferred.pop(0)()

                    def QK(j, I=I, kt=kt, qt=qt, h=h, bi=bi, NKB=NKB):
                        R = cnt["sc"] % 2
                        cnt["sc"] += 1
                        eb = cnt["et"] % 3
                        cnt["et"] += 1
                        wide = None
                        bias = None
                        if j < NOB:
                            mw = j - 4 * I + 1
                            if 0 <= mw <= 5:
                                wide = (Wt[h][mw], ("Wt", h, mw))
                            elif mw < 0:
                                bias = cpm[:, h, 1:2]
                            else:
                                bias = cpm[:, h, 0:1]
                        else:
                            jo = j - NOB
                            if I == NQG - 1 and jo == 0:
                                wide = (SW[h][0], ("SW", h, 0))
                            elif I == 0 and jo == NKB - NOB - 1:
                                wide = (SW[h][1], ("SW", h, 1))
                            else:
                                bias = coth[:, h:h + 1]
                        for m in range(2):
                            P.add("pe", lambda e, m=m: e.matmul(pb[2 * R + m], kt[m * 64:(m + 1) * 64, j * 128:(j + 1) * 128], qt[m * 64:(m + 1) * 64, I * 512:(I + 1) * 512], start=True, stop=(wide is None)),
                                  reads=[("KT", bi), ("QT", bi)], writes=[pbk[2 * R + m]])
                        if wide is not None:
                            for m in range(2):
                                P.add("pe", lambda e, m=m: e.matmul(pb[2 * R + m], antiidb[:, :], wide[0][:, :], start=False, stop=True), reads=["antiidb", wide[1]], writes=[pbk[2 * R + m]])
                            P.add("act", lambda e: e.activation(out=ET[eb][:, :], in_=pball[:, 2 * R * 512:2 * R * 512 + 1024], func=AF.Exp, scale=0.125), reads=[pbk[2 * R], pbk[2 * R + 1]], writes=[("ET", eb)])
                        else:
                            P.add("act", lambda e: e.activation(out=ET[eb][:, :], in_=pball[:, 2 * R * 512:2 * R * 512 + 1024], func=AF.Exp, bias=bias, scale=0.125), reads=[pbk[2 * R], pbk[2 * R + 1], "cpm", "coth"], writes=[("ET", eb)])
                        return eb

                    def PV(j, eb, vh=vh, bi=bi):
                        for m in range(2):
                            for qb in range(4):
                                a = m * 4 + qb
                                bank = 4 + a // 2
                                off = (a % 2) * 256
                                P.add("pe", lambda e, m=m, qb=qb, bank=bank, off=off: e.matmul(pb[bank][:, off:off + 129], ET[eb][:, m * 512 + qb * 128:m * 512 + (qb + 1) * 128], vh[:, j, :], start=False, stop=False, skip_group_check=True),
                                      reads=[("ET", eb), ("VHd", bi)], writes=[pbk[bank]])
                    prev = None
                    for j in range(NKB):
                        eb = QK(j)
                        if prev is not None:
                            PV(*prev)
                        prev = (j, eb)
                    PV(*prev)
                    acs = accs[og]
                    P.add("act", lambda e, acs=acs: e.copy(out=acs[:, 0:1024], in_=pball[:, 2048:3072]), reads=[pbk[4], pbk[5]], writes=[("accs", og, 0)])
                    P.add("dve", lambda e, acs=acs: e.tensor_copy(out=acs[:, 1024:2048], in_=pball[:, 3072:4096]), reads=[pbk[6], pbk[7]], writes=[("accs", og, 1)])

                    def epilogue(og=og, acs=acs, s=s, I=I, h=h):
                        for qb in range(4):
                            off = (qb // 2) * 512 + (qb % 2) * 256
                            acc0 = acs[:, off:off + 129]
                            acc1 = acs[:, 1024 + off:1024 + off + 129]
                            k0, k1 = ("accs", og, 0), ("accs", og, 1)
                            obq = ob[qb % 2]
                            okey = ("ob", qb % 2)
                            P.add("dve", lambda e, qb=qb, acc0=acc0: e.reciprocal(out=sm[:, qb:qb + 1], in_=acc0[:, 128:129]), reads=[k0], writes=[("sm", qb)])
                            P.add("dve", lambda e, qb=qb, acc1=acc1: e.reciprocal(out=sm[:, 4 + qb:5 + qb], in_=acc1[:, 128:129]), reads=[k1], writes=[("sm1", qb)])
                            P.add("dve", lambda e, qb=qb: e.tensor_tensor(out=sm[:, 4 + qb:5 + qb], in0=sm[:, 4 + qb:5 + qb], in1=neglam[:, :], op=ALU.mult), reads=[("sm1", qb), "neglam"], writes=[("sm1", qb)])
                            P.add("dve", lambda e, qb=qb, acc0=acc0: e.tensor_scalar(out=o0[:, qb, :], in0=acc0[:, 0:128], scalar1=sm[:, qb:qb + 1], scalar2=None, op0=ALU.mult), reads=[k0, ("sm", qb)], writes=[("o0", qb)])
                            P.add("dve", lambda e, qb=qb, acc1=acc1, obq=obq: e.scalar_tensor_tensor(out=obq[:, :], in0=acc1[:, 0:128], scalar=sm[:, 4 + qb:5 + qb], in1=o0[:, qb, :], op0=ALU.mult, op1=ALU.add),
                                  reads=[k1, ("sm1", qb), ("o0", qb)], writes=[okey])
                            P.add("act", lambda e, qb=qb, obq=obq: e.activation(out=junk[:, :], in_=obq[:, :], func=AF.Square, accum_out=sm[:, 8 + qb:9 + qb]), reads=[okey], writes=["junk", ("sm2", qb)])
                            P.add("act", lambda e, qb=qb: e.activation(out=sm[:, 8 + qb:9 + qb], in_=sm[:, 8 + qb:9 + qb], func=AF.Sqrt, bias=epsc[:, :], scale=1.0 / 128), reads=[("sm2", qb), "epsc"], writes=[("sm2", qb)])
                            P.add("dve", lambda e, qb=qb: e.reciprocal(out=sm[:, 8 + qb:9 + qb], in_=sm[:, 8 + qb:9 + qb]), reads=[("sm2", qb)], writes=[("sm2", qb)])
                            P.add("dve", lambda e, qb=qb, og=og, obq=obq: e.scalar_tensor_tensor(out=ostg[og][:, qb, :], in0=obq[:, :], scalar=sm[:, 8 + qb:9 + qb], in1=gsub[:, :], op0=ALU.mult, op1=ALU.mult),
                                  reads=[okey, ("sm2", qb), "gsub"], writes=[("ostg", og)])
                        dst = ods[s][I * 512:(I + 1) * 512, h * 128:(h + 1) * 128].rearrange("(qb p) e -> p qb e", p=128)
                        P.add("pool", lambda e, dst=dst, og=og: e.dma_start(out=dst, in_=ostg[og][:, :, :]), reads=[("ostg", og)], writes=[("ods", s, I, h)], dma=True)
                    deferred.append(epilogue)
        while deferred:
            deferred.pop(0)()
        P.barrier()
        pb_es.close()


    if dbg in (None, "C", "S", "Ss", "Sm"):
        TWO_PI = 2.0 * math.pi
        NB1 = NOWN // T1
        BS = min(128, NB1)
        NBLK = NB1 // BS
        NCH = NB1 // 16
        GB = 8
        WAd = dscr("WAd", [4, 128, 4096], BF16)
        WCd = dscr("WCd", [4, 64, 8192], BF16)
        KTd = dscr("KTd", [4, 128, 4096], BF16)
        yscr = [dscr("yscr%d" % s_, [NOWN, 512], F32) for s_ in range(NSEQ)]
        PW = sb("PW", [64, 2, 2, NG, 16], F32)

        def V_(eng, fn, reads, writes):
            P.add(eng, fn, reads=reads, writes=writes)

        def tt(eng, out, in0, in1, op, r, w):
            P.add(eng, lambda e: e.tensor_tensor(out=out, in0=in0, in1=in1, op=op), reads=r, writes=w)

        def cmul(eng, ore, oim, are, aim, bre, bim, t1, t2, r, w, tk, neg_im=False):
            tt(eng, t1, are, bre, ALU.mult, r, [tk + "1"])
            tt(eng, t2, aim, bim, ALU.mult, r, [tk + "2"])
            tt(eng, ore, t1, t2, ALU.subtract, [tk + "1", tk + "2"], [w + "re"])
            tt(eng, t1, are, bim, ALU.mult, r, [tk + "1"])
            tt(eng, t2, aim, bre, ALU.mult, r, [tk + "2"])
            if neg_im:
                P.add(eng, lambda e: e.scalar_tensor_tensor(out=oim, in0=t1, scalar=-1.0, in1=t2, op0=ALU.mult, op1=ALU.subtract), reads=[tk + "1", tk + "2"], writes=[w + "im"])
            else:
                tt(eng, oim, t1, t2, ALU.add, [tk + "1", tk + "2"], [w + "im"])

        st_es = ExitStack()
        kv = sb("kv", [64, 2, 17], F32, st_es)
        msk = sb("msk", [128, 2, 2, 256], F32, st_es)
        lraw = sb("lraw", [32, 2, 64], F32, st_es)
        lT = sb("lT", [64, 2, 32], F32, st_es)
        stp = sb("stp", [64, 32], F32, st_es)
        rho = sb("rho", [64, 32], F32, st_es)
        th = sb("th", [64, 32], F32, st_es)
        Etab = sb("Etab", [64, 2, 2, 2, 32, 17], F32, st_es)
        ang = sb("ang", [64, 32, 17], F32, st_es)
        mag = sb("mag", [64, 32, 17], F32, st_es)
        ai_ = sb("angi", [64, 32, 17], I32, st_es)
        af_ = sb("angf", [64, 32, 17], F32, st_es)
        m1 = sb("m1", [64, 32, 17], F32, st_es)
        sn = sb("sn", [64, 32, 17], F32, st_es)
        cs = sb("cs", [64, 32, 17], F32, st_es)
        inv16 = sb("inv16", [64, 2, 2, 32], F32, st_es)
        cf = sb("cf", [64, 2, 2, 32], F32, st_es)
        q1 = sb("q1", [64, 32], F32, st_es)
        q2 = sb("q2", [64, 32], F32, st_es)
        q3 = sb("q3", [64, 32], F32, st_es)
        q4 = sb("q4", [64, 32], F32, st_es)
        bp = sb("bp", [64, 2, 2, 32, 16], F32, st_es)
        bbp = sb("bbp", [64, 2, 2, 32, 16], F32, st_es)
        craw = sb("craw", [128, 4, 64], F32, st_es)
        cT = sb("cT", [64, 2, 2, 512], F32, st_es)
        tb1 = sb("tb1", [64, 32, 16], F32, st_es)
        tb2 = sb("tb2", [64, 32, 16], F32, st_es)
        WAp = sb("WAp", [64, 2, GB, 16, 16], F32, st_es)
        WA2 = sb("WA2", [64, 2, GB, 16, 16], F32, st_es)
        WCp = sb("WCp", [64, 2, GB, 16, 16], F32, st_es)
        u1 = sb("u1", [64, GB, 16, 16], F32, st_es)
        u2 = sb("u2", [64, GB, 16, 16], F32, st_es)
        WAst = sb("WAst", [128, 2, 2, GB, 2, 64], BF16, st_es)
        WCst = sb("WCst", [64, 2, GB, 2, 256], BF16, st_es)
        KTs = sb("KTs", [128, GB, 2, 256], F32, st_es)
        KTt = sb("KTt", [128, 256], F32, st_es)
        KTb = sb("KTb", [128, GB, 2, 256], BF16, st_es)
        P.add("sp", lambda e: e.dma_start(out=kv[:, :, :], in_=kv_in[:, :, :]), writes=["kv"], dma=True)
        P.add("sp", lambda e: e.dma_start(out=msk[:, :, :, :], in_=msk_in[:, :, :, :]), writes=["msk"], dma=True)
        for d_ in range(2):
            dk = "d%d" % d_
            P.add("sp", lambda e, d_=d_: e.dma_start(out=lraw[:, 0, :], in_=ssm["ssm_lam_re"][d_]), writes=["lraw"], dma=True)
            P.add("sp", lambda e, d_=d_: e.dma_start(out=lraw[:, 1, :], in_=ssm["ssm_lam_im"][d_]), writes=["lraw"], dma=True)
            for ri in range(2):
                P.add("pe", lambda e, ri=ri: e.transpose(pb[6][0:64, ri * 32:(ri + 1) * 32], lraw[:, ri, :], ident[0:32, 0:32]), reads=["lraw", "ident"], writes=[pbk[6]])
            P.add("act", lambda e: e.copy(out=lT[:, :, :].rearrange("p a g -> p (a g)"), in_=pb[6][0:64, 0:64]), reads=[pbk[6]], writes=["lT"])
            P.add("sp", lambda e, d_=d_: e.dma_start(out=stp[:, :], in_=ssm["ssm_log_step"][d_].partition_broadcast(64)), writes=["stp"], dma=True)
            P.add("act", lambda e: e.activation(out=stp[:, :], in_=stp[:, :], func=AF.Exp), reads=["stp"], writes=["stp"])
            tt("dve", rho[:, :], lT[:, 0, :], stp[:, :], ALU.mult, ["lT", "stp"], ["rho"])
            tt("dve", th[:, :], lT[:, 1, :], stp[:, :], ALU.mult, ["lT", "stp"], ["th"])
            for od in range(2):
                kb_ = kv[:, od, :].unsqueeze(1).broadcast_to([64, 32, 17])
                tt("dve", ang[:, :, :], th[:, :].unsqueeze(2).broadcast_to([64, 32, 17]), kb_, ALU.mult, ["th", "kv"], ["ang"])
                tt("dve", mag[:, :, :], rho[:, :].unsqueeze(2).broadcast_to([64, 32, 17]), kb_, ALU.mult, ["rho", "kv"], ["mag"])
                P.add("act", lambda e: e.activation(out=mag[:, :, :], in_=mag[:, :, :], func=AF.Exp), reads=["mag"], writes=["mag"])
                P.add("dve", lambda e: e.tensor_scalar(out=af_[:, :, :], in0=ang[:, :, :], scalar1=1.0 / TWO_PI, scalar2=None, op0=ALU.mult), reads=["ang"], writes=["af"])
                P.add("dve", lambda e: e.tensor_copy(out=ai_[:, :, :], in_=af_[:, :, :]), reads=["af"], writes=["ai"])
                P.add("dve", lambda e: e.tensor_copy(out=af_[:, :, :], in_=ai_[:, :, :]), reads=["ai"], writes=["af"])
                P.add("dve", lambda e: e.scalar_tensor_tensor(out=ang[:, :, :], in0=af_[:, :, :], scalar=-TWO_PI, in1=ang[:, :, :], op0=ALU.mult, op1=ALU.add), reads=["af", "ang"], writes=["ang"])

                def fold(x, xk):
                    P.add("dve", lambda e: e.tensor_scalar(out=m1[:, :, :], in0=x, scalar1=math.pi, scalar2=None, op0=ALU.is_gt), reads=[xk], writes=["m1"])
                    P.add("dve", lambda e: e.scalar_tensor_tensor(out=x, in0=m1[:, :, :], scalar=-TWO_PI, in1=x, op0=ALU.mult, op1=ALU.add), reads=["m1", xk], writes=[xk])
                    P.add("dve", lambda e: e.tensor_scalar(out=m1[:, :, :], in0=x, scalar1=-math.pi, scalar2=None, op0=ALU.is_lt), reads=[xk], writes=["m1"])
                    P.add("dve", lambda e: e.scalar_tensor_tensor(out=x, in0=m1[:, :, :], scalar=TWO_PI, in1=x, op0=ALU.mult, op1=ALU.add), reads=["m1", xk], writes=[xk])
                fold(ang[:, :, :], "ang")
                P.add("act", lambda e: e.activation(out=sn[:, :, :], in_=ang[:, :, :], func=AF.Sin), reads=["ang"], writes=["sn"])
                P.add("dve", lambda e: e.tensor_scalar(out=ang[:, :, :], in0=ang[:, :, :], scalar1=math.pi / 2, scalar2=None, op0=ALU.add), reads=["ang"], writes=["ang"])
                fold(ang[:, :, :], "ang")
                P.add("act", lambda e: e.activation(out=cs[:, :, :], in_=ang[:, :, :], func=AF.Sin), reads=["ang"], writes=["cs"])
                tt("dve", Etab[:, d_, od, 0, :, :], mag[:, :, :], cs[:, :, :], ALU.mult, ["mag", "cs"], ["E" + dk])
                tt("dve", Etab[:, d_, od, 1, :, :], mag[:, :, :], sn[:, :, :], ALU.mult, ["mag", "sn"], ["E" + dk])
                if od == 0:
                    P.add("act", lambda e: e.activation(out=q1[:, :], in_=rho[:, :], func=AF.Exp, scale=-16.0), reads=["rho"], writes=["q1"])
                    tt("dve", inv16[:, d_, 0, :], q1[:, :], cs[:, :, 16], ALU.mult, ["q1", "cs"], ["inv16"])
                    P.add("dve", lambda e, d_=d_: e.scalar_tensor_tensor(out=inv16[:, d_, 1, :], in0=q1[:, :], scalar=-1.0, in1=sn[:, :, 16], op0=ALU.mult, op1=ALU.mult), reads=["q1", "sn"], writes=["inv16"])
            are, aim = Etab[:, d_, 0, 0, :, 1], Etab[:, d_, 0, 1, :, 1]
            P.add("dve", lambda e, are=are: e.tensor_scalar(out=q1[:, :], in0=are, scalar1=-1.0, scalar2=None, op0=ALU.add), reads=["E" + dk], writes=["q1"])
            tt("dve", q2[:, :], lT[:, 0, :], lT[:, 0, :], ALU.mult, ["lT"], ["q2"])
            tt("dve", q3[:, :], lT[:, 1, :], lT[:, 1, :], ALU.mult, ["lT"], ["q3"])
            tt("dve", q2[:, :], q2[:, :], q3[:, :], ALU.add, ["q2", "q3"], ["q2"])
            P.add("dve", lambda e: e.reciprocal(out=q2[:, :], in_=q2[:, :]), reads=["q2"], writes=["q2"])
            tt("dve", q3[:, :], q1[:, :], lT[:, 0, :], ALU.mult, ["q1", "lT"], ["q3"])
            tt("dve", q4[:, :], aim, lT[:, 1, :], ALU.mult, ["E" + dk, "lT"], ["q4"])
            tt("dve", q3[:, :], q3[:, :], q4[:, :], ALU.add, ["q3", "q4"], ["q3"])
            tt("dve", cf[:, d_, 0, :], q3[:, :], q2[:, :], ALU.mult, ["q3", "q2"], ["cf"])
            tt("dve", q3[:, :], aim, lT[:, 0, :], ALU.mult, ["E" + dk, "lT"], ["q3"])
            tt("dve", q4[:, :], q1[:, :], lT[:, 1, :], ALU.mult, ["q1", "lT"], ["q4"])
            tt("dve", q3[:, :], q3[:, :], q4[:, :], ALU.subtract, ["q3", "q4"], ["q3"])
            tt("dve", cf[:, d_, 1, :], q3[:, :], q2[:, :], ALU.mult, ["q3", "q2"], ["cf"])
            for ri, nm in ((0, "ssm_b_re"), (1, "ssm_b_im")):
                for g4 in range(4):
                    src = ssm[nm][d_][g4 * 8:(g4 + 1) * 8].rearrange("g p h -> p g h")
                    P.add("sp", lambda e, src=src, ri=ri, g4=g4, d_=d_: e.dma_start(out=bp[:, d_, ri, g4 * 8:(g4 + 1) * 8, :], in_=src), writes=["bp"], dma=True)
            crb = cf[:, d_, 0, :].unsqueeze(2).broadcast_to([64, 32, 16])
            cib = cf[:, d_, 1, :].unsqueeze(2).broadcast_to([64, 32, 16])
            cmul("dve", bbp[:, d_, 0, :, :], bbp[:, d_, 1, :, :], crb, cib, bp[:, d_, 0, :, :], bp[:, d_, 1, :, :], tb1[:, :, :], tb2[:, :, :], ["cf", "bp"], "bbp", "tb")
            for ri, nm in ((0, "ssm_c_re"), (1, "ssm_c_im")):
                src = ssm[nm][d_].rearrange("g h p -> (g h) p").rearrange("(c q) p -> q c p", q=128)
                P.add("sp", lambda e, src=src: e.dma_start(out=craw[:, :, :], in_=src), writes=["craw"], dma=True)
                for c4 in range(4):
                    P.add("pe", lambda e, c4=c4: e.transpose(pb[6][0:64, c4 * 128:(c4 + 1) * 128], craw[:, c4, :], ident[:, :]), reads=["craw", "ident"], writes=[pbk[6]])
                P.add("act", lambda e, ri=ri, d_=d_: e.copy(out=cT[:, d_, ri, :], in_=pb[6][0:64, :]), reads=[pbk[6]], writes=["cT"])
            a16r, a16i = Etab[:, d_, 0, 0, :, 16], Etab[:, d_, 0, 1, :, 16]
            order = list(range(16)) if d_ == 0 else list(range(15, -1, -1))
            for n_, i_ in enumerate(order):
                if n_ == 0:
                    P.add("dve", lambda e, i_=i_, d_=d_, a16r=a16r: e.tensor_copy(out=PW[:, 0, d_, :, i_], in_=a16r), reads=["E" + dk], writes=["PW"])
                    P.add("dve", lambda e, i_=i_, d_=d_, a16i=a16i: e.tensor_copy(out=PW[:, 1, d_, :, i_], in_=a16i), reads=["E" + dk], writes=["PW"])
                else:
                    ip = order[n_ - 1]
                    cmul("dve", PW[:, 0, d_, :, i_], PW[:, 1, d_, :, i_], PW[:, 0, d_, :, ip], PW[:, 1, d_, :, ip], a16r, a16i, q1[:, :], q2[:, :], ["PW", "E" + dk], "PW", "q")
        for b in range(4):
            g0 = b * GB
            for d_ in range(2):
                dk = "d%d" % d_
                if d_ == 0:
                    pa_re, pa_im = Etab[:, 0, 1, 0, g0:g0 + GB, 1:17], Etab[:, 0, 1, 1, g0:g0 + GB, 1:17]
                    pc_re, pc_im = Etab[:, 0, 0, 0, g0:g0 + GB, 1:17], Etab[:, 0, 0, 1, g0:g0 + GB, 1:17]
                else:
                    pa_re, pa_im = Etab[:, 1, 0, 0, g0:g0 + GB, 0:16], Etab[:, 1, 0, 1, g0:g0 + GB, 0:16]
                    pc_re, pc_im = Etab[:, 1, 1, 0, g0:g0 + GB, 0:16], Etab[:, 1, 1, 1, g0:g0 + GB, 0:16]
                shp = [64, GB, 16, 16]
                bre = bbp[:, d_, 0, g0:g0 + GB, :].unsqueeze(2).broadcast_to(shp)
                bim = bbp[:, d_, 1, g0:g0 + GB, :].unsqueeze(2).broadcast_to(shp)
                cmul("dve", WAp[:, 0], WAp[:, 1], bre, bim, pa_re.unsqueeze(3).broadcast_to(shp), pa_im.unsqueeze(3).broadcast_to(shp), u1[:, :, :, :], u2[:, :, :, :], ["bbp", "E" + dk], "WAp", "u")
                ivr = inv16[:, d_, 0, g0:g0 + GB].unsqueeze(2).unsqueeze(3).broadcast_to(shp)
                ivi = inv16[:, d_, 1, g0:g0 + GB].unsqueeze(2).unsqueeze(3).broadcast_to(shp)
                cmul("dve", WA2[:, 0], WA2[:, 1], WAp[:, 0], WAp[:, 1], ivr, ivi, u1[:, :, :, :], u2[:, :, :, :], ["WApre", "WApim", "inv16"], "WA2", "u")
                cre = cT[:, d_, 0, g0 * 16:(g0 + GB) * 16].rearrange("p (g h) -> p g h", h=16).unsqueeze(2).broadcast_to(shp)
                cim = cT[:, d_, 1, g0 * 16:(g0 + GB) * 16].rearrange("p (g h) -> p g h", h=16).unsqueeze(2).broadcast_to(shp)
                cmul("dve", WCp[:, 0], WCp[:, 1], cre, cim, pc_re.unsqueeze(3).broadcast_to(shp), pc_im.unsqueeze(3).broadcast_to(shp), u1[:, :, :, :], u2[:, :, :, :], ["cT", "E" + dk], "WCp", "u", neg_im=True)
                for ri in range(2):
                    P.add("pool", lambda e, ri=ri, d_=d_: e.tensor_copy(out=WCst[:, d_, :, ri, :], in_=WCp[:, ri].rearrange("p g t h -> p g (t h)")), reads=["WCpre", "WCpim"], writes=["WCst"])
                for hf in range(2):
                    for gq in range(GB // 4):
                        for gi in range(4):
                            gl = gq * 4 + gi
                            for ri in range(2):
                                col = (gi * 2 + ri) * 64
                                P.add("pe", lambda e, gl=gl, ri=ri, hf=hf, col=col: e.transpose(pb[6][:, col:col + 64], WAp[:, ri, gl, hf * 8:(hf + 1) * 8, :].rearrange("p s h -> p (s h)"), ident[0:64, 0:64]),
                                      reads=["WApre", "WApim", "ident"], writes=[pbk[6]])
                        P.add("act", lambda e, hf=hf, gq=gq, d_=d_: e.copy(out=WAst[:, d_, hf, gq * 4:(gq + 1) * 4, :, :].rearrange("p g r q -> p (g r q)"), in_=pb[6][:, :]), reads=[pbk[6]], writes=["WAst"])
                for gl in range(GB):
                    for hf in range(2):
                        po = 4 + rr["o"] % 2
                        rr["o"] += 1
                        for ri in range(2):
                            P.add("pe", lambda e, po=po, gl=gl, hf=hf, ri=ri: e.matmul(pb[po][:, 0:256], WA2[:, ri, gl, hf * 8:(hf + 1) * 8, :].rearrange("p s h -> p (s h)"), WCp[:, ri, gl].rearrange("p t h -> p (t h)"), start=(ri == 0), stop=(ri == 1)),
                                  reads=["WA2re", "WA2im", "WCpre", "WCpim"], writes=[pbk[po]])
                        if d_ == 0:
                            P.add("dve", lambda e, po=po, gl=gl, hf=hf: e.tensor_tensor(out=KTs[:, gl, hf, :], in0=pb[po][:, 0:256], in1=msk[:, 0, hf, :], op=ALU.mult), reads=[pbk[po], "msk"], writes=["KTs"])
                        else:
                            P.add("dve", lambda e, po=po, hf=hf: e.tensor_tensor(out=KTt[:, :], in0=pb[po][:, 0:256], in1=msk[:, 1, hf, :], op=ALU.mult), reads=[pbk[po], "msk"], writes=["KTt"])
                            P.add("dve", lambda e, gl=gl, hf=hf: e.tensor_tensor(out=KTb[:, gl, hf, :], in0=KTs[:, gl, hf, :], in1=KTt[:, :], op=ALU.add), reads=["KTs", "KTt"], writes=["KTb"])
            P.add("pool", lambda e, b=b: e.dma_start(out=WAd[b], in_=WAst[:, :, :, :, :, :].rearrange("p d f g r q -> p (d f g r q)")), reads=["WAst"], writes=[("WAd", b)], dma=True)
            P.add("pool", lambda e, b=b: e.dma_start(out=WCd[b], in_=WCst[:, :, :, :, :].rearrange("p d g r c -> p (d g r c)")), reads=["WCst"], writes=[("WCd", b)], dma=True)
            P.add("pool", lambda e, b=b: e.dma_start(out=KTd[b], in_=KTb[:, :, :, :].rearrange("p g f c -> p (g f c)")), reads=["KTb"], writes=[("KTd", b)], dma=True)
        P.barrier()
        st_es.close()

        sm_es = ExitStack()
        WAb = sb("WAb", [128, 2, 2, GB, 2, 64], BF16, sm_es)
        WCb = sb("WCb", [64, 2, GB, 2, 256], BF16, sm_es)
        KTB = sb("KTB", [128, GB, 2, 256], BF16, sm_es)
        Usc = sb("Usc", [128, 16, 512], BF16, sm_es)
        imU2 = [sb("imU%d" % i, [128, NBLK, GB, 2, 128], BF16, sm_es) for i in range(2)]
        Up = sb("Up", [128, GB, 16, 16], BF16, sm_es)
        S2 = [sb("S_%d" % i, [64, 2, 2, GB, NB1], F32, sm_es) for i in range(2)]
        XP2 = [sb("XP%d" % i, [64, 2, 2, GB, NB1], BF16, sm_es) for i in range(2)]
        Cc = sb("Cc", [64, 2, 2, GB, NCH + 1], F32, sm_es)
        ini = sb("ini", [64, 2, 2, GB], F32, sm_es)
        zini = sb("zini", [64, 2, 2, GB], F32, sm_es)
        sT1 = sb("sT1", [64, 2, 2, GB * NCH], F32, sm_es)
        sT2 = sb("sT2", [64, 2, 2, GB * NCH], F32, sm_es)
        sQ1 = sb("sQ1", [64, 2, 2, GB], F32, sm_es)
        sQ2 = sb("sQ2", [64, 2, 2, GB], F32, sm_es)
        ArX = sb("ArX", [64, 2, GB, NCH], F32, sm_es)
        AiX = sb("AiX", [64, 2, 2, GB, NCH], F32, sm_es)
        BrX = sb("BrX", [64, 2, GB], F32, sm_es)
        BiX = sb("BiX", [64, 2, 2, GB], F32, sm_es)
        f1 = [sb("f1_0", [64, GB, NCH, 16], F32, sm_es)] * 2
        f2 = [sb("f2_0", [64, GB, NCH, 16], F32, sm_es)] * 2
        ystg = sb("ystg", [128, 16, GB * 8], F32, sm_es)
        pb6b = pb[6].bitcast(BF16)
        P.add("dve", lambda e: e.memset(zini[:, :, :, :], 0.0), writes=["zini"])
        ENG = ["dve", "pool"]

        def seg_states(s_, tok0, b, bi):
            S_, imU = S2[bi], imU2[bi]
            P.add("sp", lambda e, b=b: e.dma_start(out=WAb[:, :, :, :, :, :].rearrange("p d f g r q -> p (d f g r q)"), in_=WAd[b]), reads=[("WAd", b)], writes=["WAb"], dma=True)
            for blk in range(NBLK):
                t0 = tok0 + blk * BS * T1
                src = us[s_][t0:t0 + BS * T1, :].rearrange("(c t) ch -> c t ch", t=T1)
                P.add("sp", lambda e, src=src: e.dma_start(out=Usc[0:BS, :, :], in_=src), writes=["Usc"], dma=True)
                P.add("pool", lambda e, b=b: e.tensor_copy(out=Up[0:BS, :, :, :], in_=Usc[0:BS, :, b * GB * 16:(b + 1) * GB * 16].rearrange("c s (g h) -> c g s h", g=GB)),
                      reads=["Usc"], writes=["Up"])
                for gl in range(GB):
                    g = b * GB + gl
                    for hf in range(2):
                        P.add("pe", lambda e, gl=gl, hf=hf, g=g: e.transpose(pb6b[:, (gl % 2 * 2 + hf) * 128:(gl % 2 * 2 + hf) * 128 + BS], Up[0:BS, gl, hf * 8:(hf + 1) * 8, :].rearrange("c s h -> c (s h)"), identb[0:BS, 0:BS]),
                              reads=["Up", "identb"], writes=[pbk[6]])
                    if gl % 2 == 1:
                        P.add("act", lambda e, gl=gl, blk=blk: e.copy(out=imU[:, blk, gl - 1:gl + 1, :, 0:BS], in_=pb6b[:, 0:512].rearrange("p (g f c) -> p g f c", g=2, f=2)[:, :, :, 0:BS]),
                              reads=[pbk[6]], writes=[("imU", bi, blk)])
                for d_ in range(2):
                    for ri in range(2):
                        for gq in range(GB // 4):
                            po = 4 + rr["o"] % 2
                            rr["o"] += 1
                            for gi in range(4):
                                gl = gq * 4 + gi
                                for hf in range(2):
                                    P.add("pe", lambda e, po=po, gi=gi, gl=gl, hf=hf, d_=d_, ri=ri, blk=blk: e.matmul(pb[po][0:64, gi * 128:gi * 128 + BS], WAb[:, d_, hf, gl, ri, :], imU[:, blk, gl, hf, 0:BS], start=(hf == 0), stop=(hf == 1)),
                                          reads=["WAb", ("imU", bi, blk)], writes=[pbk[po]])
                            P.add("act", lambda e, po=po, gq=gq, d_=d_, ri=ri, blk=blk: e.copy(out=S_[:, d_, ri, gq * 4:(gq + 1) * 4, blk * BS:(blk + 1) * BS], in_=pb[po][0:64, :].rearrange("p (g c) -> p g c", g=4)[:, :, 0:BS]),
                                  reads=[pbk[po]], writes=[("S", bi, d_)])

        def pair(apf, apb):
            assert [list(x) for x in apf.ap] == [list(x) for x in apb.ap], (apf.ap, apb.ap)
            return bass.AP(tensor=apf.tensor, offset=apf.offset,
                           ap=[list(apf.ap[0]), [apb.offset - apf.offset, 2]] + [list(x) for x in apf.ap[1:]])

        def mkap(t, off, dims):
            pstep = t[:, 0, 0, 0, 0].ap[0]
            return bass.AP(tensor=t.tensor if hasattr(t, "tensor") else t, offset=off, ap=[list(pstep)] + [list(x) for x in dims])

        def seg_scan(b, bi, init, initk, fixup):
            g0 = b * GB
            S_, XP = S2[bi], XP2[bi]
            Sv = [[S_[:, d_, ri].rearrange("p g (c i) -> p g c i", i=16) for ri in range(2)] for d_ in range(2)]
            eng = "dve"
            skb = [("S", bi, 0), ("S", bi, 1)]
            ckb = [("Cc", 0), ("Cc", 1)]
            tk = "sc"
            RS = GB * NB1
            GC = GB * NCH

            def so(d_, ri, i):
                return S_[:, d_, ri, 0, i].offset
            for d_ in range(2):
                i16 = 0 if d_ == 0 else 15
                P.add(eng, lambda e, d_=d_, i16=i16: e.tensor_copy(out=ArX[:, d_, :, :], in_=PW[:, 0, d_, g0:g0 + GB, i16].unsqueeze(2).broadcast_to([64, GB, NCH])), reads=["PW"], writes=["ArX"])
                P.add(eng, lambda e, d_=d_, i16=i16: e.tensor_copy(out=AiX[:, d_, 1, :, :], in_=PW[:, 1, d_, g0:g0 + GB, i16].unsqueeze(2).broadcast_to([64, GB, NCH])), reads=["PW"], writes=["AiX"])
                P.add(eng, lambda e, d_=d_, i16=i16: e.tensor_scalar(out=AiX[:, d_, 0, :, :], in0=PW[:, 1, d_, g0:g0 + GB, i16].unsqueeze(2).broadcast_to([64, GB, NCH]), scalar1=-1.0, scalar2=None, op0=ALU.mult), reads=["PW"], writes=["AiX"])
                i256 = 15 if d_ == 0 else 0
                P.add(eng, lambda e, d_=d_, i256=i256: e.tensor_copy(out=BrX[:, d_, :], in_=PW[:, 0, d_, g0:g0 + GB, i256]), reads=["PW"], writes=["BrX"])
                P.add(eng, lambda e, d_=d_, i256=i256: e.tensor_copy(out=BiX[:, d_, 1, :], in_=PW[:, 1, d_, g0:g0 + GB, i256]), reads=["PW"], writes=["BiX"])
                P.add(eng, lambda e, d_=d_, i256=i256: e.tensor_scalar(out=BiX[:, d_, 0, :], in0=PW[:, 1, d_, g0:g0 + GB, i256], scalar1=-1.0, scalar2=None, op0=ALU.mult), reads=["PW"], writes=["BiX"])
            ArV = mkap(ArX, ArX[:, 0, 0, 0].offset, [[GC, 2], [0, 2], [1, GC]]) if False else None
            arx = bass.AP(tensor=ArX.tensor if hasattr(ArX, "tensor") else ArX, offset=ArX[:, 0, 0, 0].offset, ap=[list(ArX[:, 0, 0, 0].ap[0]), [GC, 2], [0, 2], [1, GC]])
            aix = AiX[:, :, :, :, :].rearrange("p d r g c -> p d r (g c)")
            T1, T2 = sT1[:, :, :, :], sT2[:, :, :, :]

            def XP_(i_f, i_b):
                o = so(0, 0, i_f)
                return mkap(S_, o, [[so(1, 0, i_b) - o, 2], [RS, 2], [16, GC]])

            def XS_(i_f, i_b):
                o = so(0, 1, i_f)
                return mkap(S_, o, [[so(1, 1, i_b) - o, 2], [-RS, 2], [16, GC]])
            for n in range(1, 16):
                pP, pS, cur = XP_(n - 1, 16 - n), XS_(n - 1, 16 - n), XP_(n, 15 - n)
                tt(eng, T1, pP, arx, ALU.mult, skb + ["ArX"], [tk + "1"])
                tt(eng, T2, pS, aix, ALU.mult, skb + ["AiX"], [tk + "2"])
                tt(eng, T1, T1, T2, ALU.add, [tk + "1", tk + "2"], [tk + "1"])
                tt(eng, cur, cur, T1, ALU.add, skb + [tk + "1"], skb)
            CS = GB * (NCH + 1)

            def co(d_, ri, c):
                return Cc[:, d_, ri, 0, c].offset

            def mkc(t, off, dims):
                return bass.AP(tensor=t.tensor if hasattr(t, "tensor") else t, offset=off, ap=[list(Cc[:, 0, 0, 0, 0].ap[0])] + [list(x) for x in dims])
            brx = bass.AP(tensor=BrX.tensor if hasattr(BrX, "tensor") else BrX, offset=BrX[:, 0, 0].offset, ap=[list(BrX[:, 0, 0].ap[0]), [GB, 2], [0, 2], [1, GB]])
            bix = BiX[:, :, :, :]
            Q1, Q2 = sQ1[:, :, :, :], sQ2[:, :, :, :]
            for ri in range(2):
                P.add(eng, lambda e, ri=ri: e.tensor_copy(out=pair(Cc[:, 0, ri, :, 0], Cc[:, 1, ri, :, NCH]), in_=init[:, :, ri, :]), reads=[initk], writes=ckb)
            for n in range(NCH):
                cpf, cnf, cpb, cnb = n, n + 1, NCH - n, NCH - 1 - n
                o = co(0, 0, cpf)
                CpP = mkc(Cc, o, [[co(1, 0, cpb) - o, 2], [CS, 2], [NCH + 1, GB]])
                o = co(0, 1, cpf)
                CpS = mkc(Cc, o, [[co(1, 1, cpb) - o, 2], [-CS, 2], [NCH + 1, GB]])
                o = co(0, 0, cnf)
                CnP = mkc(Cc, o, [[co(1, 0, cnb) - o, 2], [CS, 2], [NCH + 1, GB]])
                o = so(0, 0, n * 16 + 15)
                EP = mkap(S_, o, [[so(1, 0, (NCH - 1 - n) * 16) - o, 2], [RS, 2], [NB1, GB]])
                tt(eng, Q1, CpP, brx, ALU.mult, ckb + ["BrX"], [tk + "1"])
                tt(eng, Q2, CpS, bix, ALU.mult, ckb + ["BiX"], [tk + "2"])
                tt(eng, Q1, Q1, Q2, ALU.add, [tk + "1", tk + "2"], [tk + "1"])
                tt(eng, CnP, EP, Q1, ALU.add, skb + [tk + "1"], ckb)
            for d_ in range(2):
                eng = "dve"
                Sre = Sv[d_][0]
                Sim = Sv[d_][1]
                Cre, Cim = Cc[:, d_, 0], Cc[:, d_, 1]
                sk = ("S", bi, d_)
                ck = ("Cc", d_)
                tk = "scf"
                if not fixup:
                    continue
                shp4 = [64, GB, NCH, 16]
                Pr = PW[:, 0, d_, g0:g0 + GB, :].unsqueeze(2).broadcast_to(shp4)
                Pi = PW[:, 1, d_, g0:g0 + GB, :].unsqueeze(2).broadcast_to(shp4)
                cb0 = 0 if d_ == 0 else 1
                Cbr = Cre[:, :, cb0:cb0 + NCH].unsqueeze(3).broadcast_to(shp4)
                Cbi = Cim[:, :, cb0:cb0 + NCH].unsqueeze(3).broadcast_to(shp4)
                g1, g2 = f1[d_][:, :, :, :], f2[d_][:, :, :, :]
                tt(eng, g1, Pr, Cbr, ALU.mult, [ck, "PW"], [tk + "f1"])
                tt(eng, g2, Pi, Cbi, ALU.mult, [ck, "PW"], [tk + "f2"])
                tt(eng, g1, g1, g2, ALU.subtract, [tk + "f1", tk + "f2"], [tk + "f1"])
                tt(eng, Sre, Sre, g1, ALU.add, [sk, tk + "f1"], [sk])
                tt(eng, g1, Pr, Cbi, ALU.mult, [ck, "PW"], [tk + "f1"])
                tt(eng, g2, Pi, Cbr, ALU.mult, [ck, "PW"], [tk + "f2"])
                tt(eng, g1, g1, g2, ALU.add, [tk + "f1", tk + "f2"], [tk + "f1"])
                tt(eng, Sim, Sim, g1, ALU.add, [sk, tk + "f1"], [sk])
                for ri in range(2):
                    X = S_[:, d_, ri]
                    if d_ == 0:
                        P.add("act", lambda e, ri=ri, X=X, XP=XP: e.copy(out=XP[:, 0, ri, :, 1:NB1], in_=X[:, :, 0:NB1 - 1]), reads=[sk], writes=[("XP", bi, 0)])
                        P.add("act", lambda e, ri=ri, XP=XP: e.copy(out=XP[:, 0, ri, :, 0], in_=init[:, 0, ri, :]), reads=[initk], writes=[("XP", bi, 0)])
                    else:
                        P.add("act", lambda e, ri=ri, X=X, XP=XP: e.copy(out=XP[:, 1, ri, :, 0:NB1 - 1], in_=X[:, :, 1:NB1]), reads=[sk], writes=[("XP", bi, 1)])
                        P.add("act", lambda e, ri=ri, XP=XP: e.copy(out=XP[:, 1, ri, :, NB1 - 1], in_=init[:, 1, ri, :]), reads=[initk], writes=[("XP", bi, 1)])

        def seg_outputs(s_, b, bi):
            imU, XP = imU2[bi], XP2[bi]
            P.add("sp", lambda e, b=b: e.dma_start(out=WCb[:, :, :, :, :].rearrange("p d g r c -> p (d g r c)"), in_=WCd[b]), reads=[("WCd", b)], writes=["WCb"], dma=True)
            P.add("sp", lambda e, b=b: e.dma_start(out=KTB[:, :, :, :].rearrange("p g f c -> p (g f c)"), in_=KTd[b]), reads=[("KTd", b)], writes=["KTB"], dma=True)
            for blk in range(NBLK):
                for gq in range(GB // 2):
                    po = 4 + rr["o"] % 2
                    rr["o"] += 1
                    for gi in range(2):
                        gl = gq * 2 + gi
                        col = gi * 256
                        for hf in range(2):
                            P.add("pe", lambda e, po=po, col=col, gl=gl, hf=hf, blk=blk: e.matmul(pb[po][0:BS, col:col + 256], imU[:, blk, gl, hf, 0:BS], KTB[:, gl, hf, :], start=(hf == 0), stop=False),
                                  reads=[("imU", bi, blk), "KTB"], writes=[pbk[po]])
                        for d_ in range(2):
                            for ri in range(2):
                                P.add("pe", lambda e, po=po, col=col, gl=gl, d_=d_, ri=ri, blk=blk: e.matmul(pb[po][0:BS, col:col + 256], XP[:, d_, ri, gl, blk * BS:(blk + 1) * BS], WCb[:, d_, gl, ri, :], start=False, stop=(d_ == 1 and ri == 1)),
                                      reads=[("XP", bi, d_), "WCb"], writes=[pbk[po]])
                    P.add("act", lambda e, po=po, gq=gq: e.copy(out=ystg[0:BS, :, (gq % 2) * 32:(gq % 2 + 1) * 32].rearrange("c t (g h) -> c g t h", g=2), in_=pb[po][0:BS, :].rearrange("c (g t h) -> c g t h", g=2, t=16)),
                          reads=[pbk[po]], writes=["ystg"])
                    if gq % 2 == 1:
                        t0 = blk * BS * T1
                        c0 = b * 128 + (gq // 2) * 64
                        dst = yscr[s_][t0:t0 + BS * T1, c0:c0 + 64].rearrange("(c t) ch -> c t ch", t=T1)
                        P.add("pool", lambda e, dst=dst: e.dma_start(out=dst, in_=ystg[0:BS, :, :]), reads=["ystg"], writes=[("yscr", s_, blk, b, gq // 2)], dma=True)

        seqsS = range(NSEQ) if dbg != "S1" else range(1)
        if dbg == "Ss":
            seqsS = range(0)
        units = []
        for s_ in seqsS:
            for b in range(4):
                if s_ == 0 and NCTX:
                    units.append((s_, b, "ctx"))
                units.append((s_, b, "own"))

        def do_states(k):
            s_, b, kind = units[k]
            seg_states(s_, NOWN if kind == "ctx" else 0, b, k % 2)
        if units:
            do_states(0)
        for k, (s_, b, kind) in enumerate(units):
            if k + 1 < len(units):
                do_states(k + 1)
            bi = k % 2
            if kind == "ctx":
                seg_scan(b, bi, zini, "zini", fixup=False)
                for ri in range(2):
                    P.add("dve", lambda e, ri=ri: e.tensor_scalar(out=ini[:, 0, ri, :], in0=Cc[:, 0, ri, :, NCH], scalar1=flg[0:64, 0:1], scalar2=None, op0=ALU.mult), reads=[("Cc", 0), "flg"], writes=["ini"])
                    P.add("dve", lambda e, ri=ri: e.tensor_scalar(out=ini[:, 1, ri, :], in0=Cc[:, 1, ri, :, 0], scalar1=flg[0:64, 1:2], scalar2=None, op0=ALU.mult), reads=[("Cc", 1), "flg"], writes=["ini"])
            else:
                if s_ == 0 and NCTX:
                    seg_scan(b, bi, ini, "ini", fixup=True)
                else:
                    seg_scan(b, bi, zini, "zini", fixup=True)
                seg_outputs(s_, b, bi)
        P.barrier()
        sm_es.close()

        se_es = ExitStack()
        yb = sb("yb", [128, 16, 512], F32, se_es)
        tb = sb("tbb", [128, 16, 512], F32, se_es)
        ub = sb("ub", [128, 16, 512], BF16, se_es)
        zb = sb("zb", [128, 16, 512], BF16, se_es)
        ogb = sb("ogb", [128, 16, 512], BF16, se_es)
        zT = [sb("zT%d" % i, [128, 4, 128], BF16, se_es) for i in range(2)]
        gt = [sb("gt%d" % i, [128, 512], F32, se_es) for i in range(2)]
        Dt = sb("Dt", [128, 512], F32, se_es)
        Gb = sb("Gb", [128, 512], F32, se_es)
        gluw = sb("gluw", [128, 4, 512], BF16, se_es)
        pb6b = pb[6].bitcast(BF16)
        P.add("sp", lambda e: e.dma_start(out=Dt[:, :], in_=ssm["ssm_d"].partition_broadcast(128)), writes=["Dt"], dma=True)
        P.add("sp", lambda e: e.dma_start(out=Gb[:, :], in_=ssm["ssm_glu_b"].partition_broadcast(128)), writes=["Gb"], dma=True)
        P.add("sp", lambda e: e.dma_start(out=gluw[:, :, :].rearrange("p k c -> p (k c)"), in_=wdglu[:, :]), reads=[("wdglu",)], writes=["gluw"], dma=True)
        shp = [BS, 16, 512]
        for s_ in (seqsS if dbg != "Sm" else range(0)):
            for blk in range(NBLK):
                t0 = blk * BS * T1
                P.add("sp", lambda e, s_=s_, t0=t0: e.dma_start(out=yb[0:BS, :, :], in_=yscr[s_][t0:t0 + BS * T1, :].rearrange("(c t) ch -> c t ch", t=T1)), writes=["yb"], dma=True)
                P.add("sp", lambda e, s_=s_, t0=t0: e.dma_start(out=ub[0:BS, :, :], in_=us[s_][t0:t0 + BS * T1, :].rearrange("(c t) ch -> c t ch", t=T1)), writes=["ub"], dma=True)
                Y, Tt, Uu, Zz = yb[0:BS, :, :], tb[0:BS, :, :], ub[0:BS, :, :], zb[0:BS, :, :]
                tt("pool", Tt, Uu, Dt[0:BS, :].unsqueeze(1).broadcast_to(shp), ALU.mult, ["ub", "Dt"], ["tb"])
                tt("dve", Y, Y, Tt, ALU.add, ["yb", "tb"], ["yb"])
                P.add("act", lambda e, Y=Y, Tt=Tt: e.activation(out=Tt, in_=Y, func=AF.Square), reads=["yb"], writes=["tb"])
                P.add("dve", lambda e, Tt=Tt: e.tensor_scalar(out=Tt, in0=Tt, scalar1=0.044715, scalar2=1.0, op0=ALU.mult, op1=ALU.add), reads=["tb"], writes=["tb"])
                tt("pool", Tt, Tt, Y, ALU.mult, ["tb", "yb"], ["tb"])
                P.add("act", lambda e, Tt=Tt: e.activation(out=Tt, in_=Tt, func=AF.Sigmoid, scale=2.0 * math.sqrt(2.0 / math.pi)), reads=["tb"], writes=["tb"])
                tt("dve", Zz, Y, Tt, ALU.mult, ["yb", "tb"], ["zb"])
                for t_ in range(16):
                    zt = zT[t_ % 2]
                    for kc in range(4):
                        P.add("pe", lambda e, t_=t_, kc=kc: e.transpose(pb6b[:, kc * 128:kc * 128 + BS], zb[0:BS, t_, kc * 128:(kc + 1) * 128], identb[0:BS, 0:BS]), reads=["zb", "identb"], writes=[pbk[6]])
                    P.add("act", lambda e, zt=zt: e.copy(out=zt[:, :, 0:BS], in_=pb6b[:, 0:512].rearrange("p (k c) -> p k c", k=4)[:, :, 0:BS]), reads=[pbk[6]], writes=[("zT", t_ % 2)])
                    po = 4 + rr["o"] % 2
                    rr["o"] += 1
                    for kc in range(4):
                        P.add("pe", lambda e, po=po, kc=kc, zt=zt: e.matmul(pb[po][0:BS, :], zt[:, kc, 0:BS], gluw[:, kc, :], start=(kc == 0), stop=(kc == 3)), reads=[("zT", t_ % 2), "gluw"], writes=[pbk[po]])
                    g_ = gt[t_ % 2]
                    P.add("dve", lambda e, po=po, g_=g_: e.tensor_tensor(out=g_[0:BS, :], in0=pb[po][0:BS, :], in1=Gb[0:BS, :], op=ALU.add), reads=[pbk[po], "Gb"], writes=[("gt", t_ % 2)])
                    P.add("act", lambda e, g_=g_: e.activation(out=g_[0:BS, :], in_=g_[0:BS, :], func=AF.Sigmoid), reads=[("gt", t_ % 2)], writes=[("gt", t_ % 2)])
                    P.add("pool", lambda e, g_=g_, t_=t_: e.tensor_tensor(out=ogb[0:BS, t_, :], in0=zb[0:BS, t_, :], in1=g_[0:BS, :], op=ALU.mult), reads=[("gt", t_ % 2), "zb"], writes=["ogb"])
                dst = oss[s_][t0:t0 + BS * T1, :].rearrange("(c t) ch -> c t ch", t=T1)
                P.add("pool", lambda e, dst=dst: e.dma_start(out=dst, in_=ogb[0:BS, :, :]), reads=["ogb"], writes=[("oss", s_, blk)], dma=True)
        P.barrier()
        se_es.close()

    if dbg in (None, "C"):
        pc_es = ExitStack()
        wslots[:] = [sb("wslotC%d" % i, [128, 4096], BF16, pc_es) for i in range(NWS)]
        bA = sb("cA", [128, 8, 512], F32, pc_es)
        bB = sb("cB", [128, 8, 512], F32, pc_es)
        bC = sb("cC", [128, 8, 512], F32, pc_es)
        bD = sb("cD", [128, 8, 512], BF16, pc_es)
        bE = sb("cE", [128, 8, 512], BF16, pc_es)
        QmT = sb("QmT", [128, 8, 512], BF16, pc_es)
        omt = sb("omt", [128, 4, 1024], BF16, pc_es)
        odt = sb("odt", [128, 2, 4, 512], BF16, pc_es)
        gTc = sb("gTc", [128, NJ, 512], BF16, pc_es)
        sgc = [sb("sgc%d" % i, [128, 512], F32, pc_es) for i in range(2)]
        tmpC = {"rb": sb("rbc", [128, 2, 512], BF16, pc_es), "rsq": sb("rsqc", [128, 2, 512], BF16, pc_es),
                "mean": sb("meanc", [128, 512], F32, pc_es), "rstd": sb("rstdc", [128, 512], F32, pc_es), "t1": sb("t1c", [128, 512], F32, pc_es)}
        memt = sb("memt", [128, 2, D], F32, pc_es)
        memT = sb("memT", [128, 8, 256], BF16, pc_es)
        KmT = sb("KmT", [128, 8, 256], BF16, pc_es)
        Vm = sb("Vm", [128, 2, 4, 257], BF16, pc_es)
        ETm = [sb("ETm%d" % i, [128, 512], BF16, pc_es) for i in range(4)]
        smc = sb("smc", [128, 4], F32, pc_es)
        yt = bB[:, :, :].rearrange("p a b -> p (a b)").rearrange("p (st d) -> p st d", st=4)
        pb6b = pb[6].bitcast(BF16)
        P.add("dve", lambda e: e.memset(Vm[:, :, :, 256:257], 1.0), writes=["Vm1"])

        def tr_tok2feat(src_fn, dstT, dkey, rkeys):
            for kc in range(8):
                bk = 6 + kc % 2
                pbb = pb[bk].bitcast(BF16)
                for st in range(4):
                    P.add("pe", lambda e, kc=kc, st=st, pbb=pbb: e.transpose(pbb[:, st * 128:(st + 1) * 128], src_fn(kc, st), identb[:, :]),
                          reads=rkeys + ["identb"], writes=[pbk[bk]])
                eng_ = "act" if kc % 2 == 0 else "dve"
                if eng_ == "act":
                    P.add("act", lambda e, kc=kc, pbb=pbb: e.copy(out=dstT[:, kc, :], in_=pbb[:, 0:512]), reads=[pbk[bk]], writes=[(dkey, kc)])
                else:
                    P.add("dve", lambda e, kc=kc, pbb=pbb: e.tensor_copy(out=dstT[:, kc, :], in_=pbb[:, 0:512]), reads=[pbk[bk]], writes=[(dkey, kc)])

        def proj_fm(wd, wname, xb, xbk, evac):
            for half in range(2):
                wt, wk = wload(wd[half], (wname, half))
                wv = wt[:, :].rearrange("p (kc c) -> p kc c", kc=8)
                for c4 in range(4):
                    po = 4 + rr["o"] % 2
                    rr["o"] += 1
                    for kc in range(8):
                        P.add("pe", lambda e, po=po, kc=kc, c4=c4, wv=wv: e.matmul(pb[po][:, :], wv[:, kc, c4 * 128:(c4 + 1) * 128], xb[:, kc, :], start=(kc == 0), stop=(kc == 7)),
                              reads=[wk, (xbk, kc)], writes=[pbk[po]])
                    evac(half * 4 + c4, pb[po], pbk[po])

        seqsC = range(NSEQ) if dbg != "C1" else range(1)
        for s in seqsC:
            msrc = mem[s].rearrange("(kb p) d -> p kb d", p=128)
            P.add("sp", lambda e, msrc=msrc: e.dma_start(out=memt[:, :, :], in_=msrc), writes=["memt"], dma=True)
            for kc in range(8):
                for kb in range(2):
                    P.add("pe", lambda e, kc=kc, kb=kb: e.transpose(pb[6][:, kb * 128:(kb + 1) * 128], memt[:, kb, kc * 128:(kc + 1) * 128], ident[:, :]),
                          reads=["memt", "ident"], writes=[pbk[6]])
                P.add("act", lambda e, kc=kc: e.copy(out=memT[:, kc, :], in_=pb[6][:, 0:256]), reads=[pbk[6]], writes=[("memT", kc)])
            for cg in range(4):
                wt, wk = wload(wdkv[cg], ("mem_wkv", cg))
                wv = wt[:, :].rearrange("p (kc c) -> p kc c", kc=8)
                if cg < 2:
                    for c4 in range(4):
                        po = 4 + rr["o"] % 2
                        rr["o"] += 1
                        for kc in range(8):
                            P.add("pe", lambda e, po=po, kc=kc, c4=c4, wv=wv: e.matmul(pb[po][:, 0:256], wv[:, kc, c4 * 128:(c4 + 1) * 128], memT[:, kc, :], start=(kc == 0), stop=(kc == 7)),
                                  reads=[wk, ("memT", kc)], writes=[pbk[po]])
                        P.add("act", lambda e, po=po, cc=cg * 4 + c4: e.copy(out=KmT[:, cc, :], in_=pb[po][:, 0:256]), reads=[pbk[po]], writes=[("KmT", cg * 4 + c4)])
                else:
                    for kb in range(2):
                        po = 4 + rr["o"] % 2
                        rr["o"] += 1
                        for kc in range(8):
                            P.add("pe", lambda e, po=po, kc=kc, kb=kb, wv=wv: e.matmul(pb[po][:, :], memT[:, kc, kb * 128:(kb + 1) * 128], wv[:, kc, :], start=(kc == 0), stop=(kc == 7)),
                                  reads=[wk, ("memT", kc)], writes=[pbk[po]])
                        hm0 = (cg - 2) * 2
                        P.add("act", lambda e, po=po, kb=kb, hm0=hm0: e.copy(out=Vm[:, kb, hm0:hm0 + 2, 0:256], in_=pb[po][:, :].rearrange("p (a b) -> p a b", a=2)),
                              reads=[pbk[po], "Vm1"], writes=[("Vm", kb, hm0)])
            ntile = NOWN // TT if dbg != "C1" else 1
            for i in range(ntile):
                src = x1s[s][:, i * TT:(i + 1) * TT].rearrange("(dc p) t -> p dc t", p=128)
                P.add("sp", lambda e, src=src: e.dma_start(out=bA[:, :, :], in_=src), writes=[("bA", dc) for dc in range(8)], dma=True)
                for w_, dsrc in ((0, ods), (1, oss)):
                    src = dsrc[s][i * TT:(i + 1) * TT, :].rearrange("(st p) c -> p st c", p=128)
                    P.add("sp", lambda e, src=src, w_=w_: e.dma_start(out=odt[:, w_, :, :], in_=src), writes=[("odt", w_)], dma=True)
                tr_tok2feat(lambda kc, st: odt[:, kc // 4, st, (kc % 4) * 128:(kc % 4 + 1) * 128], bE, "bE", [("odt", 0), ("odt", 1)])
                proj_fm(wdout, "w_out", bE, "bE",
                        lambda dc, bank, bk: P.add("dve", lambda e: e.scalar_tensor_tensor(out=bB[:, dc, :], in0=bA[:, dc, :], scalar=ALPHA, in1=bank[:, :], op0=ALU.mult, op1=ALU.add),
                                                   reads=[bk, ("bA", dc)], writes=[("bB", dc)]))
                layernorm(bB, "bB", "ln_mix_g", "ln_mix_b", bC, bD, "bC", "bD", tmpC)
                proj_fm(wdq, "mem_wq", bD, "bD",
                        lambda dc, bank, bk: P.add("act", lambda e: e.copy(out=QmT[:, dc, :], in_=bank[:, :]), reads=[bk], writes=[("QmT", dc)]))
                for hm in range(4):
                    for kb in range(2):
                        sbk = rr["ab"] % 4
                        rr["ab"] += 1
                        for dd in range(2):
                            P.add("pe", lambda e, sbk=sbk, hm=hm, kb=kb, dd=dd: e.matmul(pb[sbk][:, :], KmT[:, 2 * hm + dd, kb * 128:(kb + 1) * 128], QmT[:, 2 * hm + dd, :], start=(dd == 0), stop=(dd == 1)),
                                  reads=[("KmT", 2 * hm + dd), ("QmT", 2 * hm + dd)], writes=[pbk[sbk]])
                        P.add("act", lambda e, sbk=sbk, kb=kb, hm=hm: e.activation(out=ETm[(hm % 2) * 2 + kb][:, :], in_=pb[sbk][:, :], func=AF.Exp, scale=1.0 / 16), reads=[pbk[sbk]], writes=[("ETm", (hm % 2) * 2 + kb)])
                    for qb in range(4):
                        po = 4 + rr["o"] % 2
                        rr["o"] += 1
                        for kb in range(2):
                            P.add("pe", lambda e, po=po, hm=hm, kb=kb, qb=qb: e.matmul(pb[po][:, 0:257], ETm[(hm % 2) * 2 + kb][:, qb * 128:(qb + 1) * 128], Vm[:, kb, hm, :], start=(kb == 0), stop=(kb == 1)),
                                  reads=[("ETm", (hm % 2) * 2 + kb), ("Vm", kb, (hm // 2) * 2), "Vm1"], writes=[pbk[po]])
                        P.add("dve", lambda e, po=po, qb=qb: e.reciprocal(out=smc[:, qb:qb + 1], in_=pb[po][:, 256:257]), reads=[pbk[po]], writes=[("smc", qb)])
                        P.add("dve", lambda e, po=po, qb=qb, hm=hm: e.tensor_scalar(out=omt[:, qb, hm * 256:(hm + 1) * 256], in0=pb[po][:, 0:256], scalar1=smc[:, qb:qb + 1], scalar2=None, op0=ALU.mult),
                              reads=[pbk[po], ("smc", qb)], writes=[("omt", qb)])
                tr_tok2feat(lambda kc, st: omt[:, st, kc * 128:(kc + 1) * 128], bE, "bE", [("omt", q) for q in range(4)])
                proj_fm(wdo, "mem_wo", bE, "bE",
                        lambda dc, bank, bk: P.add("dve", lambda e: e.scalar_tensor_tensor(out=bB[:, dc, :], in0=bC[:, dc, :], scalar=ALPHA, in1=bank[:, :], op0=ALU.mult, op1=ALU.add),
                                                   reads=[bk, ("bC", dc)], writes=[("bB", dc)]))
                layernorm(bB, "bB", "ln_mem_g", "ln_mem_b", bA, bD, "bA", "bD", tmpC)
                ffn(1, bD, "bD", bA, "bA", bB, "bB", gTc, sgc)
                layernorm(bB, "bB", "ln_ffn2_g", "ln_ffn2_b", bC, bD, "bC", "bD", tmpC)
                for st in range(4):
                    for dg in range(2):
                        for c4 in range(4):
                            dc = dg * 4 + c4
                            P.add("pe", lambda e, st=st, dc=dc, c4=c4: e.transpose(pb[6][:, c4 * 128:(c4 + 1) * 128], bC[:, dc, st * 128:(st + 1) * 128], ident[:, :]),
                                  reads=[("bC", dc), "ident"], writes=[pbk[6]])
                        P.add("act", lambda e, st=st, dg=dg: e.copy(out=yt[:, st, dg * 512:(dg + 1) * 512], in_=pb[6][:, :]),
                              reads=[pbk[6]] + [("bB", dc) for dc in range(8)], writes=[("yt", st)])
                ydst = (yp if s == 0 else ys[s - 1])[i * TT:(i + 1) * TT, :].rearrange("(st p) d -> p st d", p=128)
                P.add("pool", lambda e, ydst=ydst: e.dma_start(out=ydst, in_=yt), reads=[("yt", st) for st in range(4)], writes=[("bB", dc) for dc in range(8)], dma=True)
        P.barrier()
        pc_es.close()

    P.barrier()
    P.add("sp", lambda e: e.dma_start(out=flg[:, :], in_=flags[:, :]), writes=["flg"], dma=True)
    P.barrier()
    P.add("pool", lambda e: e.memset(epsc[:, :], LN_EPS), writes=["epsc"])
    P.emit()
    es.close()
    return nc


_CACHE = {}


def _prep_inputs(inputs, core, nown, nsamp):
    xpr = inputs["x_prompt"]; xsm = inputs["x_sample"]
    b, half = core // 2, core % 2
    d = {}
    d["xp_own"] = np.ascontiguousarray(xpr[b, half * nown:(half + 1) * nown])
    d["xp_ctx"] = np.ascontiguousarray(xpr[b, (1 - half) * nown:(2 - half) * nown])
    d["xs"] = np.ascontiguousarray(xsm[core * nsamp:(core + 1) * nsamp])
    d["mem"] = np.ascontiguousarray(np.concatenate([inputs["mem_prompt"][b:b + 1], inputs["mem_sample"][core * nsamp:(core + 1) * nsamp]], 0))
    fl = np.zeros((128, 2), np.float32)
    fl[:, 0] = 1.0 if half == 1 else 0.0
    fl[:, 1] = 1.0 if half == 0 else 0.0
    d["flags"] = fl
    d["ident"] = np.eye(128, dtype=np.float32)
    d["antiid"] = np.ascontiguousarray(np.eye(128, dtype=np.float32)[::-1])
    d["bucket_oh"] = _bucket_table()
    kv = np.zeros((64, 2, 17), np.float32)
    kv[:, 0, :] = np.arange(17); kv[:, 1, :] = 16 - np.arange(17)
    d["kv17"] = kv
    sl = (np.arange(128) // 16)[:, None, None] + 8 * np.arange(2)[None, :, None]
    tl = (np.arange(256) // 16)[None, None, :]
    mk = np.zeros((128, 2, 2, 256), np.float32)
    mk[:, 0] = (sl <= tl); mk[:, 1] = (sl >= tl)
    d["ssm_mask"] = mk
    for nm in ["ffn1_w13", "ffn1_w2", "ffn2_w13", "ffn2_w2", "w_in", "w_out", "mem_wq", "mem_wkv", "mem_wo", "ssm_glu_w"]:
        d[nm] = np.ascontiguousarray(inputs[nm][0])
    for nm in ["ln_ffn1_g", "ln_ffn1_b", "ln_mix_g", "ln_mix_b", "ln_mem_g", "ln_mem_b", "ln_ffn2_g", "ln_ffn2_b",
               "ssm_lam_re", "ssm_lam_im", "ssm_log_step", "ssm_b_re", "ssm_b_im", "ssm_c_re", "ssm_c_im", "ssm_d", "ssm_glu_b"]:
        d[nm] = np.ascontiguousarray(inputs[nm][0])
    d["diff_lambda"] = np.ascontiguousarray(inputs["diff_lambda"][0].reshape(256))
    d["diff_subln_g"] = np.ascontiguousarray(inputs["diff_subln_g"][0])
    d["rel_bias"] = np.ascontiguousarray(inputs["rel_bias"])
    return {k: np.asarray(v, np.float32) for k, v in d.items()}


def kernel(_debug=None, **inputs):
    import os
    _probe = os.environ.get("K_PROBE")
    xpr = inputs["x_prompt"]; xsm = inputs["x_sample"]
    nown = xsm.shape[1]
    assert xpr.shape[1] == 2 * nown and xpr.shape[0] * 2 == 8
    nsamp = xsm.shape[0] // 8
    key = (nown, nsamp, _debug or _probe)
    if key not in _CACHE:
        _CACHE[key] = build(Cfg(nown, True, nsamp, debug=_debug or _probe))
    nc = _CACHE[key]
    in_maps = [_prep_inputs(inputs, c, nown, nsamp) for c in range(8)]
    if _debug is not None:
        import os
        n = int(os.environ.get("K_NCORES", "8"))
        res = run_bass_kernel_spmd(nc, in_maps[:n], core_ids=list(range(n)))
        return res.results
    res = run_bass_kernel_spmd(nc, in_maps, core_ids=list(range(8)))
    if _debug is not None:
        return res.results
    y_p = np.empty(xpr.shape, np.float32)
    y_s = np.empty(xsm.shape, np.float32)
    for c in range(8):
        r = res.results[c]
        y_p[c // 2, (c % 2) * nown:(c % 2 + 1) * nown] = r["yp"]
        y_s[c * nsamp:(c + 1) * nsamp] = r["ys"]
    return (y_p, y_s)
```

```python
import math
from contextlib import ExitStack

import numpy as np
import concourse.bass as bass
import concourse.mybir as mybir
from concourse.bass_utils import run_bass_kernel_spmd

F32 = mybir.dt.float32
BF16 = mybir.dt.bfloat16
I32 = mybir.dt.int32
ALU = mybir.AluOpType
AF = mybir.ActivationFunctionType
AX = mybir.AxisListType

D = 1024
DFF = 2816
NJ = DFF // 128
ALPHA = 2.0 ** 0.25
LN_EPS = 1e-5
LAM_INIT = 0.8 - 0.6 * math.exp(0.0)
NG = 32
NP_ = 64
T1 = 16

ENGS = ["pe", "act", "dve", "pool", "sp"]
BLOCKNAME = {"pe": "tensor", "act": "scalar", "dve": "vector", "pool": "gpsimd", "sp": "sync"}


class Op:
    __slots__ = ("eng", "fn", "deps", "signal", "semval", "is_dma", "dsem", "dval", "pos")


class Prog:
    def __init__(self, nc, es, ring=None):
        self.nc = nc
        self.ops = {e: [] for e in ENGS}
        self.lastw = {}
        self.readers = {}
        self.sem = {e: es.enter_context(nc.semaphore("s_" + e)) for e in ENGS if e != "sp"}
        ring = ring or {"sp": 12, "pool": 8, "act": 4}
        self.ring = {q: [[es.enter_context(nc.semaphore("d_%s%d" % (q, i))), 0, None] for i in range(n)]
                     for q, n in ring.items()}
        self.ringpos = {q: 0 for q in ring}
        self.bar = {e: [] for e in ENGS}
        self.last = {e: None for e in ENGS}

    def add(self, eng, fn, reads=(), writes=(), dma=False):
        op = Op()
        op.eng = eng; op.fn = fn; op.deps = set(); op.signal = False; op.is_dma = dma
        op.semval = 0; op.dsem = None; op.dval = 0
        pbr = [r for r in reads if isinstance(r, tuple) and r[0] == "pb"]
        if pbr:
            reads = [r for r in reads if not (isinstance(r, tuple) and r[0] == "pb")]
            writes = list(writes) + pbr
        for r in reads:
            w = self.lastw.get(r)
            if w is not None:
                op.deps.add(w)
        for r in writes:
            w = self.lastw.get(r)
            if w is not None:
                op.deps.add(w)
            lastrd = {}
            for rd in self.readers.get(r, ()):
                if rd.is_dma:
                    op.deps.add(rd)
                else:
                    lastrd[rd.eng] = rd
            for rd in lastrd.values():
                op.deps.add(rd)
        for r in writes:
            self.lastw[r] = op
            self.readers[r] = []
        for r in reads:
            self.readers.setdefault(r, []).append(op)
        if self.bar[eng]:
            op.deps.update(self.bar[eng])
            self.bar[eng] = []
        if dma:
            slot = self.ring[eng][self.ringpos[eng] % len(self.ring[eng])]
            self.ringpos[eng] += 1
            if slot[2] is not None:
                op.deps.add(slot[2])
            slot[1] += 16
            slot[2] = op
            op.dsem = slot[0]
            op.dval = slot[1]
        op.deps.discard(op)
        self.ops[eng].append(op)
        self.last[eng] = op
        return op

    def barrier(self):
        pend = [self.last[e] for e in ENGS if self.last[e] is not None]
        for q in self.ring:
            for slot in self.ring[q]:
                if slot[2] is not None:
                    pend.append(slot[2])
        for e in ENGS:
            self.bar[e] = list(pend)
        self.lastw = {}
        self.readers = {}

    def emit(self):
        for e in ENGS:
            for i, op in enumerate(self.ops[e]):
                op.pos = i
        for e in ENGS:
            for op in self.ops[e]:
                keep = set()
                for d in op.deps:
                    if (not d.is_dma) and d.eng == e:
                        if e == "pe":
                            continue
                    keep.add(d)
                op.deps = keep
                for d in op.deps:
                    d.signal = True
        for e in ENGS:
            n = 0
            for op in self.ops[e]:
                if op.signal and not op.is_dma:
                    n += 1
                    op.semval = n
        with self.nc.Block() as block:
            for e in ENGS:
                def body(eng, e=e):
                    seen = {}
                    for op in self.ops[e]:
                        waits = {}
                        for d in op.deps:
                            if d.is_dma:
                                key, val = d.dsem, d.dval
                            else:
                                if d.eng == e and e == "pe":
                                    continue
                                key, val = self.sem[d.eng], d.semval
                            k = id(key)
                            if k not in waits or waits[k][1] < val:
                                waits[k] = (key, val)
                        for k, (key, val) in waits.items():
                            if seen.get(k, 0) < val:
                                eng.wait_ge(key, val)
                                seen[k] = val
                        ins = op.fn(eng)
                        if op.is_dma:
                            ins.then_inc(op.dsem, 16)
                        elif op.signal:
                            ins.then_inc(self.sem[e], 1)
                getattr(block, BLOCKNAME[e])(body)


def _bucket_table():
    i = np.arange(1281)
    rel = 640 - i
    half, max_exact = 16, 8
    ret = np.where(rel > 0, half, 0).astype(np.int32)
    n = np.abs(rel)
    nf = np.maximum(n, 1).astype(np.float32)
    lg = (np.log(nf / np.float32(max_exact)) / np.float32(math.log(128 / max_exact)) * np.float32(half - max_exact))
    large = max_exact + lg.astype(np.float32).astype(np.int32)
    large = np.minimum(large, half - 1)
    b = ret + np.where(n < max_exact, n, large)
    oh = np.zeros((32, 1281), np.float32)
    oh[b, i] = 1.0
    return oh


class Cfg:
    def __init__(self, nown, has_ctx=True, nsamp=2, debug=None):
        self.NOWN = nown
        self.NCTX = nown if has_ctx else 0
        self.nsamp = nsamp
        self.debug = debug
        self.TT = 512
        assert nown % 512 == 0


def build(cfg):
    nc = bass.Bass("TRN2", target_bir_lowering=False)
    NOWN, NCTX, TT = cfg.NOWN, cfg.NCTX, cfg.TT
    NSEQ = 1 + cfg.nsamp
    dbg = cfg.debug

    def din(name, shape, dt=F32):
        return nc.dram_tensor(name, list(shape), dt, kind="ExternalInput").ap()

    def dscr(name, shape, dt, out=False):
        return nc.dram_tensor(name, list(shape), dt, kind="ExternalOutput" if out else "Internal").ap()

    xp_own = din("xp_own", [NOWN, D]); xp_ctx = din("xp_ctx", [NOWN, D])
    xs = din("xs", [cfg.nsamp, NOWN, D])
    mem = din("mem", [NSEQ, 256, D])
    flags = din("flags", [128, 2])
    ident_in = din("ident", [128, 128])
    antiid_in = din("antiid", [128, 128])
    oh_in = din("bucket_oh", [32, 1281])
    kv_in = din("kv17", [64, 2, 17])
    msk_in = din("ssm_mask", [128, 2, 2, 256])
    W = {}
    for nm, shp in [("ffn1_w13", [D, 2 * DFF]), ("ffn1_w2", [DFF, D]), ("ffn2_w13", [D, 2 * DFF]), ("ffn2_w2", [DFF, D]),
                    ("w_in", [D, 2048]), ("w_out", [D, D]), ("mem_wq", [D, D]), ("mem_wkv", [D, 2048]), ("mem_wo", [D, D]),
                    ("ssm_glu_w", [512, 512])]:
        W[nm] = din(nm, shp)
    V = {}
    for nm in ["ln_ffn1_g", "ln_ffn1_b", "ln_mix_g", "ln_mix_b", "ln_mem_g", "ln_mem_b", "ln_ffn2_g", "ln_ffn2_b"]:
        V[nm] = din(nm, [D])
    diff_lambda = din("diff_lambda", [256]); subln_g = din("diff_subln_g", [128]); rel_bias = din("rel_bias", [32, 4])
    ssm = {}
    for nm, shp in [("ssm_lam_re", [2, NG, NP_]), ("ssm_lam_im", [2, NG, NP_]), ("ssm_log_step", [2, NG]),
                    ("ssm_b_re", [2, NG, NP_, 16]), ("ssm_b_im", [2, NG, NP_, 16]),
                    ("ssm_c_re", [2, NG, 16, NP_]), ("ssm_c_im", [2, NG, 16, NP_]),
                    ("ssm_d", [512]), ("ssm_glu_b", [512])]:
        ssm[nm] = din(nm, shp)

    yp = nc.dram_tensor("yp", [NOWN, D], F32, kind="ExternalOutput").ap()
    ys = nc.dram_tensor("ys", [cfg.nsamp, NOWN, D], F32, kind="ExternalOutput").ap()

    wd13 = [dscr("wd13_%d" % f, [11, 128, 4096], BF16) for f in range(2)]
    wd2 = [dscr("wd2_%d" % f, [8, 128, NJ * 128], BF16) for f in range(2)]
    wdin = dscr("wdin", [4, 128, 4096], BF16)
    wdout = dscr("wdout", [2, 128, 4096], BF16)
    wdq = dscr("wdq", [2, 128, 4096], BF16)
    wdkv = dscr("wdkv", [4, 128, 4096], BF16)
    wdo = dscr("wdo", [2, 128, 4096], BF16)
    wdglu = dscr("wdglu", [128, 2048], BF16)
    NK = [NOWN + NCTX] + [NOWN] * cfg.nsamp
    dflag = dbg is not None
    x1s = [dscr("x1s%d" % s, [D, NOWN], F32, out=dflag) for s in range(NSEQ)]
    qTs = [dscr("qTs%d" % s, [512, NOWN], BF16, out=dflag) for s in range(NSEQ)]
    kTs = [dscr("kTs%d" % s, [512, NK[s]], BF16, out=dflag) for s in range(NSEQ)]
    vs = [dscr("vs%d" % s, [NK[s], 512], BF16, out=dflag) for s in range(NSEQ)]
    us = [dscr("us%d" % s, [NK[s], 512], BF16, out=dflag) for s in range(NSEQ)]
    ods = [dscr("ods%d" % s, [NOWN, 512], BF16, out=dflag) for s in range(NSEQ)]
    oss = [dscr("oss%d" % s, [NOWN, 512], BF16, out=dflag) for s in range(NSEQ)]

    es = ExitStack()
    P = Prog(nc, es)

    def sb(name, shape, dt, stack=None):
        return (stack or es).enter_context(nc.sbuf_tensor("sb_" + name, list(shape), dt))

    def ps(name, shape, dt, stack=None):
        return (stack or es).enter_context(nc.psum_tensor("ps_" + name, list(shape), dt))

    ident = sb("ident", [128, 128], F32)
    identb = sb("identb", [128, 128], BF16)
    antiidb = sb("antiidb", [128, 128], BF16)
    onesb = sb("onesb", [128, 128], BF16)
    flg = sb("flg", [128, 2], F32)
    lnv = sb("lnv", [128, 8, 8], F32)
    epsc = sb("epsc", [128, 1], F32)
    P.add("sp", lambda e: e.dma_start(out=ident[:, :], in_=ident_in[:, :]), writes=["ident"], dma=True)
    P.add("sp", lambda e: e.dma_start(out=flg[:, :], in_=flags[:, :]), writes=["flg"], dma=True)
    P.add("pool", lambda e: e.dma_start(out=identb[:, :], in_=ident_in[:, :]), writes=["identb"], dma=True)
    P.add("pool", lambda e: e.dma_start(out=antiidb[:, :], in_=antiid_in[:, :]), writes=["antiidb"], dma=True)
    P.add("dve", lambda e: e.memset(onesb[:, :], 1.0), writes=["onesb"])
    P.add("dve", lambda e: e.memset(epsc[:, :], LN_EPS), writes=["epsc"])
    LNI = {}
    lnraw = sb("lnraw", [64, 128], F32)
    for i, nm in enumerate(["ln_ffn1_g", "ln_ffn1_b", "ln_mix_g", "ln_mix_b", "ln_mem_g", "ln_mem_b", "ln_ffn2_g", "ln_ffn2_b"]):
        LNI[nm] = i
        src = V[nm].rearrange("(c p) -> c p", p=128)
        P.add("sp", lambda e, i=i, src=src: e.dma_start(out=lnraw[i * 8:(i + 1) * 8, :], in_=src), writes=["lnraw"], dma=True)

    def castdma(dst, src, key):
        P.add("pool", lambda e: e.dma_start(out=dst, in_=src), writes=[key], dma=True)

    for f, nm in enumerate(["ffn1_w13", "ffn2_w13"]):
        w = W[nm].rearrange("(kc p) n -> p kc n", p=128)
        for s_ in range(11):
            dst = wd13[f][s_].rearrange("p (jj ab kc c) -> p jj ab kc c", jj=2, ab=2, kc=8)
            for jj in range(2):
                for ab in range(2):
                    c0 = ab * DFF + (2 * s_ + jj) * 128
                    castdma(dst[:, jj, ab], w[:, :, c0:c0 + 128], ("wd13", f, s_))
    for f, nm in enumerate(["ffn1_w2", "ffn2_w2"]):
        w = W[nm].rearrange("(j p) n -> p j n", p=128)
        for dc in range(8):
            dst = wd2[f][dc].rearrange("p (j c) -> p j c", j=NJ)
            castdma(dst, w[:, :, dc * 128:(dc + 1) * 128], ("wd2", f, dc))
    for (nm, dd, ns) in [("w_in", wdin, 4), ("w_out", wdout, 2), ("mem_wq", wdq, 2), ("mem_wkv", wdkv, 4), ("mem_wo", wdo, 2)]:
        w = W[nm].rearrange("(kc p) n -> p kc n", p=128)
        for s_ in range(ns):
            dst = dd[s_].rearrange("p (kc c) -> p kc c", kc=8)
            castdma(dst, w[:, :, s_ * 512:(s_ + 1) * 512], (nm, s_))
    castdma(wdglu.rearrange("p (kc c) -> p kc c", kc=4), W["ssm_glu_w"].rearrange("(kc p) n -> p kc n", p=128), ("wdglu",))

    NWS = 4
    wslots = []
    wpos = [0]

    def wload(src, key, ncols=4096):
        i = wpos[0] % NWS
        wpos[0] += 1
        t = wslots[i]
        P.add("sp", lambda e: e.dma_start(out=t[:, 0:ncols], in_=src), reads=[key], writes=[("wslot", i)], dma=True)
        return t, ("wslot", i)

    pball = ps("pball", [128, 4096], F32)
    pb = [pball[:, i * 512:(i + 1) * 512] for i in range(8)]
    pbk = [("pb", i) for i in range(8)]
    rr = {"ab": 0, "o": 0}
    P.add("pe", lambda e: e.transpose(pb[6][:, 0:64], lnraw[:, :], ident[0:64, 0:64]), reads=["lnraw", "ident"], writes=[pbk[6]])
    P.add("dve", lambda e: e.tensor_copy(out=lnv[:, :, :].rearrange("p a b -> p (a b)"), in_=pb[6][:, 0:64]), reads=[pbk[6]], writes=["lnv"])

    def layernorm(r, rk, gname, bname, outf, outb, okf, okb, tmp, need_b=True, need_f=True):
        rb, rsq, mean, rstd, t1 = tmp["rb"], tmp["rsq"], tmp["mean"], tmp["rstd"], tmp["t1"]
        for dc in range(8):
            P.add("dve", lambda e, dc=dc: e.tensor_copy(out=rb[:, dc % 2, :], in_=r[:, dc, :]), reads=[(rk, dc)], writes=[("rb", dc % 2)])
            P.add("act", lambda e, dc=dc: e.activation(out=rsq[:, dc % 2, :], in_=r[:, dc, :], func=AF.Square), reads=[(rk, dc)], writes=[("rsq", dc % 2)])
            P.add("pe", lambda e, dc=dc: e.matmul(pb[6][:, :], onesb[:, :], rb[:, dc % 2, :], start=(dc == 0), stop=(dc == 7)),
                  reads=[("rb", dc % 2), "onesb"], writes=[pbk[6]])
            P.add("pe", lambda e, dc=dc: e.matmul(pb[7][:, :], onesb[:, :], rsq[:, dc % 2, :], start=(dc == 0), stop=(dc == 7)),
                  reads=[("rsq", dc % 2), "onesb"], writes=[pbk[7]])
        P.add("dve", lambda e: e.tensor_scalar(out=mean[:, :], in0=pb[6][:, :], scalar1=1.0 / D, scalar2=None, op0=ALU.mult),
              reads=[pbk[6]], writes=["mean"])
        P.add("dve", lambda e: e.tensor_tensor(out=t1[:, :], in0=mean[:, :], in1=mean[:, :], op=ALU.mult), reads=["mean"], writes=["t1"])
        P.add("dve", lambda e: e.scalar_tensor_tensor(out=t1[:, :], in0=pb[7][:, :], scalar=1.0 / D, in1=t1[:, :], op0=ALU.mult, op1=ALU.subtract),
              reads=[pbk[7], "t1"], writes=["t1"])
        P.add("act", lambda e: e.activation(out=t1[:, :], in_=t1[:, :], func=AF.Sqrt, bias=epsc[:, :], scale=1.0), reads=["t1", "epsc"], writes=["t1"])
        P.add("dve", lambda e: e.reciprocal(out=rstd[:, :], in_=t1[:, :]), reads=["t1"], writes=["rstd"])
        gi, bi = LNI[gname], LNI[bname]
        for dc in range(8):
            P.add("dve", lambda e, dc=dc: e.tensor_tensor(out=r[:, dc, :], in0=r[:, dc, :], in1=mean[:, :], op=ALU.subtract),
                  reads=[(rk, dc), "mean"], writes=[(rk, dc)])
            P.add("dve", lambda e, dc=dc: e.tensor_tensor(out=r[:, dc, :], in0=r[:, dc, :], in1=rstd[:, :], op=ALU.mult),
                  reads=[(rk, dc), "rstd"], writes=[(rk, dc)])
            if need_b:
                P.add("act", lambda e, dc=dc: e.activation(out=outb[:, dc, :], in_=r[:, dc, :], func=AF.Identity,
                                                            bias=lnv[:, bi, dc:dc + 1], scale=lnv[:, gi, dc:dc + 1]),
                      reads=[(rk, dc), "lnv"], writes=[(okb, dc)])
        for dc in range(8 if need_f else 0):
            P.add("act", lambda e, dc=dc: e.activation(out=outf[:, dc, :], in_=r[:, dc, :], func=AF.Identity,
                                                        bias=lnv[:, bi, dc:dc + 1], scale=lnv[:, gi, dc:dc + 1]),
                  reads=[(rk, dc), "lnv"], writes=[(okf, dc)])

    def ffn(f, xb, xbk, xf, xfk, r, rk, gT, sg, stage=9):
        for s_ in range(11):
            wt, wk = wload(wd13[f][s_], ("wd13", f, s_))
            wv = wt[:, :].rearrange("p (jj ab kc c) -> p jj ab kc c", jj=2, ab=2, kc=8)
            for jj in range(2):
                j = 2 * s_ + jj
                pa = rr["ab"] % 2
                rr["ab"] += 1
                ba, bb_ = pb[2 * pa], pb[2 * pa + 1]
                for ab, bank, bkey in ((0, ba, pbk[2 * pa]), (1, bb_, pbk[2 * pa + 1])):
                    for kc in range(8):
                        P.add("pe", lambda e, bank=bank, jj=jj, ab=ab, kc=kc, wv=wv: e.matmul(bank[:, :], wv[:, jj, ab, kc, :], xb[:, kc, :], start=(kc == 0), stop=(kc == 7)),
                              reads=[wk, (xbk, kc)], writes=[bkey])
                sgt = sg[pa]
                P.add("act", lambda e, ba=ba, sgt=sgt: e.activation(out=sgt[:, :], in_=ba[:, :], func=AF.Silu), reads=[pbk[2 * pa]], writes=[("sg", pa)])
                P.add("dve", lambda e, bb_=bb_, sgt=sgt, j=j: e.scalar_tensor_tensor(out=gT[:, j, :], in0=sgt[:, :], scalar=0.5, in1=bb_[:, :], op0=ALU.mult, op1=ALU.mult),
                      reads=[("sg", pa), pbk[2 * pa + 1]], writes=[("gT", j)])
        if stage < 3:
            return
        for dc in range(8):
            wt, wk = wload(wd2[f][dc], ("wd2", f, dc), ncols=NJ * 128)
            wv = wt[:, 0:NJ * 128].rearrange("p (j c) -> p j c", j=NJ)
            po = 4 + rr["o"] % 2
            rr["o"] += 1
            for j in range(NJ):
                P.add("pe", lambda e, po=po, j=j, wv=wv: e.matmul(pb[po][:, :], wv[:, j, :], gT[:, j, :], start=(j == 0), stop=(j == NJ - 1)),
                      reads=[wk, ("gT", j)], writes=[pbk[po]])
            P.add("dve", lambda e, po=po, dc=dc: e.scalar_tensor_tensor(out=r[:, dc, :], in0=xf[:, dc, :], scalar=ALPHA, in1=pb[po][:, :], op0=ALU.mult, op1=ALU.add),
                  reads=[pbk[po], (xfk, dc)], writes=[(rk, dc)])

    pa_es = ExitStack()
    wslots[:] = [sb("wslotA%d" % i, [128, 4096], BF16, pa_es) for i in range(NWS)]
    xt = [sb("xt%d" % i, [128, 4, D], F32, pa_es) for i in range(2)]
    xTf = sb("xTf", [128, 8, 512], F32, pa_es)
    xTb = sb("xTb", [128, 8, 512], BF16, pa_es)
    gT = sb("gT", [128, NJ, 512], BF16, pa_es)
    sg = [sb("sg%d" % i, [128, 512], F32, pa_es) for i in range(2)]
    rA = sb("rA", [128, 8, 512], F32, pa_es)
    x1f = sb("x1f", [128, 8, 512], F32, pa_es)
    x1b = sb("x1b", [128, 8, 512], BF16, pa_es)
    tmpA = {"rb": sb("rb", [128, 2, 512], BF16, pa_es), "rsq": sb("rsq", [128, 2, 512], BF16, pa_es),
            "mean": sb("mean", [128, 512], F32, pa_es), "rstd": sb("rstd", [128, 512], F32, pa_es), "t1": sb("t1", [128, 512], F32, pa_es)}
    stg = [sb("stg%d" % i, [128, 4, 512], BF16, pa_es) for i in range(2)]
    stgpos = [0]

    tiles = []
    for s in range(NSEQ):
        srcs = [("own", xp_own if s == 0 else xs[s - 1])]
        if s == 0 and NCTX:
            srcs.append(("ctx", xp_ctx))
        for kind, src in srcs:
            for i in range(NOWN // TT):
                tiles.append((s, kind, src, i))

    if dbg == 'P':
        tiles = []
    stage = 9
    if dbg and dbg.startswith('T'):
        stage = int(dbg[1:]); tiles = tiles[:1]
    if dbg == 'A1':
        tiles = tiles[:1]
    for ti, (s, kind, src, i) in enumerate(tiles):
        xtt = xt[ti % 2]
        xk = ("xt", ti % 2)
        srcv = src[i * TT:(i + 1) * TT, :].rearrange("(st p) d -> p st d", p=128)
        P.add("sp", lambda e, xtt=xtt, srcv=srcv: e.dma_start(out=xtt[:, :, :], in_=srcv), writes=[xk], dma=True)
        for kc in range(8):
            for st in range(4):
                P.add("pe", lambda e, kc=kc, st=st, xtt=xtt: e.transpose(pb[6][:, st * 128:(st + 1) * 128], xtt[:, st, kc * 128:(kc + 1) * 128], ident[:, :]),
                      reads=[xk, "ident"], writes=[pbk[6]])
            P.add("act", lambda e, kc=kc: e.copy(out=xTf[:, kc, :], in_=pb[6][:, :]), reads=[pbk[6]], writes=[("xTf", kc)])
            P.add("pool", lambda e, kc=kc: e.tensor_copy(out=xTb[:, kc, :], in_=xTf[:, kc, :]), reads=[("xTf", kc)], writes=[("xTb", kc)])
        if stage < 2:
            continue
        ffn(0, xTb, "xTb", xTf, "xTf", rA, "rA", gT, sg, stage)
        if stage < 4:
            continue
        layernorm(rA, "rA", "ln_ffn1_g", "ln_ffn1_b", x1f, x1b, "x1f", "x1b", tmpA, need_f=(kind == "own"))
        if stage < 5:
            continue
        tok0 = i * TT + (NOWN if kind == "ctx" else 0)
        if kind == "own":
            dst = x1s[s][:, i * TT:(i + 1) * TT].rearrange("(dc p) t -> p dc t", p=128)
            P.add("pool", lambda e, dst=dst: e.dma_start(out=dst, in_=x1f[:, :, :]), reads=[("x1f", dc) for dc in range(8)], writes=[("x1s", s, i)], dma=True)
        for cg in range(4):
            if cg == 0 and kind != "own":
                continue
            wt, wk = wload(wdin[cg], ("w_in", cg))
            wv = wt[:, :].rearrange("p (kc c) -> p kc c", kc=8)
            sgi = stgpos[0] % 2
            stgpos[0] += 1
            stt = stg[sgi]
            for c4 in range(4):
                po = 4 + rr["o"] % 2
                rr["o"] += 1
                for kc in range(8):
                    if cg < 2:
                        P.add("pe", lambda e, po=po, kc=kc, c4=c4, wv=wv: e.matmul(pb[po][:, :], wv[:, kc, c4 * 128:(c4 + 1) * 128], x1b[:, kc, :], start=(kc == 0), stop=(kc == 7)),
                              reads=[wk, ("x1b", kc)], writes=[pbk[po]])
                    else:
                        P.add("pe", lambda e, po=po, kc=kc, c4=c4, wv=wv: e.matmul(pb[po][:, :], x1b[:, kc, c4 * 128:(c4 + 1) * 128], wv[:, kc, :], start=(kc == 0), stop=(kc == 7)),
                              reads=[wk, ("x1b", kc)], writes=[pbk[po]])
                P.add("act", lambda e, po=po, c4=c4, stt=stt: e.copy(out=stt[:, c4, :], in_=pb[po][:, :]), reads=[pbk[po]], writes=[("stg", sgi)])
            if cg == 0:
                dst = qTs[s][:, i * TT:(i + 1) * TT].rearrange("(c p) t -> p c t", p=128)
                key = ("qTs", s, i)
            elif cg == 1:
                dst = kTs[s][:, tok0:tok0 + TT].rearrange("(c p) t -> p c t", p=128)
                key = ("kTs", s, tok0)
            elif cg == 2:
                dst = vs[s][tok0:tok0 + TT, :].rearrange("(st p) c -> p st c", p=128)
                key = ("vs", s, tok0)
            else:
                dst = us[s][tok0:tok0 + TT, :].rearrange("(st p) c -> p st c", p=128)
                key = ("us", s, tok0)
            P.add("pool", lambda e, dst=dst, stt=stt: e.dma_start(out=dst, in_=stt[:, :, :]), reads=[("stg", sgi)], writes=[key], dma=True)
    P.barrier()
    pa_es.close()


    if dbg in (None, "B", "C", "S", "Ss", "Sm"):
        pb_es = ExitStack()
        NOB = NOWN // 128
        NQG = NOWN // 512
        NKBmax = max(NK) // 128
        Gd = dscr("Gd", [4, 1281], F32)
        Gd8 = dscr("Gd8", [4, 1281], F32)
        dl = sb("dl", [128, 256], F32, pb_es)
        dlp = sb("dlp", [128, 128], F32, pb_es)
        lam2 = sb("lam2", [128, 4], F32, pb_es)
        neglam = sb("neglam", [128, 1], F32, pb_es)
        gsub = sb("gsub", [128, 128], F32, pb_es)
        rbt = sb("rbt", [32, 4], F32, pb_es)
        oht = sb("oht", [32, 1281], F32, pb_es)
        Gsb = sb("Gsb", [4, 1281], F32, pb_es)
        Gsb8 = sb("Gsb8", [4, 1281], F32, pb_es)
        cpm = sb("cpm", [128, 4, 2], F32, pb_es)
        coth = sb("coth", [128, 4], F32, pb_es)
        tcol = sb("tcol", [128, 4, 2], F32, pb_es)
        Wt = [[sb("Wt%d_%d" % (h, m), [128, 512], BF16, pb_es) for m in range(6)] for h in range(4)]
        SW = [[sb("SW%d_%d" % (h, a), [128, 512], BF16, pb_es) for a in range(2)] for h in range(4)]
        P.add("sp", lambda e: e.dma_start(out=dl[:, :], in_=diff_lambda.partition_broadcast(128)), writes=["dl"], dma=True)
        P.add("sp", lambda e: e.dma_start(out=gsub[:, :], in_=subln_g.partition_broadcast(128)), writes=["gsub"], dma=True)
        P.add("sp", lambda e: e.dma_start(out=rbt[:, :], in_=rel_bias[:, :]), writes=["rbt"], dma=True)
        P.add("sp", lambda e: e.dma_start(out=oht[:, :], in_=oh_in[:, :]), writes=["oht"], dma=True)
        dlv = dl[:, :].rearrange("p (a b) -> p a b", a=4)
        dpv = dlp[:, :].rearrange("p (a b) -> p a b", a=2)
        P.add("dve", lambda e: e.tensor_tensor(out=dpv[:, 0, :], in0=dlv[:, 0, :], in1=dlv[:, 1, :], op=ALU.mult), reads=["dl"], writes=["dlp"])
        P.add("dve", lambda e: e.tensor_tensor(out=dpv[:, 1, :], in0=dlv[:, 2, :], in1=dlv[:, 3, :], op=ALU.mult), reads=["dl"], writes=["dlp"])
        P.add("dve", lambda e: e.tensor_reduce(out=lam2[:, 0:2], in_=dpv, axis=AX.X, op=ALU.add), reads=["dlp"], writes=["lam2"])
        P.add("act", lambda e: e.activation(out=lam2[:, 2:4], in_=lam2[:, 0:2], func=AF.Exp), reads=["lam2"], writes=["lam2e"])
        P.add("dve", lambda e: e.tensor_tensor(out=neglam[:, :], in0=lam2[:, 3:4], in1=lam2[:, 2:3], op=ALU.subtract), reads=["lam2e"], writes=["neglam"])
        P.add("dve", lambda e: e.tensor_scalar(out=neglam[:, :], in0=neglam[:, :], scalar1=-LAM_INIT, scalar2=None, op0=ALU.add), reads=["neglam"], writes=["neglam"])
        P.add("dve", lambda e: e.tensor_scalar(out=gsub[:, :], in0=gsub[:, :], scalar1=1.0 - LAM_INIT, scalar2=None, op0=ALU.mult), reads=["gsub"], writes=["gsub"])
        for c0 in range(0, 1281, 512):
            n = min(512, 1281 - c0)
            P.add("pe", lambda e, c0=c0, n=n: e.matmul(pb[0][0:4, 0:n], rbt[:, :], oht[:, c0:c0 + n], start=True, stop=True), reads=["rbt", "oht"], writes=[pbk[0]])
            P.add("act", lambda e, c0=c0, n=n: e.copy(out=Gsb[:, c0:c0 + n], in_=pb[0][0:4, 0:n]), reads=[pbk[0]], writes=["Gsb"])
        P.add("dve", lambda e: e.tensor_scalar(out=Gsb8[:, :], in0=Gsb[:, :], scalar1=8.0, scalar2=None, op0=ALU.mult), reads=["Gsb"], writes=["Gsb8"])
        P.add("sp", lambda e: e.dma_start(out=Gd[:, :], in_=Gsb[:, :]), reads=["Gsb"], writes=["Gd"], dma=True)
        P.add("sp", lambda e: e.dma_start(out=Gd8[:, :], in_=Gsb8[:, :]), reads=["Gsb8"], writes=["Gd8"], dma=True)
        for h in range(4):
            P.add("sp", lambda e, h=h: e.dma_start(out=cpm[:, h, 0:1], in_=Gd[h:h + 1, 1:2].partition_broadcast(128)), reads=["Gd"], writes=["cpm"], dma=True)
            P.add("sp", lambda e, h=h: e.dma_start(out=cpm[:, h, 1:2], in_=Gd[h:h + 1, 1279:1280].partition_broadcast(128)), reads=["Gd"], writes=["cpm"], dma=True)
            for m in range(6):
                src = bass.AP(tensor=Gd8.tensor, offset=h * 1281 + 641 - 128 * m, ap=[[1, 128], [1, 512]])
                P.add("pool", lambda e, h=h, m=m, src=src: e.dma_start(out=Wt[h][m][:, :], in_=src), reads=["Gd8"], writes=[("Wt", h, m)], dma=True)
        for h in range(4):
            P.add("dve", lambda e, h=h: e.tensor_scalar(out=coth[:, h:h + 1], in0=cpm[:, h, 0:1], scalar1=flg[:, 1:2], scalar2=None, op0=ALU.mult), reads=["cpm", "flg"], writes=["coth"])
            P.add("dve", lambda e, h=h: e.scalar_tensor_tensor(out=coth[:, h:h + 1], in0=cpm[:, h, 1:2], scalar=flg[:, 0:1], in1=coth[:, h:h + 1], op0=ALU.mult, op1=ALU.add), reads=["cpm", "flg", "coth"], writes=["coth"])
            P.add("dve", lambda e, h=h: e.tensor_scalar(out=tcol[:, h, 0:1], in0=cpm[:, h, 1:2], scalar1=flg[:, 0:1], scalar2=8.0, op0=ALU.mult, op1=ALU.mult), reads=["cpm", "flg"], writes=["tcol"])
            P.add("dve", lambda e, h=h: e.tensor_scalar(out=tcol[:, h, 1:2], in0=cpm[:, h, 0:1], scalar1=flg[:, 1:2], scalar2=8.0, op0=ALU.mult, op1=ALU.mult), reads=["cpm", "flg"], writes=["tcol"])
            P.add("dve", lambda e, h=h: e.tensor_scalar(out=SW[h][0][:, :], in0=Wt[h][5][:, :], scalar1=flg[:, 1:2], scalar2=tcol[:, h, 0:1], op0=ALU.mult, op1=ALU.add), reads=[("Wt", h, 5), "flg", "tcol"], writes=[("SW", h, 0)])
            P.add("dve", lambda e, h=h: e.tensor_scalar(out=SW[h][1][:, :], in0=Wt[h][0][:, :], scalar1=flg[:, 0:1], scalar2=tcol[:, h, 1:2], op0=ALU.mult, op1=ALU.add), reads=[("Wt", h, 0), "flg", "tcol"], writes=[("SW", h, 1)])

        KT = [sb("KT%d" % i, [128, max(NK)], BF16, pb_es) for i in range(2)]
        VH = [sb("VH%d" % i, [128, NKBmax, 129], BF16, pb_es) for i in range(2)]
        QT = [sb("QT%d" % i, [128, NOWN], BF16, pb_es) for i in range(2)]
        ET = [sb("ET%d" % i, [128, 1024], BF16, pb_es) for i in range(3)]
        o0 = sb("o0", [128, 4, 128], F32, pb_es)
        ob = [sb("ob%d" % i, [128, 128], F32, pb_es) for i in range(2)]
        junk = sb("junk", [128, 128], F32, pb_es)
        sm = sb("sm", [128, 16], F32, pb_es)
        ostg = [sb("ostg%d" % i, [128, 4, 128], BF16, pb_es) for i in range(2)]
        for i in range(2):
            P.add("dve", lambda e, i=i: e.memset(VH[i][:, :, 128:129], 1.0), writes=[("VH", i)])
        cnt = {"hs": 0, "sc": 0, "et": 0, "og": 0}
        seqsB = range(NSEQ) if dbg != "B1" else range(1)
        for s in seqsB:
            NKB = NK[s] // 128
            for h in range(4):
                bi = cnt["hs"] % 2
                cnt["hs"] += 1
                kt, vh, qt = KT[bi], VH[bi], QT[bi]
                P.add("sp", lambda e, kt=kt, s=s, h=h: e.dma_start(out=kt[:, 0:NK[s]], in_=kTs[s][h * 128:(h + 1) * 128, :]), writes=[("KT", bi)], dma=True)
                P.add("sp", lambda e, qt=qt, s=s, h=h: e.dma_start(out=qt[:, :], in_=qTs[s][h * 128:(h + 1) * 128, :]), writes=[("QT", bi)], dma=True)
                vsrc = vs[s][:, h * 128:(h + 1) * 128].rearrange("(kb p) c -> p kb c", p=128)
                P.add("sp", lambda e, vh=vh, vsrc=vsrc, NKB=NKB: e.dma_start(out=vh[:, 0:NKB, 0:128], in_=vsrc), reads=[("VH", bi)], writes=[("VHd", bi)], dma=True)
                for I in range(NQG):
                    og = cnt["og"] % 2
                    cnt["og"] += 1
                    P.add("dve", lambda e: e.memset(pball[:, 2048:4096], 0.0), writes=[pbk[4], pbk[5], pbk[6], pbk[7]])

                    def QK(j, I=I, kt=kt, qt=qt, h=h, bi=bi, NKB=NKB):
                        R = cnt["sc"] % 2
                        cnt["sc"] += 1
                        eb = cnt["et"] % 3
                        cnt["et"] += 1
                        wide = None
                        bias = None
                        if j < NOB:
                            mw = j - 4 * I + 1
                            if 0 <= mw <= 5:
                                wide = (Wt[h][mw], ("Wt", h, mw))
                            elif mw < 0:
                                bias = cpm[:, h, 1:2]
                            else:
                                bias = cpm[:, h, 0:1]
                        else:
                            jo = j - NOB
                            if I == NQG - 1 and jo == 0:
                                wide = (SW[h][0], ("SW", h, 0))
                            elif I == 0 and jo == NKB - NOB - 1:
                                wide = (SW[h][1], ("SW", h, 1))
                            else:
                                bias = coth[:, h:h + 1]
                        for m in range(2):
                            P.add("pe", lambda e, m=m: e.matmul(pb[2 * R + m], kt[m * 64:(m + 1) * 64, j * 128:(j + 1) * 128], qt[m * 64:(m + 1) * 64, I * 512:(I + 1) * 512], start=True, stop=(wide is None)),
                                  reads=[("KT", bi), ("QT", bi)], writes=[pbk[2 * R + m]])
                        if wide is not None:
                            for m in range(2):
                                P.add("pe", lambda e, m=m: e.matmul(pb[2 * R + m], antiidb[:, :], wide[0][:, :], start=False, stop=True), reads=["antiidb", wide[1]], writes=[pbk[2 * R + m]])
                            P.add("act", lambda e: e.activation(out=ET[eb][:, :], in_=pball[:, 2 * R * 512:2 * R * 512 + 1024], func=AF.Exp, scale=0.125), reads=[pbk[2 * R], pbk[2 * R + 1]], writes=[("ET", eb)])
                        else:
                            P.add("act", lambda e: e.activation(out=ET[eb][:, :], in_=pball[:, 2 * R * 512:2 * R * 512 + 1024], func=AF.Exp, bias=bias, scale=0.125), reads=[pbk[2 * R], pbk[2 * R + 1], "cpm", "coth"], writes=[("ET", eb)])
                        return eb

                    def PV(j, eb, vh=vh, bi=bi):
                        for m in range(2):
                            for qb in range(4):
                                a = m * 4 + qb
                                bank = 4 + a // 2
                                off = (a % 2) * 256
                                P.add("pe", lambda e, m=m, qb=qb, bank=bank, off=off: e.matmul(pb[bank][:, off:off + 129], ET[eb][:, m * 512 + qb * 128:m * 512 + (qb + 1) * 128], vh[:, j, :], start=False, stop=False, skip_group_check=True),
                                      reads=[("ET", eb), ("VHd", bi)], writes=[pbk[bank]])
                    prev = None
                    for j in range(NKB):
                        eb = QK(j)
                        if prev is not None:
                            PV(*prev)
                        prev = (j, eb)
                    PV(*prev)
                    for qb in range(4):
                        b0, b1 = 4 + qb // 2, 6 + qb // 2
                        off = (qb % 2) * 256
                        acc0 = pb[b0][:, off:off + 129]
                        acc1 = pb[b1][:, off:off + 129]
                        obq = ob[qb % 2]
                        okey = ("ob", qb % 2)
                        P.add("dve", lambda e, qb=qb, acc0=acc0: e.reciprocal(out=sm[:, qb:qb + 1], in_=acc0[:, 128:129]), reads=[pbk[b0]], writes=[("sm", qb)])
                        P.add("dve", lambda e, qb=qb, acc1=acc1: e.reciprocal(out=sm[:, 4 + qb:5 + qb], in_=acc1[:, 128:129]), reads=[pbk[b1]], writes=[("sm1", qb)])
                        P.add("dve", lambda e, qb=qb: e.tensor_tensor(out=sm[:, 4 + qb:5 + qb], in0=sm[:, 4 + qb:5 + qb], in1=neglam[:, :], op=ALU.mult), reads=[("sm1", qb), "neglam"], writes=[("sm1", qb)])
                        P.add("act", lambda e, qb=qb, acc0=acc0: e.activation(out=o0[:, qb, :], in_=acc0[:, 0:128], func=AF.Copy, scale=sm[:, qb:qb + 1]), reads=[pbk[b0], ("sm", qb)], writes=[("o0", qb)])
                        P.add("dve", lambda e, qb=qb, acc1=acc1, obq=obq: e.scalar_tensor_tensor(out=obq[:, :], in0=acc1[:, 0:128], scalar=sm[:, 4 + qb:5 + qb], in1=o0[:, qb, :], op0=ALU.mult, op1=ALU.add),
                              reads=[pbk[b1], ("sm1", qb), ("o0", qb)], writes=[okey])
                        P.add("act", lambda e, qb=qb, obq=obq: e.activation(out=junk[:, :], in_=obq[:, :], func=AF.Square, accum_out=sm[:, 8 + qb:9 + qb]), reads=[okey], writes=["junk", ("sm2", qb)])
                        P.add("act", lambda e, qb=qb: e.activation(out=sm[:, 8 + qb:9 + qb], in_=sm[:, 8 + qb:9 + qb], func=AF.Sqrt, bias=epsc[:, :], scale=1.0 / 128), reads=[("sm2", qb), "epsc"], writes=[("sm2", qb)])
                        P.add("dve", lambda e, qb=qb: e.reciprocal(out=sm[:, 8 + qb:9 + qb], in_=sm[:, 8 + qb:9 + qb]), reads=[("sm2", qb)], writes=[("sm2", qb)])
                        P.add("dve", lambda e, qb=qb, og=og, obq=obq: e.scalar_tensor_tensor(out=ostg[og][:, qb, :], in0=obq[:, :], scalar=sm[:, 8 + qb:9 + qb], in1=gsub[:, :], op0=ALU.mult, op1=ALU.mult),
                              reads=[okey, ("sm2", qb), "gsub"], writes=[("ostg", og)])
                    dst = ods[s][I * 512:(I + 1) * 512, h * 128:(h + 1) * 128].rearrange("(qb p) e -> p qb e", p=128)
                    P.add("pool", lambda e, dst=dst, og=og: e.dma_start(out=dst, in_=ostg[og][:, :, :]), reads=[("ostg", og)], writes=[("ods", s, I, h)], dma=True)
        P.barrier()
        pb_es.close()


    if dbg in (None, "C", "S", "Ss", "Sm"):
        TWO_PI = 2.0 * math.pi
        NB1 = NOWN // T1
        BS = min(128, NB1)
        NBLK = NB1 // BS
        NCH = NB1 // 16
        GB = 8
        WAd = dscr("WAd", [4, 128, 4096], BF16)
        WCd = dscr("WCd", [4, 64, 8192], BF16)
        KTd = dscr("KTd", [4, 128, 4096], BF16)
        yscr = [dscr("yscr%d" % s_, [NOWN, 512], F32) for s_ in range(NSEQ)]
        PW = sb("PW", [64, 2, 2, NG, 16], F32)

        def V_(eng, fn, reads, writes):
            P.add(eng, fn, reads=reads, writes=writes)

        def tt(eng, out, in0, in1, op, r, w):
            P.add(eng, lambda e: e.tensor_tensor(out=out, in0=in0, in1=in1, op=op), reads=r, writes=w)

        def cmul(eng, ore, oim, are, aim, bre, bim, t1, t2, r, w, tk, neg_im=False):
            tt(eng, t1, are, bre, ALU.mult, r, [tk + "1"])
            tt(eng, t2, aim, bim, ALU.mult, r, [tk + "2"])
            tt(eng, ore, t1, t2, ALU.subtract, [tk + "1", tk + "2"], [w + "re"])
            tt(eng, t1, are, bim, ALU.mult, r, [tk + "1"])
            tt(eng, t2, aim, bre, ALU.mult, r, [tk + "2"])
            if neg_im:
                P.add(eng, lambda e: e.scalar_tensor_tensor(out=oim, in0=t1, scalar=-1.0, in1=t2, op0=ALU.mult, op1=ALU.subtract), reads=[tk + "1", tk + "2"], writes=[w + "im"])
            else:
                tt(eng, oim, t1, t2, ALU.add, [tk + "1", tk + "2"], [w + "im"])

        st_es = ExitStack()
        kv = sb("kv", [64, 2, 17], F32, st_es)
        msk = sb("msk", [128, 2, 2, 256], F32, st_es)
        lraw = sb("lraw", [32, 2, 64], F32, st_es)
        lT = sb("lT", [64, 2, 32], F32, st_es)
        stp = sb("stp", [64, 32], F32, st_es)
        rho = sb("rho", [64, 32], F32, st_es)
        th = sb("th", [64, 32], F32, st_es)
        Etab = sb("Etab", [64, 2, 2, 2, 32, 17], F32, st_es)
        ang = sb("ang", [64, 32, 17], F32, st_es)
        mag = sb("mag", [64, 32, 17], F32, st_es)
        ai_ = sb("angi", [64, 32, 17], I32, st_es)
        af_ = sb("angf", [64, 32, 17], F32, st_es)
        m1 = sb("m1", [64, 32, 17], F32, st_es)
        sn = sb("sn", [64, 32, 17], F32, st_es)
        cs = sb("cs", [64, 32, 17], F32, st_es)
        inv16 = sb("inv16", [64, 2, 2, 32], F32, st_es)
        cf = sb("cf", [64, 2, 2, 32], F32, st_es)
        q1 = sb("q1", [64, 32], F32, st_es)
        q2 = sb("q2", [64, 32], F32, st_es)
        q3 = sb("q3", [64, 32], F32, st_es)
        q4 = sb("q4", [64, 32], F32, st_es)
        bp = sb("bp", [64, 2, 2, 32, 16], F32, st_es)
        bbp = sb("bbp", [64, 2, 2, 32, 16], F32, st_es)
        craw = sb("craw", [128, 4, 64], F32, st_es)
        cT = sb("cT", [64, 2, 2, 512], F32, st_es)
        tb1 = sb("tb1", [64, 32, 16], F32, st_es)
        tb2 = sb("tb2", [64, 32, 16], F32, st_es)
        WAp = sb("WAp", [64, 2, GB, 16, 16], F32, st_es)
        WA2 = sb("WA2", [64, 2, GB, 16, 16], F32, st_es)
        WCp = sb("WCp", [64, 2, GB, 16, 16], F32, st_es)
        u1 = sb("u1", [64, GB, 16, 16], F32, st_es)
        u2 = sb("u2", [64, GB, 16, 16], F32, st_es)
        WAst = sb("WAst", [128, 2, 2, GB, 2, 64], BF16, st_es)
        WCst = sb("WCst", [64, 2, GB, 2, 256], BF16, st_es)
        KTs = sb("KTs", [128, GB, 2, 256], F32, st_es)
        KTt = sb("KTt", [128, 256], F32, st_es)
        KTb = sb("KTb", [128, GB, 2, 256], BF16, st_es)
        P.add("sp", lambda e: e.dma_start(out=kv[:, :, :], in_=kv_in[:, :, :]), writes=["kv"], dma=True)
        P.add("sp", lambda e: e.dma_start(out=msk[:, :, :, :], in_=msk_in[:, :, :, :]), writes=["msk"], dma=True)
        for d_ in range(2):
            dk = "d%d" % d_
            P.add("sp", lambda e, d_=d_: e.dma_start(out=lraw[:, 0, :], in_=ssm["ssm_lam_re"][d_]), writes=["lraw"], dma=True)
            P.add("sp", lambda e, d_=d_: e.dma_start(out=lraw[:, 1, :], in_=ssm["ssm_lam_im"][d_]), writes=["lraw"], dma=True)
            for ri in range(2):
                P.add("pe", lambda e, ri=ri: e.transpose(pb[6][0:64, ri * 32:(ri + 1) * 32], lraw[:, ri, :], ident[0:32, 0:32]), reads=["lraw", "ident"], writes=[pbk[6]])
            P.add("act", lambda e: e.copy(out=lT[:, :, :].rearrange("p a g -> p (a g)"), in_=pb[6][0:64, 0:64]), reads=[pbk[6]], writes=["lT"])
            P.add("sp", lambda e, d_=d_: e.dma_start(out=stp[:, :], in_=ssm["ssm_log_step"][d_].partition_broadcast(64)), writes=["stp"], dma=True)
            P.add("act", lambda e: e.activation(out=stp[:, :], in_=stp[:, :], func=AF.Exp), reads=["stp"], writes=["stp"])
            tt("dve", rho[:, :], lT[:, 0, :], stp[:, :], ALU.mult, ["lT", "stp"], ["rho"])
            tt("dve", th[:, :], lT[:, 1, :], stp[:, :], ALU.mult, ["lT", "stp"], ["th"])
            for od in range(2):
                kb_ = kv[:, od, :].unsqueeze(1).broadcast_to([64, 32, 17])
                tt("dve", ang[:, :, :], th[:, :].unsqueeze(2).broadcast_to([64, 32, 17]), kb_, ALU.mult, ["th", "kv"], ["ang"])
                tt("dve", mag[:, :, :], rho[:, :].unsqueeze(2).broadcast_to([64, 32, 17]), kb_, ALU.mult, ["rho", "kv"], ["mag"])
                P.add("act", lambda e: e.activation(out=mag[:, :, :], in_=mag[:, :, :], func=AF.Exp), reads=["mag"], writes=["mag"])
                P.add("dve", lambda e: e.tensor_scalar(out=af_[:, :, :], in0=ang[:, :, :], scalar1=1.0 / TWO_PI, scalar2=None, op0=ALU.mult), reads=["ang"], writes=["af"])
                P.add("dve", lambda e: e.tensor_copy(out=ai_[:, :, :], in_=af_[:, :, :]), reads=["af"], writes=["ai"])
                P.add("dve", lambda e: e.tensor_copy(out=af_[:, :, :], in_=ai_[:, :, :]), reads=["ai"], writes=["af"])
                P.add("dve", lambda e: e.scalar_tensor_tensor(out=ang[:, :, :], in0=af_[:, :, :], scalar=-TWO_PI, in1=ang[:, :, :], op0=ALU.mult, op1=ALU.add), reads=["af", "ang"], writes=["ang"])

                def fold(x, xk):
                    P.add("dve", lambda e: e.tensor_scalar(out=m1[:, :, :], in0=x, scalar1=math.pi, scalar2=None, op0=ALU.is_gt), reads=[xk], writes=["m1"])
                    P.add("dve", lambda e: e.scalar_tensor_tensor(out=x, in0=m1[:, :, :], scalar=-TWO_PI, in1=x, op0=ALU.mult, op1=ALU.add), reads=["m1", xk], writes=[xk])
                    P.add("dve", lambda e: e.tensor_scalar(out=m1[:, :, :], in0=x, scalar1=-math.pi, scalar2=None, op0=ALU.is_lt), reads=[xk], writes=["m1"])
                    P.add("dve", lambda e: e.scalar_tensor_tensor(out=x, in0=m1[:, :, :], scalar=TWO_PI, in1=x, op0=ALU.mult, op1=ALU.add), reads=["m1", xk], writes=[xk])
                fold(ang[:, :, :], "ang")
                P.add("act", lambda e: e.activation(out=sn[:, :, :], in_=ang[:, :, :], func=AF.Sin), reads=["ang"], writes=["sn"])
                P.add("dve", lambda e: e.tensor_scalar(out=ang[:, :, :], in0=ang[:, :, :], scalar1=math.pi / 2, scalar2=None, op0=ALU.add), reads=["ang"], writes=["ang"])
                fold(ang[:, :, :], "ang")
                P.add("act", lambda e: e.activation(out=cs[:, :, :], in_=ang[:, :, :], func=AF.Sin), reads=["ang"], writes=["cs"])
                tt("dve", Etab[:, d_, od, 0, :, :], mag[:, :, :], cs[:, :, :], ALU.mult, ["mag", "cs"], ["E" + dk])
                tt("dve", Etab[:, d_, od, 1, :, :], mag[:, :, :], sn[:, :, :], ALU.mult, ["mag", "sn"], ["E" + dk])
                if od == 0:
                    P.add("act", lambda e: e.activation(out=q1[:, :], in_=rho[:, :], func=AF.Exp, scale=-16.0), reads=["rho"], writes=["q1"])
                    tt("dve", inv16[:, d_, 0, :], q1[:, :], cs[:, :, 16], ALU.mult, ["q1", "cs"], ["inv16"])
                    P.add("dve", lambda e, d_=d_: e.scalar_tensor_tensor(out=inv16[:, d_, 1, :], in0=q1[:, :], scalar=-1.0, in1=sn[:, :, 16], op0=ALU.mult, op1=ALU.mult), reads=["q1", "sn"], writes=["inv16"])
            are, aim = Etab[:, d_, 0, 0, :, 1], Etab[:, d_, 0, 1, :, 1]
            P.add("dve", lambda e, are=are: e.tensor_scalar(out=q1[:, :], in0=are, scalar1=-1.0, scalar2=None, op0=ALU.add), reads=["E" + dk], writes=["q1"])
            tt("dve", q2[:, :], lT[:, 0, :], lT[:, 0, :], ALU.mult, ["lT"], ["q2"])
            tt("dve", q3[:, :], lT[:, 1, :], lT[:, 1, :], ALU.mult, ["lT"], ["q3"])
            tt("dve", q2[:, :], q2[:, :], q3[:, :], ALU.add, ["q2", "q3"], ["q2"])
            P.add("dve", lambda e: e.reciprocal(out=q2[:, :], in_=q2[:, :]), reads=["q2"], writes=["q2"])
            tt("dve", q3[:, :], q1[:, :], lT[:, 0, :], ALU.mult, ["q1", "lT"], ["q3"])
            tt("dve", q4[:, :], aim, lT[:, 1, :], ALU.mult, ["E" + dk, "lT"], ["q4"])
            tt("dve", q3[:, :], q3[:, :], q4[:, :], ALU.add, ["q3", "q4"], ["q3"])
            tt("dve", cf[:, d_, 0, :], q3[:, :], q2[:, :], ALU.mult, ["q3", "q2"], ["cf"])
            tt("dve", q3[:, :], aim, lT[:, 0, :], ALU.mult, ["E" + dk, "lT"], ["q3"])
            tt("dve", q4[:, :], q1[:, :], lT[:, 1, :], ALU.mult, ["q1", "lT"], ["q4"])
            tt("dve", q3[:, :], q3[:, :], q4[:, :], ALU.subtract, ["q3", "q4"], ["q3"])
            tt("dve", cf[:, d_, 1, :], q3[:, :], q2[:, :], ALU.mult, ["q3", "q2"], ["cf"])
            for ri, nm in ((0, "ssm_b_re"), (1, "ssm_b_im")):
                for g4 in range(4):
                    src = ssm[nm][d_][g4 * 8:(g4 + 1) * 8].rearrange("g p h -> p g h")
                    P.add("sp", lambda e, src=src, ri=ri, g4=g4, d_=d_: e.dma_start(out=bp[:, d_, ri, g4 * 8:(g4 + 1) * 8, :], in_=src), writes=["bp"], dma=True)
            crb = cf[:, d_, 0, :].unsqueeze(2).broadcast_to([64, 32, 16])
            cib = cf[:, d_, 1, :].unsqueeze(2).broadcast_to([64, 32, 16])
            cmul("dve", bbp[:, d_, 0, :, :], bbp[:, d_, 1, :, :], crb, cib, bp[:, d_, 0, :, :], bp[:, d_, 1, :, :], tb1[:, :, :], tb2[:, :, :], ["cf", "bp"], "bbp", "tb")
            for ri, nm in ((0, "ssm_c_re"), (1, "ssm_c_im")):
                src = ssm[nm][d_].rearrange("g h p -> (g h) p").rearrange("(c q) p -> q c p", q=128)
                P.add("sp", lambda e, src=src: e.dma_start(out=craw[:, :, :], in_=src), writes=["craw"], dma=True)
                for c4 in range(4):
                    P.add("pe", lambda e, c4=c4: e.transpose(pb[6][0:64, c4 * 128:(c4 + 1) * 128], craw[:, c4, :], ident[:, :]), reads=["craw", "ident"], writes=[pbk[6]])
                P.add("act", lambda e, ri=ri, d_=d_: e.copy(out=cT[:, d_, ri, :], in_=pb[6][0:64, :]), reads=[pbk[6]], writes=["cT"])
            a16r, a16i = Etab[:, d_, 0, 0, :, 16], Etab[:, d_, 0, 1, :, 16]
            order = list(range(16)) if d_ == 0 else list(range(15, -1, -1))
            for n_, i_ in enumerate(order):
                if n_ == 0:
                    P.add("dve", lambda e, i_=i_, d_=d_, a16r=a16r: e.tensor_copy(out=PW[:, 0, d_, :, i_], in_=a16r), reads=["E" + dk], writes=["PW"])
                    P.add("dve", lambda e, i_=i_, d_=d_, a16i=a16i: e.tensor_copy(out=PW[:, 1, d_, :, i_], in_=a16i), reads=["E" + dk], writes=["PW"])
                else:
                    ip = order[n_ - 1]
                    cmul("dve", PW[:, 0, d_, :, i_], PW[:, 1, d_, :, i_], PW[:, 0, d_, :, ip], PW[:, 1, d_, :, ip], a16r, a16i, q1[:, :], q2[:, :], ["PW", "E" + dk], "PW", "q")
        for b in range(4):
            g0 = b * GB
            for d_ in range(2):
                dk = "d%d" % d_
                if d_ == 0:
                    pa_re, pa_im = Etab[:, 0, 1, 0, g0:g0 + GB, 1:17], Etab[:, 0, 1, 1, g0:g0 + GB, 1:17]
                    pc_re, pc_im = Etab[:, 0, 0, 0, g0:g0 + GB, 1:17], Etab[:, 0, 0, 1, g0:g0 + GB, 1:17]
                else:
                    pa_re, pa_im = Etab[:, 1, 0, 0, g0:g0 + GB, 0:16], Etab[:, 1, 0, 1, g0:g0 + GB, 0:16]
                    pc_re, pc_im = Etab[:, 1, 1, 0, g0:g0 + GB, 0:16], Etab[:, 1, 1, 1, g0:g0 + GB, 0:16]
                shp = [64, GB, 16, 16]
                bre = bbp[:, d_, 0, g0:g0 + GB, :].unsqueeze(2).broadcast_to(shp)
                bim = bbp[:, d_, 1, g0:g0 + GB, :].unsqueeze(2).broadcast_to(shp)
                cmul("dve", WAp[:, 0], WAp[:, 1], bre, bim, pa_re.unsqueeze(3).broadcast_to(shp), pa_im.unsqueeze(3).broadcast_to(shp), u1[:, :, :, :], u2[:, :, :, :], ["bbp", "E" + dk], "WAp", "u")
                ivr = inv16[:, d_, 0, g0:g0 + GB].unsqueeze(2).unsqueeze(3).broadcast_to(shp)
                ivi = inv16[:, d_, 1, g0:g0 + GB].unsqueeze(2).unsqueeze(3).broadcast_to(shp)
                cmul("dve", WA2[:, 0], WA2[:, 1], WAp[:, 0], WAp[:, 1], ivr, ivi, u1[:, :, :, :], u2[:, :, :, :], ["WApre", "WApim", "inv16"], "WA2", "u")
                cre = cT[:, d_, 0, g0 * 16:(g0 + GB) * 16].rearrange("p (g h) -> p g h", h=16).unsqueeze(2).broadcast_to(shp)
                cim = cT[:, d_, 1, g0 * 16:(g0 + GB) * 16].rearrange("p (g h) -> p g h", h=16).unsqueeze(2).broadcast_to(shp)
                cmul("dve", WCp[:, 0], WCp[:, 1], cre, cim, pc_re.unsqueeze(3).broadcast_to(shp), pc_im.unsqueeze(3).broadcast_to(shp), u1[:, :, :, :], u2[:, :, :, :], ["cT", "E" + dk], "WCp", "u", neg_im=True)
                for ri in range(2):
                    P.add("pool", lambda e, ri=ri, d_=d_: e.tensor_copy(out=WCst[:, d_, :, ri, :], in_=WCp[:, ri].rearrange("p g t h -> p g (t h)")), reads=["WCpre", "WCpim"], writes=["WCst"])
                for hf in range(2):
                    for gq in range(GB // 4):
                        for gi in range(4):
                            gl = gq * 4 + gi
                            for ri in range(2):
                                col = (gi * 2 + ri) * 64
                                P.add("pe", lambda e, gl=gl, ri=ri, hf=hf, col=col: e.transpose(pb[6][:, col:col + 64], WAp[:, ri, gl, hf * 8:(hf + 1) * 8, :].rearrange("p s h -> p (s h)"), ident[0:64, 0:64]),
                                      reads=["WApre", "WApim", "ident"], writes=[pbk[6]])
                        P.add("act", lambda e, hf=hf, gq=gq, d_=d_: e.copy(out=WAst[:, d_, hf, gq * 4:(gq + 1) * 4, :, :].rearrange("p g r q -> p (g r q)"), in_=pb[6][:, :]), reads=[pbk[6]], writes=["WAst"])
                for gl in range(GB):
                    for hf in range(2):
                        po = 4 + rr["o"] % 2
                        rr["o"] += 1
                        for ri in range(2):
                            P.add("pe", lambda e, po=po, gl=gl, hf=hf, ri=ri: e.matmul(pb[po][:, 0:256], WA2[:, ri, gl, hf * 8:(hf + 1) * 8, :].rearrange("p s h -> p (s h)"), WCp[:, ri, gl].rearrange("p t h -> p (t h)"), start=(ri == 0), stop=(ri == 1)),
                                  reads=["WA2re", "WA2im", "WCpre", "WCpim"], writes=[pbk[po]])
                        if d_ == 0:
                            P.add("dve", lambda e, po=po, gl=gl, hf=hf: e.tensor_tensor(out=KTs[:, gl, hf, :], in0=pb[po][:, 0:256], in1=msk[:, 0, hf, :], op=ALU.mult), reads=[pbk[po], "msk"], writes=["KTs"])
                        else:
                            P.add("dve", lambda e, po=po, hf=hf: e.tensor_tensor(out=KTt[:, :], in0=pb[po][:, 0:256], in1=msk[:, 1, hf, :], op=ALU.mult), reads=[pbk[po], "msk"], writes=["KTt"])
                            P.add("dve", lambda e, gl=gl, hf=hf: e.tensor_tensor(out=KTb[:, gl, hf, :], in0=KTs[:, gl, hf, :], in1=KTt[:, :], op=ALU.add), reads=["KTs", "KTt"], writes=["KTb"])
            P.add("pool", lambda e, b=b: e.dma_start(out=WAd[b], in_=WAst[:, :, :, :, :, :].rearrange("p d f g r q -> p (d f g r q)")), reads=["WAst"], writes=[("WAd", b)], dma=True)
            P.add("pool", lambda e, b=b: e.dma_start(out=WCd[b], in_=WCst[:, :, :, :, :].rearrange("p d g r c -> p (d g r c)")), reads=["WCst"], writes=[("WCd", b)], dma=True)
            P.add("pool", lambda e, b=b: e.dma_start(out=KTd[b], in_=KTb[:, :, :, :].rearrange("p g f c -> p (g f c)")), reads=["KTb"], writes=[("KTd", b)], dma=True)
        P.barrier()
        st_es.close()

        sm_es = ExitStack()
        WAb = sb("WAb", [128, 2, 2, GB, 2, 64], BF16, sm_es)
        WCb = sb("WCb", [64, 2, GB, 2, 256], BF16, sm_es)
        KTB = sb("KTB", [128, GB, 2, 256], BF16, sm_es)
        Usc = sb("Usc", [128, 16, 512], BF16, sm_es)
        imU2 = [sb("imU%d" % i, [128, NBLK, GB, 2, 128], BF16, sm_es) for i in range(2)]
        Up = sb("Up", [128, GB, 16, 16], BF16, sm_es)
        S2 = [sb("S_%d" % i, [64, 2, 2, GB, NB1], F32, sm_es) for i in range(2)]
        XP2 = [sb("XP%d" % i, [64, 2, 2, GB, NB1], BF16, sm_es) for i in range(2)]
        Cc = sb("Cc", [64, 2, 2, GB, NCH + 1], F32, sm_es)
        ini = sb("ini", [64, 2, 2, GB], F32, sm_es)
        zini = sb("zini", [64, 2, 2, GB], F32, sm_es)
        sT1 = sb("sT1", [64, 2, 2, GB * NCH], F32, sm_es)
        sT2 = sb("sT2", [64, 2, 2, GB * NCH], F32, sm_es)
        sQ1 = sb("sQ1", [64, 2, 2, GB], F32, sm_es)
        sQ2 = sb("sQ2", [64, 2, 2, GB], F32, sm_es)
        ArX = sb("ArX", [64, 2, GB, NCH], F32, sm_es)
        AiX = sb("AiX", [64, 2, 2, GB, NCH], F32, sm_es)
        BrX = sb("BrX", [64, 2, GB], F32, sm_es)
        BiX = sb("BiX", [64, 2, 2, GB], F32, sm_es)
        f1 = [sb("f1_0", [64, GB, NCH, 16], F32, sm_es)] * 2
        f2 = [sb("f2_0", [64, GB, NCH, 16], F32, sm_es)] * 2
        ystg = sb("ystg", [128, 16, GB * 8], F32, sm_es)
        pb6b = pb[6].bitcast(BF16)
        P.add("dve", lambda e: e.memset(zini[:, :, :, :], 0.0), writes=["zini"])
        ENG = ["dve", "pool"]

        def seg_states(s_, tok0, b, bi):
            S_, imU = S2[bi], imU2[bi]
            P.add("sp", lambda e, b=b: e.dma_start(out=WAb[:, :, :, :, :, :].rearrange("p d f g r q -> p (d f g r q)"), in_=WAd[b]), reads=[("WAd", b)], writes=["WAb"], dma=True)
            for blk in range(NBLK):
                t0 = tok0 + blk * BS * T1
                src = us[s_][t0:t0 + BS * T1, :].rearrange("(c t) ch -> c t ch", t=T1)
                P.add("sp", lambda e, src=src: e.dma_start(out=Usc[0:BS, :, :], in_=src), writes=["Usc"], dma=True)
                P.add("pool", lambda e, b=b: e.tensor_copy(out=Up[0:BS, :, :, :], in_=Usc[0:BS, :, b * GB * 16:(b + 1) * GB * 16].rearrange("c s (g h) -> c g s h", g=GB)),
                      reads=["Usc"], writes=["Up"])
                for gl in range(GB):
                    g = b * GB + gl
                    for hf in range(2):
                        P.add("pe", lambda e, gl=gl, hf=hf, g=g: e.transpose(pb6b[:, (gl % 2 * 2 + hf) * 128:(gl % 2 * 2 + hf) * 128 + BS], Up[0:BS, gl, hf * 8:(hf + 1) * 8, :].rearrange("c s h -> c (s h)"), identb[0:BS, 0:BS]),
                              reads=["Up", "identb"], writes=[pbk[6]])
                    if gl % 2 == 1:
                        P.add("act", lambda e, gl=gl, blk=blk: e.copy(out=imU[:, blk, gl - 1:gl + 1, :, 0:BS], in_=pb6b[:, 0:512].rearrange("p (g f c) -> p g f c", g=2, f=2)[:, :, :, 0:BS]),
                              reads=[pbk[6]], writes=[("imU", bi, blk)])
                for d_ in range(2):
                    for ri in range(2):
                        for gq in range(GB // 4):
                            po = 4 + rr["o"] % 2
                            rr["o"] += 1
                            for gi in range(4):
                                gl = gq * 4 + gi
                                for hf in range(2):
                                    P.add("pe", lambda e, po=po, gi=gi, gl=gl, hf=hf, d_=d_, ri=ri, blk=blk: e.matmul(pb[po][0:64, gi * 128:gi * 128 + BS], WAb[:, d_, hf, gl, ri, :], imU[:, blk, gl, hf, 0:BS], start=(hf == 0), stop=(hf == 1)),
                                          reads=["WAb", ("imU", bi, blk)], writes=[pbk[po]])
                            P.add("act", lambda e, po=po, gq=gq, d_=d_, ri=ri, blk=blk: e.copy(out=S_[:, d_, ri, gq * 4:(gq + 1) * 4, blk * BS:(blk + 1) * BS], in_=pb[po][0:64, :].rearrange("p (g c) -> p g c", g=4)[:, :, 0:BS]),
                                  reads=[pbk[po]], writes=[("S", bi, d_)])

        def pair(apf, apb):
            assert [list(x) for x in apf.ap] == [list(x) for x in apb.ap], (apf.ap, apb.ap)
            return bass.AP(tensor=apf.tensor, offset=apf.offset,
                           ap=[list(apf.ap[0]), [apb.offset - apf.offset, 2]] + [list(x) for x in apf.ap[1:]])

        def mkap(t, off, dims):
            pstep = t[:, 0, 0, 0, 0].ap[0]
            return bass.AP(tensor=t.tensor if hasattr(t, "tensor") else t, offset=off, ap=[list(pstep)] + [list(x) for x in dims])

        def seg_scan(b, bi, init, initk, fixup):
            g0 = b * GB
            S_, XP = S2[bi], XP2[bi]
            Sv = [[S_[:, d_, ri].rearrange("p g (c i) -> p g c i", i=16) for ri in range(2)] for d_ in range(2)]
            eng = "dve"
            skb = [("S", bi, 0), ("S", bi, 1)]
            ckb = [("Cc", 0), ("Cc", 1)]
            tk = "sc"
            RS = GB * NB1
            GC = GB * NCH

            def so(d_, ri, i):
                return S_[:, d_, ri, 0, i].offset
            for d_ in range(2):
                i16 = 0 if d_ == 0 else 15
                P.add(eng, lambda e, d_=d_, i16=i16: e.tensor_copy(out=ArX[:, d_, :, :], in_=PW[:, 0, d_, g0:g0 + GB, i16].unsqueeze(2).broadcast_to([64, GB, NCH])), reads=["PW"], writes=["ArX"])
                P.add(eng, lambda e, d_=d_, i16=i16: e.tensor_copy(out=AiX[:, d_, 1, :, :], in_=PW[:, 1, d_, g0:g0 + GB, i16].unsqueeze(2).broadcast_to([64, GB, NCH])), reads=["PW"], writes=["AiX"])
                P.add(eng, lambda e, d_=d_, i16=i16: e.tensor_scalar(out=AiX[:, d_, 0, :, :], in0=PW[:, 1, d_, g0:g0 + GB, i16].unsqueeze(2).broadcast_to([64, GB, NCH]), scalar1=-1.0, scalar2=None, op0=ALU.mult), reads=["PW"], writes=["AiX"])
                i256 = 15 if d_ == 0 else 0
                P.add(eng, lambda e, d_=d_, i256=i256: e.tensor_copy(out=BrX[:, d_, :], in_=PW[:, 0, d_, g0:g0 + GB, i256]), reads=["PW"], writes=["BrX"])
                P.add(eng, lambda e, d_=d_, i256=i256: e.tensor_copy(out=BiX[:, d_, 1, :], in_=PW[:, 1, d_, g0:g0 + GB, i256]), reads=["PW"], writes=["BiX"])
                P.add(eng, lambda e, d_=d_, i256=i256: e.tensor_scalar(out=BiX[:, d_, 0, :], in0=PW[:, 1, d_, g0:g0 + GB, i256], scalar1=-1.0, scalar2=None, op0=ALU.mult), reads=["PW"], writes=["BiX"])
            ArV = mkap(ArX, ArX[:, 0, 0, 0].offset, [[GC, 2], [0, 2], [1, GC]]) if False else None
            arx = bass.AP(tensor=ArX.tensor if hasattr(ArX, "tensor") else ArX, offset=ArX[:, 0, 0, 0].offset, ap=[list(ArX[:, 0, 0, 0].ap[0]), [GC, 2], [0, 2], [1, GC]])
            aix = AiX[:, :, :, :, :].rearrange("p d r g c -> p d r (g c)")
            T1, T2 = sT1[:, :, :, :], sT2[:, :, :, :]

            def XP_(i_f, i_b):
                o = so(0, 0, i_f)
                return mkap(S_, o, [[so(1, 0, i_b) - o, 2], [RS, 2], [16, GC]])

            def XS_(i_f, i_b):
                o = so(0, 1, i_f)
                return mkap(S_, o, [[so(1, 1, i_b) - o, 2], [-RS, 2], [16, GC]])
            for n in range(1, 16):
                pP, pS, cur = XP_(n - 1, 16 - n), XS_(n - 1, 16 - n), XP_(n, 15 - n)
                tt(eng, T1, pP, arx, ALU.mult, skb + ["ArX"], [tk + "1"])
                tt(eng, T2, pS, aix, ALU.mult, skb + ["AiX"], [tk + "2"])
                tt(eng, T1, T1, T2, ALU.add, [tk + "1", tk + "2"], [tk + "1"])
                tt(eng, cur, cur, T1, ALU.add, skb + [tk + "1"], skb)
            CS = GB * (NCH + 1)

            def co(d_, ri, c):
                return Cc[:, d_, ri, 0, c].offset

            def mkc(t, off, dims):
                return bass.AP(tensor=t.tensor if hasattr(t, "tensor") else t, offset=off, ap=[list(Cc[:, 0, 0, 0, 0].ap[0])] + [list(x) for x in dims])
            brx = bass.AP(tensor=BrX.tensor if hasattr(BrX, "tensor") else BrX, offset=BrX[:, 0, 0].offset, ap=[list(BrX[:, 0, 0].ap[0]), [GB, 2], [0, 2], [1, GB]])
            bix = BiX[:, :, :, :]
            Q1, Q2 = sQ1[:, :, :, :], sQ2[:, :, :, :]
            for ri in range(2):
                P.add(eng, lambda e, ri=ri: e.tensor_copy(out=pair(Cc[:, 0, ri, :, 0], Cc[:, 1, ri, :, NCH]), in_=init[:, :, ri, :]), reads=[initk], writes=ckb)
            for n in range(NCH):
                cpf, cnf, cpb, cnb = n, n + 1, NCH - n, NCH - 1 - n
                o = co(0, 0, cpf)
                CpP = mkc(Cc, o, [[co(1, 0, cpb) - o, 2], [CS, 2], [NCH + 1, GB]])
                o = co(0, 1, cpf)
                CpS = mkc(Cc, o, [[co(1, 1, cpb) - o, 2], [-CS, 2], [NCH + 1, GB]])
                o = co(0, 0, cnf)
                CnP = mkc(Cc, o, [[co(1, 0, cnb) - o, 2], [CS, 2], [NCH + 1, GB]])
                o = so(0, 0, n * 16 + 15)
                EP = mkap(S_, o, [[so(1, 0, (NCH - 1 - n) * 16) - o, 2], [RS, 2], [NB1, GB]])
                tt(eng, Q1, CpP, brx, ALU.mult, ckb + ["BrX"], [tk + "1"])
                tt(eng, Q2, CpS, bix, ALU.mult, ckb + ["BiX"], [tk + "2"])
                tt(eng, Q1, Q1, Q2, ALU.add, [tk + "1", tk + "2"], [tk + "1"])
                tt(eng, CnP, EP, Q1, ALU.add, skb + [tk + "1"], ckb)
            for d_ in range(2):
                eng = "dve"
                Sre = Sv[d_][0]
                Sim = Sv[d_][1]
                Cre, Cim = Cc[:, d_, 0], Cc[:, d_, 1]
                sk = ("S", bi, d_)
                ck = ("Cc", d_)
                tk = "scf"
                if not fixup:
                    continue
                shp4 = [64, GB, NCH, 16]
                Pr = PW[:, 0, d_, g0:g0 + GB, :].unsqueeze(2).broadcast_to(shp4)
                Pi = PW[:, 1, d_, g0:g0 + GB, :].unsqueeze(2).broadcast_to(shp4)
                cb0 = 0 if d_ == 0 else 1
                Cbr = Cre[:, :, cb0:cb0 + NCH].unsqueeze(3).broadcast_to(shp4)
                Cbi = Cim[:, :, cb0:cb0 + NCH].unsqueeze(3).broadcast_to(shp4)
                g1, g2 = f1[d_][:, :, :, :], f2[d_][:, :, :, :]
                tt(eng, g1, Pr, Cbr, ALU.mult, [ck, "PW"], [tk + "f1"])
                tt(eng, g2, Pi, Cbi, ALU.mult, [ck, "PW"], [tk + "f2"])
                tt(eng, g1, g1, g2, ALU.subtract, [tk + "f1", tk + "f2"], [tk + "f1"])
                tt(eng, Sre, Sre, g1, ALU.add, [sk, tk + "f1"], [sk])
                tt(eng, g1, Pr, Cbi, ALU.mult, [ck, "PW"], [tk + "f1"])
                tt(eng, g2, Pi, Cbr, ALU.mult, [ck, "PW"], [tk + "f2"])
                tt(eng, g1, g1, g2, ALU.add, [tk + "f1", tk + "f2"], [tk + "f1"])
                tt(eng, Sim, Sim, g1, ALU.add, [sk, tk + "f1"], [sk])
                for ri in range(2):
                    X = S_[:, d_, ri]
                    if d_ == 0:
                        P.add("act", lambda e, ri=ri, X=X, XP=XP: e.copy(out=XP[:, 0, ri, :, 1:NB1], in_=X[:, :, 0:NB1 - 1]), reads=[sk], writes=[("XP", bi, 0)])
                        P.add("act", lambda e, ri=ri, XP=XP: e.copy(out=XP[:, 0, ri, :, 0], in_=init[:, 0, ri, :]), reads=[initk], writes=[("XP", bi, 0)])
                    else:
                        P.add("act", lambda e, ri=ri, X=X, XP=XP: e.copy(out=XP[:, 1, ri, :, 0:NB1 - 1], in_=X[:, :, 1:NB1]), reads=[sk], writes=[("XP", bi, 1)])
                        P.add("act", lambda e, ri=ri, XP=XP: e.copy(out=XP[:, 1, ri, :, NB1 - 1], in_=init[:, 1, ri, :]), reads=[initk], writes=[("XP", bi, 1)])

        def seg_outputs(s_, b, bi):
            imU, XP = imU2[bi], XP2[bi]
            P.add("sp", lambda e, b=b: e.dma_start(out=WCb[:, :, :, :, :].rearrange("p d g r c -> p (d g r c)"), in_=WCd[b]), reads=[("WCd", b)], writes=["WCb"], dma=True)
            P.add("sp", lambda e, b=b: e.dma_start(out=KTB[:, :, :, :].rearrange("p g f c -> p (g f c)"), in_=KTd[b]), reads=[("KTd", b)], writes=["KTB"], dma=True)
            for blk in range(NBLK):
                for gq in range(GB // 2):
                    po = 4 + rr["o"] % 2
                    rr["o"] += 1
                    for gi in range(2):
                        gl = gq * 2 + gi
                        col = gi * 256
                        for hf in range(2):
                            P.add("pe", lambda e, po=po, col=col, gl=gl, hf=hf, blk=blk: e.matmul(pb[po][0:BS, col:col + 256], imU[:, blk, gl, hf, 0:BS], KTB[:, gl, hf, :], start=(hf == 0), stop=False),
                                  reads=[("imU", bi, blk), "KTB"], writes=[pbk[po]])
                        for d_ in range(2):
                            for ri in range(2):
                                P.add("pe", lambda e, po=po, col=col, gl=gl, d_=d_, ri=ri, blk=blk: e.matmul(pb[po][0:BS, col:col + 256], XP[:, d_, ri, gl, blk * BS:(blk + 1) * BS], WCb[:, d_, gl, ri, :], start=False, stop=(d_ == 1 and ri == 1)),
                                      reads=[("XP", bi, d_), "WCb"], writes=[pbk[po]])
                    P.add("act", lambda e, po=po, gq=gq: e.copy(out=ystg[0:BS, :, (gq % 2) * 32:(gq % 2 + 1) * 32].rearrange("c t (g h) -> c g t h", g=2), in_=pb[po][0:BS, :].rearrange("c (g t h) -> c g t h", g=2, t=16)),
                          reads=[pbk[po]], writes=["ystg"])
                    if gq % 2 == 1:
                        t0 = blk * BS * T1
                        c0 = b * 128 + (gq // 2) * 64
                        dst = yscr[s_][t0:t0 + BS * T1, c0:c0 + 64].rearrange("(c t) ch -> c t ch", t=T1)
                        P.add("pool", lambda e, dst=dst: e.dma_start(out=dst, in_=ystg[0:BS, :, :]), reads=["ystg"], writes=[("yscr", s_, blk, b, gq // 2)], dma=True)

        seqsS = range(NSEQ) if dbg != "S1" else range(1)
        if dbg == "Ss":
            seqsS = range(0)
        units = []
        for s_ in seqsS:
            for b in range(4):
                if s_ == 0 and NCTX:
                    units.append((s_, b, "ctx"))
                units.append((s_, b, "own"))

        def do_states(k):
            s_, b, kind = units[k]
            seg_states(s_, NOWN if kind == "ctx" else 0, b, k % 2)
        if units:
            do_states(0)
        for k, (s_, b, kind) in enumerate(units):
            if k + 1 < len(units):
                do_states(k + 1)
            bi = k % 2
            if kind == "ctx":
                seg_scan(b, bi, zini, "zini", fixup=False)
                for ri in range(2):
                    P.add("dve", lambda e, ri=ri: e.tensor_scalar(out=ini[:, 0, ri, :], in0=Cc[:, 0, ri, :, NCH], scalar1=flg[0:64, 0:1], scalar2=None, op0=ALU.mult), reads=[("Cc", 0), "flg"], writes=["ini"])
                    P.add("dve", lambda e, ri=ri: e.tensor_scalar(out=ini[:, 1, ri, :], in0=Cc[:, 1, ri, :, 0], scalar1=flg[0:64, 1:2], scalar2=None, op0=ALU.mult), reads=[("Cc", 1), "flg"], writes=["ini"])
            else:
                if s_ == 0 and NCTX:
                    seg_scan(b, bi, ini, "ini", fixup=True)
                else:
                    seg_scan(b, bi, zini, "zini", fixup=True)
                seg_outputs(s_, b, bi)
        P.barrier()
        sm_es.close()

        se_es = ExitStack()
        yb = sb("yb", [128, 16, 512], F32, se_es)
        tb = sb("tbb", [128, 16, 512], F32, se_es)
        ub = sb("ub", [128, 16, 512], BF16, se_es)
        zb = sb("zb", [128, 16, 512], BF16, se_es)
        ogb = sb("ogb", [128, 16, 512], BF16, se_es)
        zT = [sb("zT%d" % i, [128, 4, 128], BF16, se_es) for i in range(2)]
        gt = [sb("gt%d" % i, [128, 512], F32, se_es) for i in range(2)]
        Dt = sb("Dt", [128, 512], F32, se_es)
        Gb = sb("Gb", [128, 512], F32, se_es)
        gluw = sb("gluw", [128, 4, 512], BF16, se_es)
        pb6b = pb[6].bitcast(BF16)
        P.add("sp", lambda e: e.dma_start(out=Dt[:, :], in_=ssm["ssm_d"].partition_broadcast(128)), writes=["Dt"], dma=True)
        P.add("sp", lambda e: e.dma_start(out=Gb[:, :], in_=ssm["ssm_glu_b"].partition_broadcast(128)), writes=["Gb"], dma=True)
        P.add("sp", lambda e: e.dma_start(out=gluw[:, :, :].rearrange("p k c -> p (k c)"), in_=wdglu[:, :]), reads=[("wdglu",)], writes=["gluw"], dma=True)
        shp = [BS, 16, 512]
        for s_ in (seqsS if dbg != "Sm" else range(0)):
            for blk in range(NBLK):
                t0 = blk * BS * T1
                P.add("sp", lambda e, s_=s_, t0=t0: e.dma_start(out=yb[0:BS, :, :], in_=yscr[s_][t0:t0 + BS * T1, :].rearrange("(c t) ch -> c t ch", t=T1)), writes=["yb"], dma=True)
                P.add("sp", lambda e, s_=s_, t0=t0: e.dma_start(out=ub[0:BS, :, :], in_=us[s_][t0:t0 + BS * T1, :].rearrange("(c t) ch -> c t ch", t=T1)), writes=["ub"], dma=True)
                Y, Tt, Uu, Zz = yb[0:BS, :, :], tb[0:BS, :, :], ub[0:BS, :, :], zb[0:BS, :, :]
                tt("pool", Tt, Uu, Dt[0:BS, :].unsqueeze(1).broadcast_to(shp), ALU.mult, ["ub", "Dt"], ["tb"])
                tt("dve", Y, Y, Tt, ALU.add, ["yb", "tb"], ["yb"])
                P.add("act", lambda e, Y=Y, Tt=Tt: e.activation(out=Tt, in_=Y, func=AF.Square), reads=["yb"], writes=["tb"])
                P.add("dve", lambda e, Tt=Tt: e.tensor_scalar(out=Tt, in0=Tt, scalar1=0.044715, scalar2=1.0, op0=ALU.mult, op1=ALU.add), reads=["tb"], writes=["tb"])
                tt("pool", Tt, Tt, Y, ALU.mult, ["tb", "yb"], ["tb"])
                P.add("act", lambda e, Tt=Tt: e.activation(out=Tt, in_=Tt, func=AF.Sigmoid, scale=2.0 * math.sqrt(2.0 / math.pi)), reads=["tb"], writes=["tb"])
                tt("dve", Zz, Y, Tt, ALU.mult, ["yb", "tb"], ["zb"])
                for t_ in range(16):
                    zt = zT[t_ % 2]
                    for kc in range(4):
                        P.add("pe", lambda e, t_=t_, kc=kc: e.transpose(pb6b[:, kc * 128:kc * 128 + BS], zb[0:BS, t_, kc * 128:(kc + 1) * 128], identb[0:BS, 0:BS]), reads=["zb", "identb"], writes=[pbk[6]])
                    P.add("act", lambda e, zt=zt: e.copy(out=zt[:, :, 0:BS], in_=pb6b[:, 0:512].rearrange("p (k c) -> p k c", k=4)[:, :, 0:BS]), reads=[pbk[6]], writes=[("zT", t_ % 2)])
                    po = 4 + rr["o"] % 2
                    rr["o"] += 1
                    for kc in range(4):
                        P.add("pe", lambda e, po=po, kc=kc, zt=zt: e.matmul(pb[po][0:BS, :], zt[:, kc, 0:BS], gluw[:, kc, :], start=(kc == 0), stop=(kc == 3)), reads=[("zT", t_ % 2), "gluw"], writes=[pbk[po]])
                    g_ = gt[t_ % 2]
                    P.add("dve", lambda e, po=po, g_=g_: e.tensor_tensor(out=g_[0:BS, :], in0=pb[po][0:BS, :], in1=Gb[0:BS, :], op=ALU.add), reads=[pbk[po], "Gb"], writes=[("gt", t_ % 2)])
                    P.add("act", lambda e, g_=g_: e.activation(out=g_[0:BS, :], in_=g_[0:BS, :], func=AF.Sigmoid), reads=[("gt", t_ % 2)], writes=[("gt", t_ % 2)])
                    P.add("pool", lambda e, g_=g_, t_=t_: e.tensor_tensor(out=ogb[0:BS, t_, :], in0=zb[0:BS, t_, :], in1=g_[0:BS, :], op=ALU.mult), reads=[("gt", t_ % 2), "zb"], writes=["ogb"])
                dst = oss[s_][t0:t0 + BS * T1, :].rearrange("(c t) ch -> c t ch", t=T1)
                P.add("pool", lambda e, dst=dst: e.dma_start(out=dst, in_=ogb[0:BS, :, :]), reads=["ogb"], writes=[("oss", s_, blk)], dma=True)
        P.barrier()
        se_es.close()

    if dbg in (None, "C"):
        pc_es = ExitStack()
        wslots[:] = [sb("wslotC%d" % i, [128, 4096], BF16, pc_es) for i in range(NWS)]
        bA = sb("cA", [128, 8, 512], F32, pc_es)
        bB = sb("cB", [128, 8, 512], F32, pc_es)
        bC = sb("cC", [128, 8, 512], F32, pc_es)
        bD = sb("cD", [128, 8, 512], BF16, pc_es)
        bE = sb("cE", [128, 8, 512], BF16, pc_es)
        QmT = sb("QmT", [128, 8, 512], BF16, pc_es)
        omt = sb("omt", [128, 4, 1024], BF16, pc_es)
        odt = sb("odt", [128, 2, 4, 512], BF16, pc_es)
        gTc = sb("gTc", [128, NJ, 512], BF16, pc_es)
        sgc = [sb("sgc%d" % i, [128, 512], F32, pc_es) for i in range(2)]
        tmpC = {"rb": sb("rbc", [128, 2, 512], BF16, pc_es), "rsq": sb("rsqc", [128, 2, 512], BF16, pc_es),
                "mean": sb("meanc", [128, 512], F32, pc_es), "rstd": sb("rstdc", [128, 512], F32, pc_es), "t1": sb("t1c", [128, 512], F32, pc_es)}
        memt = sb("memt", [128, 2, D], F32, pc_es)
        memT = sb("memT", [128, 8, 256], BF16, pc_es)
        KmT = sb("KmT", [128, 8, 256], BF16, pc_es)
        Vm = sb("Vm", [128, 2, 4, 257], BF16, pc_es)
        ETm = [sb("ETm%d" % i, [128, 512], BF16, pc_es) for i in range(4)]
        smc = sb("smc", [128, 4], F32, pc_es)
        yt = bB[:, :, :].rearrange("p a b -> p (a b)").rearrange("p (st d) -> p st d", st=4)
        pb6b = pb[6].bitcast(BF16)
        P.add("dve", lambda e: e.memset(Vm[:, :, :, 256:257], 1.0), writes=["Vm1"])

        def tr_tok2feat(src_fn, dstT, dkey, rkeys):
            for kc in range(8):
                bk = 6 + kc % 2
                pbb = pb[bk].bitcast(BF16)
                for st in range(4):
                    P.add("pe", lambda e, kc=kc, st=st, pbb=pbb: e.transpose(pbb[:, st * 128:(st + 1) * 128], src_fn(kc, st), identb[:, :]),
                          reads=rkeys + ["identb"], writes=[pbk[bk]])
                eng_ = "act" if kc % 2 == 0 else "dve"
                if eng_ == "act":
                    P.add("act", lambda e, kc=kc, pbb=pbb: e.copy(out=dstT[:, kc, :], in_=pbb[:, 0:512]), reads=[pbk[bk]], writes=[(dkey, kc)])
                else:
                    P.add("dve", lambda e, kc=kc, pbb=pbb: e.tensor_copy(out=dstT[:, kc, :], in_=pbb[:, 0:512]), reads=[pbk[bk]], writes=[(dkey, kc)])

        def proj_fm(wd, wname, xb, xbk, evac):
            for half in range(2):
                wt, wk = wload(wd[half], (wname, half))
                wv = wt[:, :].rearrange("p (kc c) -> p kc c", kc=8)
                for c4 in range(4):
                    po = 4 + rr["o"] % 2
                    rr["o"] += 1
                    for kc in range(8):
                        P.add("pe", lambda e, po=po, kc=kc, c4=c4, wv=wv: e.matmul(pb[po][:, :], wv[:, kc, c4 * 128:(c4 + 1) * 128], xb[:, kc, :], start=(kc == 0), stop=(kc == 7)),
                              reads=[wk, (xbk, kc)], writes=[pbk[po]])
                    evac(half * 4 + c4, pb[po], pbk[po])

        seqsC = range(NSEQ) if dbg != "C1" else range(1)
        for s in seqsC:
            msrc = mem[s].rearrange("(kb p) d -> p kb d", p=128)
            P.add("sp", lambda e, msrc=msrc: e.dma_start(out=memt[:, :, :], in_=msrc), writes=["memt"], dma=True)
            for kc in range(8):
                for kb in range(2):
                    P.add("pe", lambda e, kc=kc, kb=kb: e.transpose(pb[6][:, kb * 128:(kb + 1) * 128], memt[:, kb, kc * 128:(kc + 1) * 128], ident[:, :]),
                          reads=["memt", "ident"], writes=[pbk[6]])
                P.add("act", lambda e, kc=kc: e.copy(out=memT[:, kc, :], in_=pb[6][:, 0:256]), reads=[pbk[6]], writes=[("memT", kc)])
            for cg in range(4):
                wt, wk = wload(wdkv[cg], ("mem_wkv", cg))
                wv = wt[:, :].rearrange("p (kc c) -> p kc c", kc=8)
                if cg < 2:
                    for c4 in range(4):
                        po = 4 + rr["o"] % 2
                        rr["o"] += 1
                        for kc in range(8):
                            P.add("pe", lambda e, po=po, kc=kc, c4=c4, wv=wv: e.matmul(pb[po][:, 0:256], wv[:, kc, c4 * 128:(c4 + 1) * 128], memT[:, kc, :], start=(kc == 0), stop=(kc == 7)),
                                  reads=[wk, ("memT", kc)], writes=[pbk[po]])
                        P.add("act", lambda e, po=po, cc=cg * 4 + c4: e.copy(out=KmT[:, cc, :], in_=pb[po][:, 0:256]), reads=[pbk[po]], writes=[("KmT", cg * 4 + c4)])
                else:
                    for kb in range(2):
                        po = 4 + rr["o"] % 2
                        rr["o"] += 1
                        for kc in range(8):
                            P.add("pe", lambda e, po=po, kc=kc, kb=kb, wv=wv: e.matmul(pb[po][:, :], memT[:, kc, kb * 128:(kb + 1) * 128], wv[:, kc, :], start=(kc == 0), stop=(kc == 7)),
                                  reads=[wk, ("memT", kc)], writes=[pbk[po]])
                        hm0 = (cg - 2) * 2
                        P.add("act", lambda e, po=po, kb=kb, hm0=hm0: e.copy(out=Vm[:, kb, hm0:hm0 + 2, 0:256], in_=pb[po][:, :].rearrange("p (a b) -> p a b", a=2)),
                              reads=[pbk[po], "Vm1"], writes=[("Vm", kb, hm0)])
            ntile = NOWN // TT if dbg != "C1" else 1
            for i in range(ntile):
                src = x1s[s][:, i * TT:(i + 1) * TT].rearrange("(dc p) t -> p dc t", p=128)
                P.add("sp", lambda e, src=src: e.dma_start(out=bA[:, :, :], in_=src), writes=[("bA", dc) for dc in range(8)], dma=True)
                for w_, dsrc in ((0, ods), (1, oss)):
                    src = dsrc[s][i * TT:(i + 1) * TT, :].rearrange("(st p) c -> p st c", p=128)
                    P.add("sp", lambda e, src=src, w_=w_: e.dma_start(out=odt[:, w_, :, :], in_=src), writes=[("odt", w_)], dma=True)
                tr_tok2feat(lambda kc, st: odt[:, kc // 4, st, (kc % 4) * 128:(kc % 4 + 1) * 128], bE, "bE", [("odt", 0), ("odt", 1)])
                proj_fm(wdout, "w_out", bE, "bE",
                        lambda dc, bank, bk: P.add("dve", lambda e: e.scalar_tensor_tensor(out=bB[:, dc, :], in0=bA[:, dc, :], scalar=ALPHA, in1=bank[:, :], op0=ALU.mult, op1=ALU.add),
                                                   reads=[bk, ("bA", dc)], writes=[("bB", dc)]))
                layernorm(bB, "bB", "ln_mix_g", "ln_mix_b", bC, bD, "bC", "bD", tmpC)
                proj_fm(wdq, "mem_wq", bD, "bD",
                        lambda dc, bank, bk: P.add("act", lambda e: e.copy(out=QmT[:, dc, :], in_=bank[:, :]), reads=[bk], writes=[("QmT", dc)]))
                for hm in range(4):
                    for kb in range(2):
                        sbk = rr["ab"] % 4
                        rr["ab"] += 1
                        for dd in range(2):
                            P.add("pe", lambda e, sbk=sbk, hm=hm, kb=kb, dd=dd: e.matmul(pb[sbk][:, :], KmT[:, 2 * hm + dd, kb * 128:(kb + 1) * 128], QmT[:, 2 * hm + dd, :], start=(dd == 0), stop=(dd == 1)),
                                  reads=[("KmT", 2 * hm + dd), ("QmT", 2 * hm + dd)], writes=[pbk[sbk]])
                        P.add("act", lambda e, sbk=sbk, kb=kb, hm=hm: e.activation(out=ETm[(hm % 2) * 2 + kb][:, :], in_=pb[sbk][:, :], func=AF.Exp, scale=1.0 / 16), reads=[pbk[sbk]], writes=[("ETm", (hm % 2) * 2 + kb)])
                    for qb in range(4):
                        po = 4 + rr["o"] % 2
                        rr["o"] += 1
                        for kb in range(2):
                            P.add("pe", lambda e, po=po, hm=hm, kb=kb, qb=qb: e.matmul(pb[po][:, 0:257], ETm[(hm % 2) * 2 + kb][:, qb * 128:(qb + 1) * 128], Vm[:, kb, hm, :], start=(kb == 0), stop=(kb == 1)),
                                  reads=[("ETm", (hm % 2) * 2 + kb), ("Vm", kb, (hm // 2) * 2), "Vm1"], writes=[pbk[po]])
                        P.add("dve", lambda e, po=po, qb=qb: e.reciprocal(out=smc[:, qb:qb + 1], in_=pb[po][:, 256:257]), reads=[pbk[po]], writes=[("smc", qb)])
                        P.add("dve", lambda e, po=po, qb=qb, hm=hm: e.tensor_scalar(out=omt[:, qb, hm * 256:(hm + 1) * 256], in0=pb[po][:, 0:256], scalar1=smc[:, qb:qb + 1], scalar2=None, op0=ALU.mult),
                              reads=[pbk[po], ("smc", qb)], writes=[("omt", qb)])
                tr_tok2feat(lambda kc, st: omt[:, st, kc * 128:(kc + 1) * 128], bE, "bE", [("omt", q) for q in range(4)])
                proj_fm(wdo, "mem_wo", bE, "bE",
                        lambda dc, bank, bk: P.add("dve", lambda e: e.scalar_tensor_tensor(out=bB[:, dc, :], in0=bC[:, dc, :], scalar=ALPHA, in1=bank[:, :], op0=ALU.mult, op1=ALU.add),
                                                   reads=[bk, ("bC", dc)], writes=[("bB", dc)]))
                layernorm(bB, "bB", "ln_mem_g", "ln_mem_b", bA, bD, "bA", "bD", tmpC)
                ffn(1, bD, "bD", bA, "bA", bB, "bB", gTc, sgc)
                layernorm(bB, "bB", "ln_ffn2_g", "ln_ffn2_b", bC, bD, "bC", "bD", tmpC, need_b=False)
                for st in range(4):
                    for dg in range(2):
                        for c4 in range(4):
                            dc = dg * 4 + c4
                            P.add("pe", lambda e, st=st, dc=dc, c4=c4: e.transpose(pb[6][:, c4 * 128:(c4 + 1) * 128], bC[:, dc, st * 128:(st + 1) * 128], ident[:, :]),
                                  reads=[("bC", dc), "ident"], writes=[pbk[6]])
                        P.add("act", lambda e, st=st, dg=dg: e.copy(out=yt[:, st, dg * 512:(dg + 1) * 512], in_=pb[6][:, :]),
                              reads=[pbk[6]] + [("bB", dc) for dc in range(8)], writes=[("yt", st)])
                ydst = (yp if s == 0 else ys[s - 1])[i * TT:(i + 1) * TT, :].rearrange("(st p) d -> p st d", p=128)
                P.add("pool", lambda e, ydst=ydst: e.dma_start(out=ydst, in_=yt), reads=[("yt", st) for st in range(4)], writes=[("bB", dc) for dc in range(8)], dma=True)
        P.barrier()
        pc_es.close()

    P.barrier()
    P.add("sp", lambda e: e.dma_start(out=flg[:, :], in_=flags[:, :]), writes=["flg"], dma=True)
    P.barrier()
    P.add("pool", lambda e: e.memset(epsc[:, :], LN_EPS), writes=["epsc"])
    P.emit()
    es.close()
    return nc


_CACHE = {}


def _prep_inputs(inputs, core, nown, nsamp):
    xpr = inputs["x_prompt"]; xsm = inputs["x_sample"]
    b, half = core // 2, core % 2
    d = {}
    d["xp_own"] = np.ascontiguousarray(xpr[b, half * nown:(half + 1) * nown])
    d["xp_ctx"] = np.ascontiguousarray(xpr[b, (1 - half) * nown:(2 - half) * nown])
    d["xs"] = np.ascontiguousarray(xsm[core * nsamp:(core + 1) * nsamp])
    d["mem"] = np.ascontiguousarray(np.concatenate([inputs["mem_prompt"][b:b + 1], inputs["mem_sample"][core * nsamp:(core + 1) * nsamp]], 0))
    fl = np.zeros((128, 2), np.float32)
    fl[:, 0] = 1.0 if half == 1 else 0.0
    fl[:, 1] = 1.0 if half == 0 else 0.0
    d["flags"] = fl
    d["ident"] = np.eye(128, dtype=np.float32)
    d["antiid"] = np.ascontiguousarray(np.eye(128, dtype=np.float32)[::-1])
    d["bucket_oh"] = _bucket_table()
    kv = np.zeros((64, 2, 17), np.float32)
    kv[:, 0, :] = np.arange(17); kv[:, 1, :] = 16 - np.arange(17)
    d["kv17"] = kv
    sl = (np.arange(128) // 16)[:, None, None] + 8 * np.arange(2)[None, :, None]
    tl = (np.arange(256) // 16)[None, None, :]
    mk = np.zeros((128, 2, 2, 256), np.float32)
    mk[:, 0] = (sl <= tl); mk[:, 1] = (sl >= tl)
    d["ssm_mask"] = mk
    for nm in ["ffn1_w13", "ffn1_w2", "ffn2_w13", "ffn2_w2", "w_in", "w_out", "mem_wq", "mem_wkv", "mem_wo", "ssm_glu_w"]:
        d[nm] = np.ascontiguousarray(inputs[nm][0])
    for nm in ["ln_ffn1_g", "ln_ffn1_b", "ln_mix_g", "ln_mix_b", "ln_mem_g", "ln_mem_b", "ln_ffn2_g", "ln_ffn2_b",
               "ssm_lam_re", "ssm_lam_im", "ssm_log_step", "ssm_b_re", "ssm_b_im", "ssm_c_re", "ssm_c_im", "ssm_d", "ssm_glu_b"]:
        d[nm] = np.ascontiguousarray(inputs[nm][0])
    d["diff_lambda"] = np.ascontiguousarray(inputs["diff_lambda"][0].reshape(256))
    d["diff_subln_g"] = np.ascontiguousarray(inputs["diff_subln_g"][0])
    d["rel_bias"] = np.ascontiguousarray(inputs["rel_bias"])
    return {k: np.asarray(v, np.float32) for k, v in d.items()}


def kernel(_debug=None, **inputs):
    import os
    _probe = os.environ.get("K_PROBE")
    xpr = inputs["x_prompt"]; xsm = inputs["x_sample"]
    nown = xsm.shape[1]
    assert xpr.shape[1] == 2 * nown and xpr.shape[0] * 2 == 8
    nsamp = xsm.shape[0] // 8
    key = (nown, nsamp, _debug or _probe)
    if key not in _CACHE:
        _CACHE[key] = build(Cfg(nown, True, nsamp, debug=_debug or _probe))
    nc = _CACHE[key]
    in_maps = [_prep_inputs(inputs, c, nown, nsamp) for c in range(8)]
    if _debug is not None:
        import os
        n = int(os.environ.get("K_NCORES", "8"))
        res = run_bass_kernel_spmd(nc, in_maps[:n], core_ids=list(range(n)))
        return res.results
    res = run_bass_kernel_spmd(nc, in_maps, core_ids=list(range(8)))
    if _debug is not None:
        return res.results
    y_p = np.empty(xpr.shape, np.float32)
    y_s = np.empty(xsm.shape, np.float32)
    for c in range(8):
        r = res.results[c]
        y_p[c // 2, (c % 2) * nown:(c % 2 + 1) * nown] = r["yp"]
        y_s[c * nsamp:(c + 1) * nsamp] = r["ys"]
    return (y_p, y_s)
```
